# Optimizing a Trainium2 kernel written in Bass

```python
import jax, jax.numpy as jnp
from jax import lax
import numpy as np

D_MODEL = 1024
BATCH = 8
SEQ = 4096
DEPTH = 4

CHUNK = 64
N_MEM = 256
EPS = 1e-6

BRANCH_WIDTH = D_MODEL // 2
N_BRANCH = 3

A_BLOCK = 128
A_GROUP_DIM = 128
A_WIDTH = BRANCH_WIDTH
A_GROUPS = A_WIDTH // A_GROUP_DIM

B_HEAD_DIM = 64
B_WIDTH = BRANCH_WIDTH
B_HEADS = B_WIDTH // B_HEAD_DIM
Q_BLOCK = 128

C_HEADS = 4
C_WIDTH = BRANCH_WIDTH
C_HEAD_DIM = C_WIDTH // C_HEADS
C_CONV = 4

MEM_HEADS = 4
MEM_HEAD_DIM = D_MODEL // MEM_HEADS

D_FF = 2816
FFN_CONV = 3

A_COLS = 2 * A_WIDTH
B_COLS = 3 * B_WIDTH + B_HEADS
C_COLS = 3 * C_WIDTH + 2 * C_HEADS + C_WIDTH
G_COLS = N_BRANCH * D_MODEL
IN_COLS = A_COLS + B_COLS + C_COLS + G_COLS

kernel_name = "hybrid_gmlp_fox_mlstm_encoder"


def rmsnorm(x, g):
    xf = x.astype(jnp.float32)
    y = xf * lax.rsqrt(jnp.mean(xf * xf, axis=-1, keepdims=True) + EPS)
    return (y * g.astype(jnp.float32)).astype(x.dtype)


def causal_dwconv(x, w):
    K = w.shape[0]
    S = x.shape[1]
    xp = jnp.pad(x, ((0, 0), (K - 1, 0), (0, 0)))
    return sum(w[k] * xp[:, k:k + S] for k in range(K))


def spatial_gating(u, v, g_norm, w_s, b_s):
    B, S, _ = u.shape
    nb = S // A_BLOCK
    v = rmsnorm(v, g_norm)
    idx = jnp.arange(A_BLOCK)
    mask = (idx[None, :] // CHUNK) <= (idx[:, None] // CHUNK)
    ws = jnp.where(mask[None], w_s, 0)
    vb = v.reshape(B, nb, A_BLOCK, A_GROUPS, A_GROUP_DIM)
    mixed = jnp.einsum('gts,bnsgc->bntgc', ws, vb) + b_s.T[None, None, :, :, None]
    return u * mixed.reshape(B, S, A_WIDTH)


def forgetting_attention(q, k, v, f_logit):
    B, S, H, Dh = q.shape
    nq = S // Q_BLOCK
    logf = jax.nn.log_sigmoid(f_logit.astype(jnp.float32))
    c = jnp.cumsum(logf, axis=1).transpose(0, 2, 1)
    qb = q.reshape(B, nq, Q_BLOCK, H, Dh).transpose(1, 0, 3, 2, 4)
    cq = c.reshape(B, H, nq, Q_BLOCK).transpose(2, 0, 1, 3)
    kpos = jnp.arange(S)
    scale = Dh ** -0.5

    def block(args):
        qi, ci, i = args
        s = jnp.einsum('bhqd,bkhd->bhqk', qi, k).astype(jnp.float32) * scale
        s = s + ci[..., None] - c[:, :, None, :]
        qpos = i * Q_BLOCK + jnp.arange(Q_BLOCK)
        s = jnp.where(kpos[None, :] <= qpos[:, None], s, -jnp.inf)
        p = jax.nn.softmax(s, axis=-1).astype(v.dtype)
        return jnp.einsum('bhqk,bkhd->bqhd', p, v)

    o = lax.map(block, (qb, cq, jnp.arange(nq)))
    return o.transpose(1, 0, 2, 3, 4).reshape(B, S, H * Dh)


def mlstm_chunkwise(q, k, v, i_raw, f_raw):
    out_dtype = q.dtype
    B, S, H, Dh = q.shape
    L = CHUNK
    NC = S // L
    f32 = jnp.float32

    def to_chunks(t):
        return t.astype(f32).reshape(B, NC, L, H, -1).transpose(0, 3, 1, 2, 4)

    qc = to_chunks(q)
    kc = to_chunks(k) * (Dh ** -0.5)
    vc = to_chunks(v)
    ig = i_raw.astype(f32).reshape(B, NC, L, H).transpose(0, 3, 1, 2)
    logf = jax.nn.log_sigmoid(f_raw.astype(f32)).reshape(B, NC, L, H).transpose(0, 3, 1, 2)
    b = jnp.cumsum(logf, axis=-1)
    g = b[..., -1]

    a = g[..., None] - b + ig
    m_loc = jnp.max(a, axis=-1)
    w_loc = jnp.exp(a - m_loc[..., None])
    C_loc = jnp.einsum('bhnl,bhnld,bhnle->bhnde', w_loc, kc, vc)
    n_loc = jnp.einsum('bhnl,bhnld->bhnd', w_loc, kc)

    def step(carry, inp):
        C, n, m = carry
        Cl, nl, ml, gl = inp
        m_new = jnp.maximum(gl + m, ml)
        a_old = jnp.exp(gl + m - m_new)
        a_new = jnp.exp(ml - m_new)
        C_new = a_old[..., None, None] * C + a_new[..., None, None] * Cl
        n_new = a_old[..., None] * n + a_new[..., None] * nl
        return (C_new, n_new, m_new), (C, n, m)

    init = (jnp.zeros((B, H, Dh, Dh), f32), jnp.zeros((B, H, Dh), f32), jnp.zeros((B, H), f32))
    xs = (jnp.moveaxis(C_loc, 2, 0), jnp.moveaxis(n_loc, 2, 0),
          jnp.moveaxis(m_loc, 2, 0), jnp.moveaxis(g, 2, 0))
    _, (C_in, n_in, m_in) = lax.scan(step, init, xs)
    C_in = jnp.moveaxis(C_in, 0, 2)
    n_in = jnp.moveaxis(n_in, 0, 2)
    m_in = jnp.moveaxis(m_in, 0, 2)

    tri = jnp.tril(jnp.ones((L, L), dtype=bool))
    dlog = jnp.where(tri, b[..., :, None] - b[..., None, :] + ig[..., None, :], -jnp.inf)
    inter = b + m_in[..., None]
    m_t = jnp.maximum(jnp.max(dlog, axis=-1), inter)
    sm = jnp.einsum('bhnld,bhnsd->bhnls', qc, kc) * jnp.exp(dlog - m_t[..., None])
    w_int = jnp.exp(inter - m_t)
    num = jnp.einsum('bhnls,bhnse->bhnle', sm, vc) \
        + w_int[..., None] * jnp.einsum('bhnld,bhnde->bhnle', qc, C_in)
    den = jnp.sum(sm, axis=-1) + w_int * jnp.einsum('bhnld,bhnd->bhnl', qc, n_in)
    h = num / jnp.maximum(jnp.abs(den), jnp.exp(-m_t))[..., None]
    return h.transpose(0, 2, 3, 1, 4).reshape(B, S, H * Dh).astype(out_dtype)


def memory_attention(h, mem_n, w_q, w_kv, w_o):
    B, S, _ = h.shape
    q = (h @ w_q).reshape(B, S, MEM_HEADS, MEM_HEAD_DIM)
    k, v = jnp.split(mem_n @ w_kv, 2, axis=-1)
    k = k.reshape(B, N_MEM, MEM_HEADS, MEM_HEAD_DIM)
    v = v.reshape(B, N_MEM, MEM_HEADS, MEM_HEAD_DIM)
    s = jnp.einsum('bshd,bmhd->bhsm', q, k).astype(jnp.float32) * (MEM_HEAD_DIM ** -0.5)
    p = jax.nn.softmax(s, axis=-1).astype(v.dtype)
    o = jnp.einsum('bhsm,bmhd->bshd', p, v).reshape(B, S, D_MODEL)
    return o @ w_o


def conv_glu_ffn(h, w_up, w_conv, w_down):
    up = causal_dwconv(h @ w_up, w_conv)
    a, b = jnp.split(up, 2, axis=-1)
    return (jax.nn.silu(a) * b) @ w_down


def setup_inputs(seed: int = 0) -> dict:
    key = jax.random.key(seed)
    ks = jax.random.split(key, 26)
    f32 = jnp.float32
    L = DEPTH

    def nrm(k, shape, scale):
        return jax.random.normal(k, shape, f32) * scale

    def gain(k, shape):
        return 1.0 + 0.02 * jax.random.normal(k, shape, f32)

    return {
        'x': nrm(ks[0], (BATCH, SEQ, D_MODEL), 1.0),
        'mem': nrm(ks[1], (BATCH, N_MEM, D_MODEL), 1.0),
        'g_mix': gain(ks[2], (L, D_MODEL)),
        'w_in': nrm(ks[3], (L, D_MODEL, IN_COLS), D_MODEL ** -0.5),
        'g_sgu': gain(ks[4], (L, A_WIDTH)),
        'w_s': nrm(ks[5], (L, A_GROUPS, A_BLOCK, A_BLOCK), A_BLOCK ** -0.5),
        'b_s': 1.0 + 0.1 * jax.random.normal(ks[6], (L, A_GROUPS, A_BLOCK), f32),
        'b_fox_f': jax.random.uniform(ks[7], (L, B_HEADS), f32, 1.0, 5.0),
        'w_conv_c': nrm(ks[8], (L, C_CONV, 2 * C_WIDTH), C_CONV ** -0.5),
        'b_mlstm_i': nrm(ks[9], (L, C_HEADS), 0.5),
        'b_mlstm_f': jax.random.uniform(ks[10], (L, C_HEADS), f32, 3.0, 6.0),
        'g_mh': gain(ks[11], (L, C_WIDTH)),
        'w_branch': nrm(ks[12], (L, N_BRANCH, BRANCH_WIDTH, D_MODEL), BRANCH_WIDTH ** -0.5),
        'w_out': nrm(ks[13], (L, D_MODEL, D_MODEL), D_MODEL ** -0.5),
        'g_mem_q': gain(ks[14], (L, D_MODEL)),
        'g_mem_kv': gain(ks[15], (L, D_MODEL)),
        'w_mq': nrm(ks[16], (L, D_MODEL, D_MODEL), D_MODEL ** -0.5),
        'w_mkv': nrm(ks[17], (L, D_MODEL, 2 * D_MODEL), D_MODEL ** -0.5),
        'w_mo': nrm(ks[18], (L, D_MODEL, D_MODEL), D_MODEL ** -0.5),
        'g_ffn': gain(ks[19], (L, D_MODEL)),
        'w_up': nrm(ks[20], (L, D_MODEL, 2 * D_FF), D_MODEL ** -0.5),
        'w_ffn_conv': nrm(ks[21], (L, FFN_CONV, 2 * D_FF), FFN_CONV ** -0.5),
        'w_down': nrm(ks[22], (L, D_FF, D_MODEL), D_FF ** -0.5),
        'g_final': gain(ks[23], (D_MODEL,)),
    }


def reference(x, mem, g_mix, w_in, g_sgu, w_s, b_s, b_fox_f, w_conv_c, b_mlstm_i, b_mlstm_f,
              g_mh, w_branch, w_out, g_mem_q, g_mem_kv, w_mq, w_mkv, w_mo, g_ffn, w_up,
              w_ffn_conv, w_down, g_final):
    B, S, _ = x.shape
    splits = [A_COLS, A_COLS + B_COLS, A_COLS + B_COLS + C_COLS]
    for i in range(DEPTH):
        h = rmsnorm(x, g_mix[i])
        pa, pb, pc, pg = jnp.split(h @ w_in[i], splits, axis=-1)

        u, v = jnp.split(jax.nn.gelu(pa), 2, axis=-1)
        ya = spatial_gating(u, v, g_sgu[i], w_s[i], b_s[i])

        qb = pb[..., :B_WIDTH].reshape(B, S, B_HEADS, B_HEAD_DIM)
        kb = pb[..., B_WIDTH:2 * B_WIDTH].reshape(B, S, B_HEADS, B_HEAD_DIM)
        vb = pb[..., 2 * B_WIDTH:3 * B_WIDTH].reshape(B, S, B_HEADS, B_HEAD_DIM)
        fb = pb[..., 3 * B_WIDTH:] + b_fox_f[i]
        yb = forgetting_attention(qb, kb, vb, fb)

        qk = jax.nn.silu(causal_dwconv(pc[..., :2 * C_WIDTH], w_conv_c[i]))
        qc = qk[..., :C_WIDTH].reshape(B, S, C_HEADS, C_HEAD_DIM)
        kc = qk[..., C_WIDTH:].reshape(B, S, C_HEADS, C_HEAD_DIM)
        vc = pc[..., 2 * C_WIDTH:3 * C_WIDTH].reshape(B, S, C_HEADS, C_HEAD_DIM)
        ic = pc[..., 3 * C_WIDTH:3 * C_WIDTH + C_HEADS] + b_mlstm_i[i]
        fc = pc[..., 3 * C_WIDTH + C_HEADS:3 * C_WIDTH + 2 * C_HEADS] + b_mlstm_f[i]
        oc = pc[..., 3 * C_WIDTH + 2 * C_HEADS:]
        hc = mlstm_chunkwise(qc, kc, vc, ic, fc)
        hc = rmsnorm(hc.reshape(B, S, C_HEADS, C_HEAD_DIM),
                     g_mh[i].reshape(C_HEADS, C_HEAD_DIM)).reshape(B, S, C_WIDTH)
        yc = jax.nn.sigmoid(oc) * hc

        ys = jnp.stack([ya, yb, yc], axis=2)
        branches = jnp.einsum('bsrc,rcd->bsrd', ys, w_branch[i])
        gates = jax.nn.sigmoid(pg).reshape(B, S, N_BRANCH, D_MODEL)
        x = x + jnp.sum(gates * branches, axis=2) @ w_out[i]

        x = x + memory_attention(rmsnorm(x, g_mem_q[i]), rmsnorm(mem, g_mem_kv[i]),
                                 w_mq[i], w_mkv[i], w_mo[i])

        x = x + conv_glu_ffn(rmsnorm(x, g_ffn[i]), w_up[i], w_ffn_conv[i], w_down[i])
    return rmsnorm(x, g_final)
```

```python
import numpy as np
from contextlib import ExitStack
import concourse.bass as bass
import concourse.mybir as mybir
from concourse.bass_utils import run_bass_kernel_spmd

F32 = mybir.dt.float32
BF16 = mybir.dt.bfloat16
ALU = mybir.AluOpType
AF = mybir.ActivationFunctionType

D = 1024
S_LEN = 4096
DEPTH = 4
NMEM = 256
TB = 512
NT = S_LEN // TB
NBLK = S_LEN // 128
DFF = 2816
EPS = 1e-6
IN_COLS = 7696
A_U, A_V = 0, 512
B_Q, B_K, B_V, B_F = 1024, 1536, 2048, 2560
C_Q, C_K, C_V, C_I, C_F, C_O = 2568, 3080, 3592, 4104, 4108, 4112
G_0 = 4624
PC_GMIX, PC_GMQ, PC_GMKV, PC_GFFN, PC_CONVC, PC_FCONV, PC_GNEXT = 0, 8, 16, 24, 32, 64, 196
NPC = 204
PR_GSGU, PR_BS, PR_BFOX, PR_BI, PR_BF, PR_GMH = 0, 512, 1024, 1280, 1408, 1536
NPR = 2048


class Res:
    __slots__ = ("w", "r")

    def __init__(self):
        self.w = None
        self.r = []


class Sched:
    def __init__(self, nc):
        self.nc = nc
        self.eng = {"pe": nc.tensor, "dve": nc.vector, "act": nc.scalar, "pool": nc.gpsimd, "sp": nc.sync}
        self.sem = {k: nc.alloc_semaphore(name="s_" + k) for k in self.eng}
        self.cnt = {k: 0 for k in self.eng}
        self.seen = {k: {} for k in self.eng}
        self.dsem = {}
        self.dcnt = {}
        self.ninst = 0

    def _deps(self, e, reads, writes):
        deps = {}

        def add(tok, raw):
            k, v = tok
            if k == e and e == "pe":
                return
            if deps.get(k, 0) < v:
                deps[k] = v

        for r in reads:
            if r.w is not None:
                add(r.w, True)
        for w in writes:
            if w.w is not None:
                add(w.w, False)
            for t in w.r:
                add(t, False)
        return deps

    def _semobj(self, k):
        return self.sem[k] if k in self.sem else self.dsem[k]

    def _emit_waits(self, e, deps):
        seen = self.seen[e]
        need = [(k, v) for k, v in deps.items() if seen.get(k, 0) < v]
        E = self.eng[e]
        for (k, v) in need[1:]:
            E.wait_ge(self._semobj(k), v)
            seen[k] = v
            self.ninst += 1
        return need[0] if need else None

    def _mark(self, tok, reads, writes):
        for r in reads:
            r.r.append(tok)
        for w in writes:
            w.w = tok
            w.r = []

    def op(self, e, fn, reads=(), writes=()):
        first = self._emit_waits(e, self._deps(e, reads, writes))
        ins = fn(self.eng[e])
        if first:
            ins._wait_ge(self._semobj(first[0]), first[1])
            self.seen[e][first[0]] = first[1]
        self.cnt[e] += 1
        self.ninst += 1
        ins.then_inc(self.sem[e], 1)
        tok = (e, self.cnt[e])
        self._mark(tok, reads, writes)
        return tok

    def dma(self, q, out, in_, reads=(), writes=(), **kw):
        first = self._emit_waits(q, self._deps(q, reads, writes))
        i = self.dcnt.get(q, 0)
        self.dcnt[q] = i + 1
        key = "d_%s_%d" % (q, i % 16)
        if key not in self.dsem:
            self.dsem[key] = self.nc.alloc_semaphore(name=key)
            self.dcnt[key] = 0
        elif self.seen[q].get(key, 0) < self.dcnt[key]:
            self.eng[q].wait_ge(self.dsem[key], self.dcnt[key])
            self.seen[q][key] = self.dcnt[key]
            self.ninst += 1
        ins = self.eng[q].dma_start(out=out, in_=in_, **kw)
        if first:
            ins._wait_ge(self._semobj(first[0]), first[1])
            self.seen[q][first[0]] = first[1]
        self.dcnt[key] += 16
        self.ninst += 1
        ins.then_inc(self.dsem[key], 16)
        tok = (key, self.dcnt[key])
        self._mark(tok, reads, writes)
        return tok

    def barrier(self):
        snap = [(k, self.cnt[k]) for k in self.sem if k != "sp" and self.cnt[k] > 0]
        snap += [(k, self.dcnt[k]) for k in self.dsem]
        for e in self.eng:
            seen = self.seen[e]
            for (k, v) in snap:
                if k != e and seen.get(k, 0) < v:
                    self.eng[e].wait_ge(self._semobj(k), v)
                    seen[k] = v
                    self.ninst += 1

    def finish_all(self):
        self.barrier()


class WRes:
    def __init__(self):
        self.parts = []

    def at(self, c):
        for c0, c1, r in self.parts:
            if c0 <= c < c1:
                return r
        raise KeyError(c)

    def all(self):
        return [r for _, _, r in self.parts]


def _flat(lst):
    out = []
    for x in lst:
        if isinstance(x, (list, tuple)):
            out.extend(_flat(x))
        else:
            out.append(x)
    return out


class Rot:
    def __init__(self, items):
        self.items = items
        self.i = 0

    def next(self):
        it = self.items[self.i % len(self.items)]
        self.i += 1
        return it


def build_program(depth=DEPTH, stop=None, debug=False):
    nc = bass.Bass("TRN2", target_bir_lowering=False)
    S = Sched(nc)
    ctx = ExitStack()

    def dram_in(name, shape, dt=F32):
        return nc.dram_tensor(name, shape, dt, kind="ExternalInput").ap()

    skind = "ExternalOutput" if debug else "Internal"

    xT_in = dram_in("xT", [NT, 128, 8 * TB])
    memT_in = dram_in("memT", [D, NMEM])
    w_in = dram_in("w_in", [DEPTH, D, IN_COLS])
    wsT_in = dram_in("wsT", [DEPTH, 4, 128, 128])
    w_branch = dram_in("w_branch", [DEPTH, 3, 512, D])
    w_out = dram_in("w_out", [DEPTH, D, D])
    w_mq = dram_in("w_mq", [DEPTH, D, D])
    w_mkv = dram_in("w_mkv", [DEPTH, D, 2 * D])
    w_mo = dram_in("w_mo", [DEPTH, D, D])
    w_up = dram_in("w_up", [DEPTH, D, 2 * DFF])
    w_down = dram_in("w_down", [DEPTH, DFF, D])
    pcols_in = dram_in("pcols", [DEPTH, 128, NPC])
    prows_in = dram_in("prows", [DEPTH, NPR])
    yT = nc.dram_tensor("yT", [NT, 128, 8 * TB], F32, kind="ExternalOutput").ap()
    X = nc.dram_tensor("Xs", [NT, 128, 8 * TB], F32, kind=skind).ap()
    Mr = [nc.dram_tensor("M%d" % r, [NT, 128, 8 * TB], BF16, kind=skind).ap() for r in range(3)]
    YB = nc.dram_tensor("YBs", [512, S_LEN], BF16, kind=skind).ap()
    Hdbg = nc.dram_tensor("Hdbg", [D, S_LEN], BF16, kind="ExternalOutput").ap() if debug else None

    def fm(ap2d):
        return ap2d.rearrange("(c p) t -> p c t", p=128)

    def tmaj(ap3d, t):
        return ap3d[t].rearrange("p (c t) -> p c t", t=TB)

    r_X = [Res() for _ in range(NT)]
    r_M = [[Res() for _ in range(NT)] for _ in range(3)]
    r_YB = [Res() for _ in range(NT)]
    r_y = [Res() for _ in range(NT)]

    def sb(name, shape, dt):
        return nc.alloc_sbuf_tensor(name, shape, dt)

    H = sb("H", [128, 8, S_LEN], BF16)
    r_H = [Res() for _ in range(NT)]
    ident = sb("ident", [128, 128], BF16)
    ones_bf = sb("ones_bf", [128, 128], BF16)
    tri_bf = sb("tri_bf", [128, 128], BF16)
    tri4 = sb("tri4", [128, 512], BF16)
    tri4s = sb("tri4s", [128, 512], F32)
    U32 = sb("U32", [128, 128], F32)
    ones32 = sb("ones32", [128, 128], F32)
    Umid = sb("Umid", [128, 128], F32)
    pcols = sb("pcols_sb", [128, NPC], F32)
    prows = sb("prows_sb", [128, NPR], F32)
    wtap = sb("wtap", [128, 32], F32)
    r_const, r_pc, r_pr, r_wtap = Res(), Res(), Res(), Res()

    ps = [nc.alloc_psum_tensor("ps%d" % i, [128, 512], F32) for i in range(7)]
    pst = nc.alloc_psum_tensor("pst", [128, 1024], BF16)
    r_ps = [Res() for _ in range(9)]

    S.op("pool", lambda E: E.memset(ones_bf[:], 1.0), writes=[r_const])
    S.op("pool", lambda E: E.memset(ones32[:], 1.0), writes=[r_const])
    S.op("pool", lambda E: E.memset(ident[:], 1.0), writes=[r_const])
    S.op("pool", lambda E: E.affine_select(out=ident[:], in_=ident[:], pattern=[[1, 128]], compare_op=ALU.is_equal,
                                           fill=0.0, base=0, channel_multiplier=-1), reads=[r_const], writes=[r_const])
    S.op("pool", lambda E: E.memset(tri_bf[:], 1.0), writes=[r_const])
    S.op("pool", lambda E: E.affine_select(out=tri_bf[:], in_=tri_bf[:], pattern=[[1, 128]], compare_op=ALU.is_ge,
                                           fill=0.0, base=0, channel_multiplier=-1), reads=[r_const], writes=[r_const])
    for h_ in range(4):
        S.op("pool", lambda E: E.tensor_copy(out=tri4[:, h_ * 128:(h_ + 1) * 128], in_=tri_bf[:]), reads=[r_const], writes=[r_const])
    S.op("pool", lambda E: E.tensor_scalar(out=tri4s[:], in0=tri4[:], scalar1=float(128 ** -0.5), scalar2=None, op0=ALU.mult), reads=[r_const], writes=[r_const])
    S.op("pool", lambda E: E.memset(U32[:], 1.0), writes=[r_const])
    S.op("pool", lambda E: E.affine_select(out=U32[:], in_=U32[:], pattern=[[1, 128]], compare_op=ALU.is_ge,
                                           fill=0.0, base=0, channel_multiplier=-1), reads=[r_const], writes=[r_const])
    S.op("pool", lambda E: E.memset(Umid[:], 1.0), writes=[r_const])
    S.op("pool", lambda E: E.affine_select(out=Umid[:], in_=Umid[:], pattern=[[0, 128]], compare_op=ALU.is_ge,
                                           fill=0.0, base=64, channel_multiplier=-1), reads=[r_const], writes=[r_const])

    def mm(out, lhsT, rhs, start, stop, reads, writes):
        S.op("pe", lambda E: E.matmul(out, lhsT=lhsT, rhs=rhs, start=start, stop=stop), reads=_flat(reads), writes=writes)

    def load_w(dst, src2d, wres, ncol_split=512, cols=None):
        n = src2d.shape[1]
        v = src2d.rearrange("(c p) n -> p c n", p=128)
        rng = [(c0, min(n, c0 + ncol_split)) for c0 in range(0, n, ncol_split)] if cols is None else [cols]
        for (c0, c1) in rng:
            r = Res()
            S.dma("pool", dst[:, :, c0:c1], v[:, :, c0:c1], writes=[r])
            wres.parts.append((c0, c1, r))

    def norm_tile(ph, xt, r_xt, gcol0, out_views, r_out, w=TB, after_chunk=None, defer=False):
        ssb = ph["ps_ss"]
        sqs = []
        for c in range(8):
            sq, r_sq = ph["sq"].next()
            S.op("act", lambda E: E.activation(out=sq[:, 0:w], in_=xt[:, c, :], func=AF.Square), reads=[r_xt], writes=[r_sq])
            sqs.append((sq, r_sq))
            if not defer:
                mm(ps[ssb][:, 0:w], ones_bf[:], sq[:, 0:w], c == 0, c == 7, [r_sq, r_const], [r_ps[ssb]])

        def tail():
            if defer:
                for c, (sq, r_sq) in enumerate(sqs):
                    mm(ps[ssb][:, 0:w], ones_bf[:], sq[:, 0:w], c == 0, c == 7, [r_sq, r_const], [r_ps[ssb]])
            lnb, r_ln = ph["lnb"]
            S.op("act", lambda E: E.activation(out=lnb[:, 0:w], in_=ps[ssb][:, 0:w], func=AF.Ln, scale=1.0 / D, bias=EPS),
                 reads=[r_ps[ssb]], writes=[r_ln])
            S.op("act", lambda E: E.activation(out=lnb[:, 0:w], in_=lnb[:, 0:w], func=AF.Exp, scale=-0.5), reads=[r_ln], writes=[r_ln])
            for c in range(8):
                ov, r_ov = out_views(c)
                S.op("dve", lambda E: E.scalar_tensor_tensor(out=ov, in0=xt[:, c, :], scalar=pcols[:, gcol0 + c:gcol0 + c + 1],
                                                             in1=lnb[:, 0:w], op0=ALU.mult, op1=ALU.mult),
                     reads=[r_xt, r_ln, r_pc], writes=[r_ov])
                if after_chunk is not None:
                    after_chunk(c, ov, r_ov)

        if defer:
            return tail
        tail()
        return None

    def phase_begin():
        S.barrier()
        return ExitStack()

    uid = [0]

    def talloc(es, name, shape, dt, n=1):
        items = []
        for i in range(n):
            uid[0] += 1
            t = es.enter_context(nc.sbuf_tensor("%s_%d_%d" % (name, i, uid[0]), shape, dt))
            items.append((t, Res()))
        return items

    def gate_branch(r, t, yT_views, r_y, Wg, Wb, r_w, es_bufs, hook=None):
        for oc in range(8):
            bb = es_bufs["bank_b"].next()
            bg = es_bufs["bank_g"].next()
            for k in range(8):
                mm(ps[bg][:], Wg[:, k, oc * 128:(oc + 1) * 128], H[:, k, t * TB:(t + 1) * TB], k == 0, k == 7,
                   [r_w[0].at(oc * 128), r_H[t]], [r_ps[bg]])
            for k in range(4):
                mm(ps[bb][:], Wb[:, k, oc * 128:(oc + 1) * 128], yT_views(k), k == 0, k == 3, [r_w[1].at(oc * 128), r_y], [r_ps[bb]])
            tg, r_tg = es_bufs["tg"].next()
            S.op("act", lambda E: E.activation(out=tg[:], in_=ps[bg][:], func=AF.Tanh, scale=0.5), reads=[r_ps[bg]], writes=[r_tg])
            msb, r_msb = es_bufs["msb"].next()
            S.op("dve", lambda E: E.scalar_tensor_tensor(out=msb[:], in0=tg[:], scalar=1.0, in1=ps[bb][:],
                                                         op0=ALU.add, op1=ALU.mult), reads=[r_tg, r_ps[bb]], writes=[r_msb])
            S.dma("sp", Mr[r][t][:, oc * TB:(oc + 1) * TB], msb[:], reads=[r_msb], writes=[r_M[r][t]])
            if hook is not None:
                hook(oc)

    for l in range(depth):
        S.barrier()
        S.dma("sp", pcols[:], pcols_in[l], writes=[r_pc])
        S.dma("sp", prows[:], prows_in[l].partition_broadcast(128), writes=[r_pr])
        S.op("dve", lambda E: E.tensor_scalar(out=wtap[:, 0:16], in0=pcols[:, PC_CONVC:PC_CONVC + 16], scalar1=0.5, scalar2=None,
                                              op0=ALU.mult), reads=[r_pc], writes=[r_wtap])
        S.op("dve", lambda E: E.tensor_scalar(out=wtap[:, 16:32], in0=pcols[:, PC_CONVC + 16:PC_CONVC + 32],
                                              scalar1=0.5 * (128 ** -0.5), scalar2=None, op0=ALU.mult), reads=[r_pc], writes=[r_wtap])

        if l == 0:
            with phase_begin() as es:
                ph = {"ps_ss": 0, "sq": Rot(talloc(es, "sq", [128, 512], BF16, 2)), "lnb": talloc(es, "lnb", [128, 512], F32)[0]}
                xts = Rot(talloc(es, "xt", [128, 8, TB], F32, 2))
                for t in range(NT):
                    xt, r_xt = xts.next()
                    S.dma("sp", xt[:], tmaj(xT_in, t), writes=[r_xt])
                    S.dma("act", tmaj(X, t), xt[:], reads=[r_xt], writes=[r_X[t]])
                    norm_tile(ph, xt, r_xt, PC_GMIX, lambda c: (H[:, c, t * TB:(t + 1) * TB], r_H[t]), None)
        if stop == "P0":
            break

        with phase_begin() as es:
            Wu = talloc(es, "Wu", [128, 8, 512], BF16)[0][0]
            r_Wu = WRes()
            Wv = talloc(es, "Wv", [128, 8, 512], BF16)[0][0]
            r_Wv = WRes()
            Wg = talloc(es, "Wg", [128, 8, D], BF16)[0][0]
            r_Wg = WRes()
            Wb = talloc(es, "Wb", [128, 4, D], BF16)[0][0]
            r_Wb = WRes()
            wsT32, r_ws32 = talloc(es, "wsT32", [128, 4, 128], F32)[0]
            wsT, r_ws = talloc(es, "wsT", [128, 4, 128], BF16)[0]
            load_w(Wu, w_in[l][:, A_U:A_U + 512], r_Wu)
            load_w(Wv, w_in[l][:, A_V:A_V + 512], r_Wv)
            S.dma("sp", wsT32[:], wsT_in[l].rearrange("g s t -> s g t"), writes=[r_ws32])
            S.op("dve", lambda E: E.memset(wsT32[64:128, :, 0:64], 0.0), writes=[r_ws32])
            S.op("dve", lambda E: E.tensor_copy(out=wsT[:], in_=wsT32[:]), reads=[r_ws32], writes=[r_ws])
            load_w(Wg, w_in[l][:, G_0:G_0 + D], r_Wg)
            load_w(Wb, w_branch[l, 0], r_Wb)
            u_sb = talloc(es, "u_sb", [128, 4, TB], F32)[0]
            v_sb = talloc(es, "v_sb", [128, 4, 512], F32)[0]
            v_n = talloc(es, "v_n", [128, 4, 512], BF16)[0]
            ssv = talloc(es, "ssv", [128, 4], F32)[0]
            junk = talloc(es, "junk", [128, 512], BF16)[0]
            yaT = Rot(talloc(es, "yaT", [128, 4, TB], BF16, 2))
            tmpm = Rot(talloc(es, "tmpm", [128, TB], F32, 2))
            bufs = {"msb": Rot(talloc(es, "msb", [128, TB], BF16, 3)), "tg": Rot(talloc(es, "tg", [128, TB], F32, 2)),
                    "bank_b": Rot([6, 0]), "bank_g": Rot([1, 2])}
            bu = Rot([0, 1])
            bv = Rot([2, 3])
            bm = Rot([4, 5])
            mh4 = talloc(es, "mh4", [128, 4], F32)[0]
            S.op("pool", lambda E: E.memset(mh4[0][:], -0.5), writes=[mh4[1]])
            for t in range(NT):
                tok = slice(t * TB, (t + 1) * TB)
                S.op("dve", lambda E: E.memset(ssv[0][:], 0.0), writes=[ssv[1]])
                for blk in range(4):
                    b = bv.next()
                    tk = slice(t * TB + blk * 128, t * TB + (blk + 1) * 128)
                    for k in range(8):
                        mm(ps[b][:], H[:, k, tk], Wv[:, k, :], k == 0, k == 7, [r_Wv.all(), r_H[t]], [r_ps[b]])
                    S.op("act", lambda E: E.activation(out=v_sb[0][:, blk, :], in_=ps[b][:], func=AF.Gelu_apprx_tanh),
                         reads=[r_ps[b]], writes=[v_sb[1]])
                    S.op("act", lambda E: E.activation(out=junk[0][:], in_=v_sb[0][:, blk, :], func=AF.Square,
                                                       accum_out=ssv[0][:, blk:blk + 1]), reads=[v_sb[1], ssv[1]], writes=[junk[1], ssv[1]])
                for g in range(4):
                    b = bu.next()
                    for k in range(8):
                        mm(ps[b][:], Wu[:, k, g * 128:(g + 1) * 128], H[:, k, tok], k == 0, k == 7, [r_Wu.at(g * 128), r_H[t]], [r_ps[b]])
                    S.op("act", lambda E: E.activation(out=u_sb[0][:, g, :], in_=ps[b][:], func=AF.Gelu_apprx_tanh),
                         reads=[r_ps[b]], writes=[u_sb[1]])
                S.op("dve", lambda E: E.tensor_scalar(out=ssv[0][:], in0=ssv[0][:], scalar1=1.0 / 512, scalar2=EPS, op0=ALU.mult,
                                                      op1=ALU.add), reads=[ssv[1]], writes=[ssv[1]])
                S.op("pool", lambda E: E.tensor_tensor(out=ssv[0][:], in0=ssv[0][:], in1=mh4[0][:], op=ALU.pow),
                     reads=[ssv[1], mh4[1]], writes=[ssv[1]])
                for blk in range(4):
                    S.op("dve", lambda E: E.scalar_tensor_tensor(out=v_n[0][:, blk, :], in0=v_sb[0][:, blk, :], scalar=ssv[0][:, blk:blk + 1],
                                                                 in1=prows[:, PR_GSGU:PR_GSGU + 512], op0=ALU.mult, op1=ALU.mult),
                         reads=[v_sb[1], ssv[1], r_pr], writes=[v_n[1]])
                ya, r_ya = yaT.next()
                for g in range(4):
                    b = bm.next()
                    for blk in range(4):
                        mm(ps[b][:, blk * 128:(blk + 1) * 128], v_n[0][:, blk, g * 128:(g + 1) * 128], wsT[:, g, :], True, True,
                           [v_n[1], r_ws], [r_ps[b]])
                    tm, r_tm = tmpm.next()
                    for blk in range(4):
                        S.op("dve", lambda E: E.tensor_tensor(out=tm[:, blk * 128:(blk + 1) * 128], in0=ps[b][:, blk * 128:(blk + 1) * 128],
                                                              in1=prows[:, PR_BS + g * 128:PR_BS + (g + 1) * 128], op=ALU.add),
                             reads=[r_ps[b], r_pr], writes=[r_tm])
                    S.op("dve", lambda E: E.tensor_tensor(out=ya[:, g, :], in0=tm[:], in1=u_sb[0][:, g, :], op=ALU.mult),
                         reads=[r_tm, u_sb[1]], writes=[r_ya])
                gate_branch(0, t, lambda k: ya[:, k, :], r_ya, Wg, Wb, [r_Wg, r_Wb], bufs)
        if stop == "P1":
            break

        with phase_begin() as es:
            Wq = talloc(es, "Wq", [128, 8, 512], BF16)[0][0]
            r_Wq = WRes()
            Wk = talloc(es, "Wk", [128, 8, 512], BF16)[0][0]
            r_Wk = WRes()
            Wv = talloc(es, "Wv", [128, 8, 512], BF16)[0][0]
            r_Wv = WRes()
            Wf = talloc(es, "Wf", [128, 8, 8], BF16)[0][0]
            r_Wf = WRes()
            load_w(Wf, w_in[l][:, B_F:B_F + 8], r_Wf)
            load_w(Wq, w_in[l][:, B_Q:B_Q + 512], r_Wq)
            load_w(Wk, w_in[l][:, B_K:B_K + 512], r_Wk)
            load_w(Wv, w_in[l][:, B_V:B_V + 512], r_Wv)
            for blk in range(NBLK):
                for k in range(8):
                    mm(ps[0][:, blk * 8:(blk + 1) * 8], H[:, k, blk * 128:(blk + 1) * 128], Wf[:, k, :], k == 0, k == 7,
                       [r_Wf.all(), r_H[blk // 4]], [r_ps[0]])
            zf, r_zf = talloc(es, "zf", [128, 256], F32)[0]
            tot, r_tot = talloc(es, "tot", [128, 256], F32)[0]
            pre, r_pre = talloc(es, "pre", [128, 256], F32)[0]
            Csp, r_Csp = talloc(es, "Csp", [128, 256], F32)[0]
            Rsp, r_Rsp = talloc(es, "Rsp", [128, 256], F32)[0]
            S.op("dve", lambda E: E.tensor_tensor(out=zf[:], in0=ps[0][:, 0:256], in1=prows[:, PR_BFOX:PR_BFOX + 256], op=ALU.add),
                 reads=[r_ps[0], r_pr], writes=[r_zf])
            S.op("act", lambda E: E.activation(out=zf[:], in_=zf[:], func=AF.Exp, scale=-1.0), reads=[r_zf], writes=[r_zf])
            S.op("act", lambda E: E.activation(out=zf[:], in_=zf[:], func=AF.Ln, bias=1.0), reads=[r_zf], writes=[r_zf])
            mm(ps[1][:, 0:256], U32[:], zf[:], True, True, [r_const, r_zf], [r_ps[1]])
            mm(ps[2][:, 0:256], ones32[:], zf[:], True, True, [r_const, r_zf], [r_ps[2]])
            mm(ps[3][:, 0:256], Umid[:], zf[:], True, True, [r_const, r_zf], [r_ps[3]])
            S.op("dve", lambda E: E.tensor_copy(out=tot[:], in_=ps[2][:, 0:256]), reads=[r_ps[2]], writes=[r_tot])
            S.op("dve", lambda E: E.memset(pre[:], 0.0), writes=[r_pre])
            pre3 = pre[:].rearrange("p (m h) -> p m h", h=8)
            tot3 = tot[:].rearrange("p (m h) -> p m h", h=8)
            for h in range(8):
                S.op("dve", lambda E: E.tensor_tensor_scan(out=pre3[:, 1:32, h], data0=ones32[:, 0:31], data1=tot3[:, 0:31, h], initial=0.0,
                                                           op0=ALU.mult, op1=ALU.add), reads=[r_tot, r_const], writes=[r_pre])
            S.op("dve", lambda E: E.tensor_tensor(out=Csp[:], in0=ps[1][:, 0:256], in1=pre[:], op=ALU.add), reads=[r_ps[1], r_pre], writes=[r_Csp])
            S.op("dve", lambda E: E.tensor_tensor(out=Rsp[:], in0=ps[3][:, 0:256], in1=pre[:], op=ALU.add), reads=[r_ps[3], r_pre], writes=[r_Rsp])
            Csp3 = Csp[:].rearrange("p (m h) -> p m h", h=8)

            qT, r_qT = talloc(es, "qT", [128, S_LEN], BF16)[0]
            kT, r_kT = talloc(es, "kT", [128, S_LEN], BF16)[0]
            vaug, r_va = talloc(es, "vaug", [128, NBLK, 192], BF16)[0]
            biasT, r_bias = talloc(es, "biasT", [128, 2, NBLK, NBLK], F32)[0]
            ybT, r_ybT = talloc(es, "ybT", [128, S_LEN], BF16)[0]
            PTs = Rot(talloc(es, "PT", [128, TB], BF16, 5))
            recs = Rot(talloc(es, "rec", [128, TB], F32, 2))
            bqk = Rot([0, 1])
            bvv = Rot([2])
            bS = Rot([0, 1, 2, 5])
            bA = Rot([3, 4])
            bB = Rot([6])
            combs = Rot(talloc(es, "comb", [128, TB], F32, 2))
            alpha, r_alpha = talloc(es, "alpha", [128, 256], F32)[0]
            Rsp4 = Rsp[:].rearrange("p (t b h) -> p t b h", b=4, h=8)
            al4 = alpha[:].rearrange("p (t b h) -> p t b h", b=4, h=8)
            for b_ in range(4):
                S.op("dve", lambda E: E.tensor_tensor(out=al4[:, :, b_, :], in0=Rsp4[:, :, b_, :], in1=Rsp4[:, :, 0, :], op=ALU.subtract),
                     reads=[r_Rsp], writes=[r_alpha])
            S.op("act", lambda E: E.activation(out=alpha[:], in_=alpha[:], func=AF.Exp, scale=-1.0), reads=[r_alpha], writes=[r_alpha])
            S.op("pool", lambda E: E.memset(vaug[:, :, 64:128], 1.0), writes=[r_va])
            for j in range(4):
                fc = slice(j * 128, (j + 1) * 128)
                for t in range(NT):
                    tok = slice(t * TB, (t + 1) * TB)
                    b = bqk.next()
                    for k in range(8):
                        mm(ps[b][:], Wq[:, k, fc], H[:, k, tok], k == 0, k == 7, [r_Wq.at(j * 128), r_H[t]], [r_ps[b]])
                    S.op("dve", lambda E: E.tensor_copy(out=qT[:, tok], in_=ps[b][:]), reads=[r_ps[b]], writes=[r_qT])
                    b = bqk.next()
                    for k in range(8):
                        mm(ps[b][:], Wk[:, k, fc], H[:, k, tok], k == 0, k == 7, [r_Wk.at(j * 128), r_H[t]], [r_ps[b]])
                    S.op("dve", lambda E: E.tensor_copy(out=kT[:, tok], in_=ps[b][:]), reads=[r_ps[b]], writes=[r_kT])
                    b = bvv.next()
                    for blk in range(4):
                        tk = slice(t * TB + blk * 128, t * TB + (blk + 1) * 128)
                        for k in range(8):
                            mm(ps[b][:, blk * 128:(blk + 1) * 128], H[:, k, tk], Wv[:, k, fc], k == 0, k == 7, [r_Wv.at(j * 128), r_H[t]], [r_ps[b]])
                    psv = ps[b][:].rearrange("p (b c) -> p b c", c=128)
                    S.op("dve", lambda E: E.tensor_copy(out=vaug[:, 4 * t:4 * t + 4, 0:64], in_=psv[:, :, 0:64]),
                         reads=[r_ps[b]], writes=[r_va])
                    S.op("dve", lambda E: E.tensor_copy(out=vaug[:, 4 * t:4 * t + 4, 128:192], in_=psv[:, :, 64:128]),
                         reads=[r_ps[b]], writes=[r_va])
                for hh in range(2):
                    h = 2 * j + hh
                    for n in range(NBLK):
                        S.op("dve", lambda E: E.tensor_scalar(out=biasT[:, hh, n, 0:n + 1], in0=Csp3[:, 0:n + 1, h],
                                                              scalar1=Rsp[:, n * 8 + h:n * 8 + h + 1], scalar2=None, op0=ALU.subtract),
                             reads=[r_Csp, r_Rsp], writes=[r_bias])
                items = []
                for hh in range(2):
                    for t in range(NT):
                        for m in range(4 * t + 4):
                            items.append((hh, t, m))
                state = {}

                def stage1(it):
                    hh, t, m = it
                    prt = slice(64 * hh, 64 * hh + 64)
                    jin = m - 4 * t
                    c0 = max(jin, 0) * 128
                    bs_ = bS.next()
                    mm(ps[bs_][:, c0:TB], kT[prt, m * 128:(m + 1) * 128], qT[prt, t * TB + c0:(t + 1) * TB], True, True,
                       [r_kT, r_qT], [r_ps[bs_]])
                    PT, r_PT = PTs.next()
                    if jin < 0:
                        S.op("act", lambda E: E.activation(out=PT[:], in_=ps[bs_][:], func=AF.Exp, scale=0.125, bias=biasT[:, hh, 4 * t, m:m + 1]),
                             reads=[r_ps[bs_], r_bias], writes=[r_PT])
                    else:
                        for qb in range(jin, 4):
                            n = 4 * t + qb
                            S.op("act", lambda E: E.activation(out=PT[:, qb * 128:(qb + 1) * 128], in_=ps[bs_][:, qb * 128:(qb + 1) * 128],
                                                               func=AF.Exp, scale=0.125, bias=biasT[:, hh, n, m:m + 1]),
                                 reads=[r_ps[bs_], r_bias], writes=[r_PT])
                        S.op("pool", lambda E: E.tensor_tensor(out=PT[:, c0:c0 + 128], in0=PT[:, c0:c0 + 128], in1=tri_bf[:], op=ALU.mult),
                             reads=[r_PT, r_const], writes=[r_PT])
                    state[it] = (PT, r_PT)

                def stage2(it):
                    hh, t, m = it
                    h = 2 * j + hh
                    prt = slice(64 * hh, 64 * hh + 64)
                    oth = slice(64 - 64 * hh, 128 - 64 * hh)
                    vcols = slice(64 * hh, 64 * hh + 128)
                    jin = m - 4 * t
                    c0 = max(jin, 0) * 128
                    PT, r_PT = state.pop(it)
                    if m == 0 and t > 0:
                        state[("A", hh, t)] = bA.next()
                    if jin == 0:
                        state[("B", hh, t)] = bB.next()
                    if jin < 0:
                        ba = state[("A", hh, t)]
                        mm(ps[ba][:], vaug[:, m, vcols], PT[:], m == 0, m == 4 * t - 1, [r_va, r_PT], [r_ps[ba]])
                        return
                    bb = state[("B", hh, t)]
                    mm(ps[bb][:, c0:TB], vaug[:, m, vcols], PT[:, c0:TB], jin == 0, jin == 3, [r_va, r_PT], [r_ps[bb]])
                    if jin < 3:
                        return
                    comb, r_comb = combs.next()
                    S.op("dve", lambda E: E.tensor_copy(out=comb[:], in_=ps[bb][:]), reads=[r_ps[bb]], writes=[r_comb])
                    if t > 0:
                        ba = state.pop(("A", hh, t))
                        for qb in range(4):
                            n = 4 * t + qb
                            S.op("dve", lambda E: E.scalar_tensor_tensor(out=comb[:, qb * 128:(qb + 1) * 128], in0=ps[ba][:, qb * 128:(qb + 1) * 128],
                                                                         scalar=alpha[:, n * 8 + h:n * 8 + h + 1], in1=comb[:, qb * 128:(qb + 1) * 128],
                                                                         op0=ALU.mult, op1=ALU.add), reads=[r_ps[ba], r_comb, r_alpha], writes=[r_comb])
                    state.pop(("B", hh, t))
                    rec, r_rec = recs.next()
                    S.op("dve", lambda E: E.reciprocal(out=rec[prt, :], in_=comb[oth, :]), reads=[r_comb], writes=[r_rec])
                    S.op("dve", lambda E: E.tensor_tensor(out=ybT[prt, t * TB:(t + 1) * TB], in0=comb[prt, :], in1=rec[prt, :], op=ALU.mult),
                         reads=[r_comb, r_rec], writes=[r_ybT])

                LA = 3
                for i, it in enumerate(items):
                    stage1(it)
                    if i >= LA:
                        stage2(items[i - LA])
                for it in items[len(items) - LA:]:
                    stage2(it)
                S.dma("sp", YB[j * 128:(j + 1) * 128, :], ybT[:], reads=[r_ybT], writes=r_YB)
        if stop == "P2a":
            break
        with phase_begin() as es:
            Wg = talloc(es, "Wg", [128, 8, D], BF16)[0][0]
            r_Wg = WRes()
            Wb = talloc(es, "Wb", [128, 4, D], BF16)[0][0]
            r_Wb = WRes()
            load_w(Wg, w_in[l][:, G_0 + D:G_0 + 2 * D], r_Wg)
            load_w(Wb, w_branch[l, 1], r_Wb)
            ybt = Rot(talloc(es, "ybt", [128, 4, TB], BF16, 2))
            bufs = {"msb": Rot(talloc(es, "msb", [128, TB], BF16, 3)), "tg": Rot(talloc(es, "tg", [128, TB], F32, 2)),
                    "bank_b": Rot([0, 1]), "bank_g": Rot([2, 3])}
            for t in range(NT):
                yb, r_yb = ybt.next()
                S.dma("sp", yb[:], YB.rearrange("(c p) t -> p c t", p=128)[:, :, t * TB:(t + 1) * TB], reads=[r_YB[t]], writes=[r_yb])
                gate_branch(1, t, lambda k: yb[:, k, :], r_yb, Wg, Wb, [r_Wg, r_Wb], bufs)
        if stop == "P2":
            break

        with phase_begin() as es:
            Wqk = talloc(es, "Wqk", [128, 8, 1024], BF16)[0][0]
            r_Wqk = WRes()
            Wv = talloc(es, "Wv3", [128, 8, 512], BF16)[0][0]
            r_Wv = WRes()
            Wo = talloc(es, "Wo3", [128, 8, 512], BF16)[0][0]
            r_Wo = WRes()
            Wif = talloc(es, "Wif", [128, 8, 8], BF16)[0][0]
            r_Wif = WRes()
            Wg = talloc(es, "Wg", [128, 8, D], BF16)[0][0]
            r_Wg = WRes()
            Wb = talloc(es, "Wb", [128, 4, D], BF16)[0][0]
            r_Wb = WRes()
            load_w(Wif, w_in[l][:, C_I:C_I + 8], r_Wif)
            load_w(Wqk, w_in[l][:, C_Q:C_Q + 1024], r_Wqk)
            load_w(Wv, w_in[l][:, C_V:C_V + 512], r_Wv)
            load_w(Wo, w_in[l][:, C_O:C_O + 512], r_Wo)
            load_w(Wg, w_in[l][:, G_0 + 2 * D:G_0 + 3 * D], r_Wg)
            load_w(Wb, w_branch[l, 2], r_Wb)
            gmh_h, r_gmh = talloc(es, "gmh_h", [128, 512], F32)[0]
            S.op("dve", lambda E: E.tensor_scalar(out=gmh_h[:], in0=prows[:, PR_GMH:PR_GMH + 512], scalar1=0.5, scalar2=None, op0=ALU.mult),
                 reads=[r_pr], writes=[r_gmh])
            for blk in range(NBLK):
                for k in range(8):
                    mm(ps[0][:, blk * 8:(blk + 1) * 8], H[:, k, blk * 128:(blk + 1) * 128], Wif[:, k, :], k == 0, k == 7,
                       [r_Wif.all(), r_H[blk // 4]], [r_ps[0]])
            ps3 = ps[0][:, 0:256].rearrange("p (b c) -> p b c", c=8)
            gi, r_gi = talloc(es, "gi", [128, 128], F32)[0]
            zf, r_zf = talloc(es, "zf3", [128, 128], F32)[0]
            E1, r_E1 = talloc(es, "E1", [128, 128], F32)[0]
            emb, r_emb = talloc(es, "emb", [128, 128], F32)[0]
            eg, r_eg = talloc(es, "eg", [128, 128], F32)[0]
            v4 = lambda ap: ap.rearrange("p (b c) -> p b c", c=4)
            S.op("dve", lambda E: E.tensor_tensor(out=v4(gi[:]), in0=ps3[:, :, 0:4], in1=v4(prows[:, PR_BI:PR_BI + 128]), op=ALU.add),
                 reads=[r_ps[0], r_pr], writes=[r_gi])
            S.op("dve", lambda E: E.tensor_tensor(out=v4(zf[:]), in0=ps3[:, :, 4:8], in1=v4(prows[:, PR_BF:PR_BF + 128]), op=ALU.add),
                 reads=[r_ps[0], r_pr], writes=[r_zf])
            S.op("act", lambda E: E.activation(out=zf[:], in_=zf[:], func=AF.Exp, scale=-1.0), reads=[r_zf], writes=[r_zf])
            S.op("act", lambda E: E.activation(out=zf[:], in_=zf[:], func=AF.Ln, bias=1.0), reads=[r_zf], writes=[r_zf])
            mm(ps[1][:, 0:128], U32[:], zf[:], True, True, [r_const, r_zf], [r_ps[1]])
            mm(ps[2][:, 0:128], ones32[:], zf[:], True, True, [r_const, r_zf], [r_ps[2]])
            S.op("dve", lambda E: E.tensor_tensor(out=gi[:], in0=ps[1][:, 0:128], in1=gi[:], op=ALU.add), reads=[r_ps[1], r_gi], writes=[r_gi])
            S.op("act", lambda E: E.activation(out=E1[:], in_=gi[:], func=AF.Exp), reads=[r_gi], writes=[r_E1])
            S.op("act", lambda E: E.activation(out=emb[:], in_=ps[1][:, 0:128], func=AF.Exp), reads=[r_ps[1]], writes=[r_emb])
            S.op("act", lambda E: E.activation(out=eg[:], in_=ps[2][:, 0:128], func=AF.Exp, scale=-1.0), reads=[r_ps[2]], writes=[r_eg])

            halo, r_halo = talloc(es, "halo", [128, 8, 3], F32)[0]
            S.op("pool", lambda E: E.memset(halo[:], 0.0), writes=[r_halo])
            prebufs = Rot(talloc(es, "prebuf", [128, 515], F32, 2))
            accs = Rot(talloc(es, "acc", [128, TB], F32, 2))
            tgs = Rot(talloc(es, "tg3", [128, TB], F32, 2))
            qkTs = Rot(talloc(es, "qkT", [128, 8, TB], BF16, 2))
            Vaugs = Rot(talloc(es, "Vaug", [128, 4, 129], BF16, 2))
            ogs = Rot(talloc(es, "og", [128, 512], F32, 2))
            Ats = Rot(talloc(es, "At", [128, 512], BF16, 2))
            ktoks = Rot(talloc(es, "ktok", [128, 512], BF16, 2))
            Cst, r_Cst = talloc(es, "Cst", [128, 4, 129], F32)[0]
            Cbf, r_Cbf = talloc(es, "Cbf", [128, 4, 129], BF16)[0]
            S.op("pool", lambda E: E.memset(Cst[:], 0.0), writes=[r_Cst])
            S.op("pool", lambda E: E.memset(Cbf[:], 0.0), writes=[r_Cbf])
            hhs = Rot(talloc(es, "hh", [128, 4, 128], F32, 2))
            dns = Rot(talloc(es, "dn", [128, 8], F32, 2))
            ss4s = Rot(talloc(es, "ss4", [128, 4], F32, 2))
            junk, r_junk = talloc(es, "junk3", [128, 128], BF16)[0]
            mh4, r_mh4 = talloc(es, "mh4_3", [128, 4], F32)[0]
            S.op("pool", lambda E: E.memset(mh4[:], -0.5), writes=[r_mh4])
            hcs = Rot(talloc(es, "hc", [128, 512], F32, 2))
            yctoks = Rot(talloc(es, "yctok", [128, 512], BF16, 2))
            ycTs = Rot(talloc(es, "ycT", [128, 4, TB], BF16, 1))
            bufs = {"msb": Rot(talloc(es, "msb", [128, TB], BF16, 3)), "tg": Rot(talloc(es, "tgg", [128, TB], F32, 2)),
                    "bank_b": Rot([0, 2]), "bank_g": Rot([1, 3])}
            bqk = Rot([0, 1])
            bvo = Rot([2, 3])
            def conv_chunk(t, c, qkT, r_qkT):
                tok = slice(t * TB, (t + 1) * TB)
                b = bvo.next()
                for k in range(8):
                    mm(ps[b][:], Wqk[:, k, c * 128:(c + 1) * 128], H[:, k, tok], k == 0, k == 7, [r_Wqk.at(c * 128), r_H[t]], [r_ps[b]])
                pbuf, r_pb = prebufs.next()
                S.op("pool", lambda E: E.tensor_copy(out=pbuf[:, 0:3], in_=halo[:, c, :]), reads=[r_halo], writes=[r_pb])
                S.op("act", lambda E: E.activation(out=pbuf[:, 3:515], in_=ps[b][:], func=AF.Copy), reads=[r_ps[b]], writes=[r_pb])
                acc, r_acc = accs.next()
                S.op("act", lambda E: E.activation(out=acc[:], in_=ps[b][:], func=AF.Copy, scale=pcols[:, PC_CONVC + c * 4 + 3:PC_CONVC + c * 4 + 4]),
                     reads=[r_ps[b], r_pc], writes=[r_acc])
                S.op("pool", lambda E: E.tensor_copy(out=halo[:, c, :], in_=pbuf[:, 512:515]), reads=[r_pb], writes=[r_halo])
                for jj in range(0, 3):
                    S.op("dve", lambda E: E.scalar_tensor_tensor(out=acc[:], in0=pbuf[:, jj:jj + 512], scalar=pcols[:, PC_CONVC + c * 4 + jj:PC_CONVC + c * 4 + jj + 1],
                                                                 in1=acc[:], op0=ALU.mult, op1=ALU.add), reads=[r_pb, r_pc, r_acc], writes=[r_acc])
                S.op("act", lambda E: E.activation(out=qkT[:, c, :], in_=acc[:], func=AF.Silu), reads=[r_acc], writes=[r_qkT])

            qk_of = {0: qkTs.next(), 1: qkTs.next()}
            for c in range(8):
                conv_chunk(0, c, *qk_of[0])
            for c in range(8):
                conv_chunk(1, c, *qk_of[1])
            ycT_of = {}
            blkst = {}

            def front(blk):
                t, bi = divmod(blk, 4)
                tk = slice(t * TB + bi * 128, t * TB + (bi + 1) * 128)
                bc = slice(bi * 128, (bi + 1) * 128)
                qkT, r_qkT = qk_of[t]
                b = bvo.next()
                for k in range(8):
                    mm(ps[b][:], H[:, k, tk], Wv[:, k, :], k == 0, k == 7, [r_Wv.all(), r_H[t]], [r_ps[b]])
                Vaug, r_Va = Vaugs.next()
                for h in range(4):
                    S.op("act", lambda E: E.activation(out=Vaug[:, h, 0:128], in_=ps[b][:, h * 128:(h + 1) * 128], func=AF.Copy,
                                                       scale=E1[:, blk * 4 + h:blk * 4 + h + 1]), reads=[r_ps[b], r_E1], writes=[r_Va])
                S.op("act", lambda E: E.activation(out=Vaug[:, :, 128:129], in_=E1[:, blk * 4:blk * 4 + 4].rearrange("p (h o) -> p h o", o=1),
                                                   func=AF.Copy), reads=[r_E1], writes=[r_Va])
                b2 = bvo.next()
                for k in range(8):
                    mm(ps[b2][:], H[:, k, tk], Wo[:, k, :], k == 0, k == 7, [r_Wo.all(), r_H[t]], [r_ps[b2]])
                og, r_og = ogs.next()
                S.op("act", lambda E: E.activation(out=og[:], in_=ps[b2][:], func=AF.Tanh, scale=0.5), reads=[r_ps[b2]], writes=[r_og])
                for h in range(4):
                    mm(ps[4][:, h * 128:(h + 1) * 128], qkT[:, 4 + h, bc], qkT[:, h, bc], True, True, [r_qkT], [r_ps[4]])
                for h in range(4):
                    S.op("pe", lambda E: E.transpose(pst[:, 512 + h * 128:512 + (h + 1) * 128], qkT[:, 4 + h, bc], ident[:]),
                         reads=[r_qkT, r_const], writes=[r_ps[7]])
                At, r_At = Ats.next()
                S.op("dve", lambda E: E.tensor_tensor(out=At[:], in0=ps[4][:], in1=tri4s[:], op=ALU.mult), reads=[r_ps[4], r_const], writes=[r_At])
                ktok, r_kt = ktoks.next()
                S.op("act", lambda E: E.activation(out=ktok[:], in_=pst[:, 512:1024], func=AF.Copy, scale=float(128 ** -0.5)), reads=[r_ps[7]], writes=[r_kt])
                blkst[blk] = (Vaug, r_Va, og, r_og, At, r_At, ktok, r_kt)

            def mid(blk):
                t, bi = divmod(blk, 4)
                bc = slice(bi * 128, (bi + 1) * 128)
                qkT, r_qkT = qk_of[t]
                Vaug, r_Va, og, r_og, At, r_At, ktok, r_kt = blkst[blk]
                for h in range(4):
                    bn = 5 + h // 2
                    cs = slice((h % 2) * 129, (h % 2) * 129 + 129)
                    mm(ps[bn][:, cs], At[:, h * 128:(h + 1) * 128], Vaug[:, h, :], True, False, [r_At, r_Va], [r_ps[bn]])
                    mm(ps[bn][:, cs], qkT[:, h, bc], Cbf[:, h, :], False, True, [r_qkT, r_Cbf], [r_ps[bn]])
                for h in range(4):
                    bk = h // 2
                    cs = slice((h % 2) * 129, (h % 2) * 129 + 129)
                    mm(ps[bk][:, cs], ktok[:, h * 128:(h + 1) * 128], Vaug[:, h, :], True, True, [r_kt, r_Va], [r_ps[bk]])
                for h in range(4):
                    bk = h // 2
                    cs = slice((h % 2) * 129, (h % 2) * 129 + 129)
                    if blk == 0:
                        S.op("dve", lambda E: E.tensor_copy(out=Cst[:, h, :], in_=ps[bk][:, cs]), reads=[r_ps[bk]], writes=[r_Cst])
                    else:
                        S.op("dve", lambda E: E.scalar_tensor_tensor(out=Cst[:, h, :], in0=Cst[:, h, :], scalar=eg[:, (blk - 1) * 4 + h:(blk - 1) * 4 + h + 1],
                                                                     in1=ps[bk][:, cs], op0=ALU.mult, op1=ALU.add),
                             reads=[r_Cst, r_eg, r_ps[bk]], writes=[r_Cst])
                for h in range(4):
                    S.op("act", lambda E: E.activation(out=Cbf[:, h, :], in_=Cst[:, h, :], func=AF.Copy, scale=eg[:, blk * 4 + h:blk * 4 + h + 1]),
                         reads=[r_Cst, r_eg], writes=[r_Cbf])

            def backA(blk):
                hh, r_hh = hhs.next()
                dn, r_dn = dns.next()
                for bn_ in range(2):
                    bn = 5 + bn_
                    wv = ps[bn][:, 0:258].rearrange("p (h c) -> p h c", c=129)[:, :, 128]
                    dcol = slice(2 * bn_, 2 * bn_ + 2)
                    S.op("dve", lambda E: E.tensor_scalar(out=dn[:, dcol], in0=wv, scalar1=-1.0, scalar2=None, op0=ALU.mult),
                         reads=[r_ps[bn]], writes=[r_dn])
                    S.op("dve", lambda E: E.tensor_tensor(out=dn[:, dcol], in0=dn[:, dcol], in1=wv, op=ALU.max), reads=[r_ps[bn], r_dn], writes=[r_dn])
                    S.op("dve", lambda E: E.tensor_tensor(out=dn[:, dcol], in0=dn[:, dcol], in1=emb[:, blk * 4 + 2 * bn_:blk * 4 + 2 * bn_ + 2], op=ALU.max),
                         reads=[r_dn, r_emb], writes=[r_dn])
                S.op("dve", lambda E: E.reciprocal(out=dn[:, 4:8], in_=dn[:, 0:4]), reads=[r_dn], writes=[r_dn])
                for h in range(4):
                    bn = 5 + h // 2
                    cs0 = (h % 2) * 129
                    S.op("dve", lambda E: E.tensor_scalar(out=hh[:, h, :], in0=ps[bn][:, cs0:cs0 + 128], scalar1=dn[:, 4 + h:5 + h], scalar2=None, op0=ALU.mult),
                         reads=[r_ps[bn], r_dn], writes=[r_hh])
                ss4, r_ss4 = ss4s.next()
                S.op("pool", lambda E: E.memset(ss4[:], 0.0), writes=[r_ss4])
                for h in range(4):
                    S.op("act", lambda E: E.activation(out=junk[:], in_=hh[:, h, :], func=AF.Square, accum_out=ss4[:, h:h + 1]),
                         reads=[r_hh, r_ss4], writes=[r_junk, r_ss4])
                blkst[("A", blk)] = (hh, r_hh, ss4, r_ss4)

            def backB(blk):
                t, bi = divmod(blk, 4)
                bc = slice(bi * 128, (bi + 1) * 128)
                Vaug, r_Va, og, r_og, At, r_At, ktok, r_kt = blkst.pop(blk)
                hh, r_hh, ss4, r_ss4 = blkst.pop(("A", blk))
                ycT, r_ycT = ycT_of[t]
                S.op("dve", lambda E: E.tensor_scalar(out=ss4[:], in0=ss4[:], scalar1=1.0 / 128, scalar2=EPS, op0=ALU.mult, op1=ALU.add),
                     reads=[r_ss4], writes=[r_ss4])
                S.op("pool", lambda E: E.tensor_tensor(out=ss4[:], in0=ss4[:], in1=mh4[:], op=ALU.pow), reads=[r_ss4, r_mh4], writes=[r_ss4])
                hc, r_hc = hcs.next()
                for h in range(4):
                    S.op("dve", lambda E: E.scalar_tensor_tensor(out=hc[:, h * 128:(h + 1) * 128], in0=hh[:, h, :], scalar=ss4[:, h:h + 1],
                                                                 in1=gmh_h[:, h * 128:(h + 1) * 128], op0=ALU.mult, op1=ALU.mult),
                         reads=[r_hh, r_ss4, r_gmh], writes=[r_hc])
                yct, r_yct = yctoks.next()
                S.op("dve", lambda E: E.scalar_tensor_tensor(out=yct[:], in0=og[:], scalar=1.0, in1=hc[:], op0=ALU.add, op1=ALU.mult),
                     reads=[r_og, r_hc], writes=[r_yct])

                def fin():
                    for c in range(4):
                        S.op("pe", lambda E: E.transpose(pst[:, c * 128:(c + 1) * 128], yct[:, c * 128:(c + 1) * 128], ident[:]),
                             reads=[r_yct, r_const], writes=[r_ps[7]])
                    S.op("act", lambda E: E.activation(out=ycT[:, :, bc], in_=pst[:, 0:512].rearrange("p (c t) -> p c t", t=128), func=AF.Copy),
                         reads=[r_ps[7]], writes=[r_ycT])
                return fin

            pending = []
            front(0)
            for blk in range(NBLK):
                t, bi = divmod(blk, 4)
                if bi == 0:
                    ycT_of[t] = ycTs.next()
                mid(blk)
                for f in pending:
                    f()
                pending = []
                backA(blk)
                if blk + 1 < NBLK:
                    front(blk + 1)
                pending.append(backB(blk))
                if bi == 3:
                    for f in pending:
                        f()
                    pending = []
                    ycT, r_ycT = ycT_of.pop(t)
                    qk_of.pop(t)
                    hook = None
                    if t + 2 < NT:
                        qk_of[t + 2] = qkTs.next()
                        hook = (lambda oc, tt=t + 2: conv_chunk(tt, oc, *qk_of[tt]))
                    gate_branch(2, t, lambda k: ycT[:, k, :], r_ycT, Wg, Wb, [r_Wg, r_Wb], bufs, hook=hook)
        if stop == "P3":
            break

        with phase_begin() as es:
            Wo_ = talloc(es, "Wout", [128, 8, D], BF16)[0][0]
            r_Wo_ = WRes()
            load_w(Wo_, w_out[l], r_Wo_)
            ph = {"ps_ss": 2, "sq": Rot(talloc(es, "sq", [128, 512], BF16, 8)), "lnb": talloc(es, "lnb", [128, 512], F32)[0]}
            mts = [Rot(talloc(es, "mt%d" % r, [128, 8, TB], BF16, 2)) for r in range(3)]
            mgs = Rot(talloc(es, "merged", [128, 8, TB], BF16, 2))
            tails = []
            xts = Rot(talloc(es, "xt", [128, 8, TB], F32, 2))
            bo = Rot([0, 1])
            for t in range(NT):
                tok = slice(t * TB, (t + 1) * TB)
                mt = [mts[r].next() for r in range(3)]
                for r in range(3):
                    S.dma("sp" if r != 1 else "act", mt[r][0][:], tmaj(Mr[r], t), reads=[r_M[r][t]], writes=[mt[r][1]])
                xt, r_xt = xts.next()
                merged, r_mg = mgs.next()
                S.dma("sp", xt[:], tmaj(X, t), reads=[r_X[t]], writes=[r_xt])
                S.op("dve", lambda E: E.tensor_tensor(out=merged[:], in0=mt[0][0][:], in1=mt[1][0][:], op=ALU.add),
                     reads=[mt[0][1], mt[1][1]], writes=[r_mg])
                S.op("dve", lambda E: E.tensor_tensor(out=merged[:], in0=merged[:], in1=mt[2][0][:], op=ALU.add),
                     reads=[r_mg, mt[2][1]], writes=[r_mg])
                for oc in range(8):
                    b = bo.next()
                    for k in range(8):
                        mm(ps[b][:], Wo_[:, k, oc * 128:(oc + 1) * 128], merged[:, k, :], k == 0, k == 7, [r_Wo_.at(oc * 128), r_mg], [r_ps[b]])
                    S.op("dve", lambda E: E.scalar_tensor_tensor(out=xt[:, oc, :], in0=ps[b][:], scalar=0.5, in1=xt[:, oc, :], op0=ALU.mult, op1=ALU.add),
                         reads=[r_ps[b], r_xt], writes=[r_xt])
                    if oc == 2:
                        for f in tails:
                            f()
                        tails = []
                S.dma("act", tmaj(X, t), xt[:], reads=[r_xt], writes=[r_X[t]])
                tails.append(norm_tile(ph, xt, r_xt, PC_GMQ, lambda c, tok=tok, t=t: (H[:, c, tok], r_H[t]), None, defer=True))
            for f in tails:
                f()
        if stop == "P4":
            break

        with phase_begin() as es:
            kmemT, r_km = talloc(es, "kmemT", [128, 8, NMEM], BF16)[0]
            vmem, r_vm = talloc(es, "vmem", [128, 2, D], BF16)[0]
            ph = {"ps_ss": 2, "sq": Rot(talloc(es, "sq", [128, 512], BF16, 8)), "lnb": talloc(es, "lnb", [128, 512], F32)[0]}
            with ExitStack() as es2:
                Wkv = talloc(es2, "Wkv", [128, 8, 2 * D], BF16)[0][0]
                r_Wkv = WRes()
                load_w(Wkv, w_mkv[l], r_Wkv)
                memx, r_memx = talloc(es2, "memx", [128, 8, NMEM], F32)[0]
                memn, r_memn = talloc(es2, "memn", [128, 8, NMEM], BF16)[0]
                S.dma("sp", memx[:], fm(memT_in), writes=[r_memx])
                norm_tile(ph, memx, r_memx, PC_GMKV, lambda c: (memn[:, c, :], r_memn), None, w=NMEM)
                bb = Rot([0, 1])
                for c in range(8):
                    b = bb.next()
                    for k in range(8):
                        mm(ps[b][:, 0:NMEM], Wkv[:, k, c * 128:(c + 1) * 128], memn[:, k, :], k == 0, k == 7, [r_Wkv.at(c * 128), r_memn], [r_ps[b]])
                    S.op("act", lambda E: E.activation(out=kmemT[:, c, :], in_=ps[b][:, 0:NMEM], func=AF.Copy), reads=[r_ps[b]], writes=[r_km])
                for mtk in range(2):
                    for hf in range(2):
                        b = bb.next()
                        for k in range(8):
                            mm(ps[b][:], memn[:, k, mtk * 128:(mtk + 1) * 128], Wkv[:, k, D + hf * 512:D + (hf + 1) * 512], k == 0, k == 7,
                               [r_Wkv.at(D + hf * 512), r_memn], [r_ps[b]])
                        S.op("dve", lambda E: E.tensor_copy(out=vmem[:, mtk, hf * 512:(hf + 1) * 512], in_=ps[b][:]), reads=[r_ps[b]], writes=[r_vm])
                S.barrier()
            Wmq = talloc(es, "Wmq", [128, 8, D], BF16)[0][0]
            r_Wmq = WRes()
            Wmo = talloc(es, "Wmo", [128, 8, D], BF16)[0][0]
            r_Wmo = WRes()
            load_w(Wmq, w_mq[l], r_Wmq)
            load_w(Wmo, w_mo[l], r_Wmo)
            qm, r_qm = talloc(es, "qm", [128, 8, TB], BF16)[0]
            om, r_om = talloc(es, "om", [128, 8, TB], BF16)[0]
            PTm = Rot(talloc(es, "PTm", [128, TB], BF16, 4))
            rls = Rot(talloc(es, "rl", [128, TB], F32, 2))
            xts = Rot(talloc(es, "xt", [128, 8, TB], F32, 2))
            bq = Rot([0, 1])
            bs_ = Rot([3, 4])
            tails = []
            for t in range(NT):
                tok = slice(t * TB, (t + 1) * TB)
                xt, r_xt = xts.next()
                S.dma("sp", xt[:], tmaj(X, t), reads=[r_X[t]], writes=[r_xt])
                for c in range(8):
                    if c == 3:
                        for f in tails:
                            f()
                        tails = []
                    b = bq.next()
                    for k in range(8):
                        mm(ps[b][:], Wmq[:, k, c * 128:(c + 1) * 128], H[:, k, tok], k == 0, k == 7, [r_Wmq.at(c * 128), r_H[t]], [r_ps[b]])
                    if c % 2 == 0:
                        S.op("act", lambda E: E.activation(out=qm[:, c, :], in_=ps[b][:], func=AF.Copy), reads=[r_ps[b]], writes=[r_qm])
                    else:
                        S.op("dve", lambda E: E.tensor_copy(out=qm[:, c, :], in_=ps[b][:]), reads=[r_ps[b]], writes=[r_qm])
                for hd in range(4):
                    pts = []
                    for mtk in range(2):
                        b = bs_.next()
                        for kc in range(2):
                            mm(ps[b][:], kmemT[:, 2 * hd + kc, mtk * 128:(mtk + 1) * 128], qm[:, 2 * hd + kc, :], kc == 0, kc == 1,
                               [r_km, r_qm], [r_ps[b]])
                        PT, r_PT = PTm.next()
                        S.op("act", lambda E: E.activation(out=PT[:], in_=ps[b][:], func=AF.Exp, scale=1.0 / 16), reads=[r_ps[b]], writes=[r_PT])
                        pts.append((PT, r_PT))
                    for mtk in range(2):
                        mm(ps[5][:], ones_bf[:], pts[mtk][0][:], mtk == 0, mtk == 1, [r_const, pts[mtk][1]], [r_ps[5]])
                    rl, r_rl = rls.next()
                    S.op("act", lambda E: E.activation(out=rl[:], in_=ps[5][:], func=AF.Ln), reads=[r_ps[5]], writes=[r_rl])
                    S.op("act", lambda E: E.activation(out=rl[:], in_=rl[:], func=AF.Exp, scale=-1.0), reads=[r_rl], writes=[r_rl])
                    for ec in range(2):
                        for mtk in range(2):
                            mm(ps[6][:], vmem[:, mtk, (2 * hd + ec) * 128:(2 * hd + ec + 1) * 128], pts[mtk][0][:], mtk == 0, mtk == 1,
                               [r_vm, pts[mtk][1]], [r_ps[6]])
                        S.op("dve", lambda E: E.tensor_tensor(out=om[:, 2 * hd + ec, :], in0=ps[6][:], in1=rl[:], op=ALU.mult),
                             reads=[r_ps[6], r_rl], writes=[r_om])
                for oc in range(8):
                    b = bq.next()
                    for k in range(8):
                        mm(ps[b][:], Wmo[:, k, oc * 128:(oc + 1) * 128], om[:, k, :], k == 0, k == 7, [r_Wmo.at(oc * 128), r_om], [r_ps[b]])
                    S.op("dve", lambda E: E.tensor_tensor(out=xt[:, oc, :], in0=ps[b][:], in1=xt[:, oc, :], op=ALU.add),
                         reads=[r_ps[b], r_xt], writes=[r_xt])
                S.dma("act", tmaj(X, t), xt[:], reads=[r_xt], writes=[r_X[t]])
                tails.append(norm_tile(ph, xt, r_xt, PC_GFFN, lambda c, tok=tok, t=t: (H[:, c, tok], r_H[t]), None, defer=True))
            for f in tails:
                f()
        if stop == "P5":
            break

        NH = DFF // 2
        for g in range(2):
            with phase_begin() as es:
                Wa = talloc(es, "Wa", [128, 8, NH], BF16)[0][0]
                r_Wa = WRes()
                Wb_ = talloc(es, "Wbb", [128, 8, NH], BF16)[0][0]
                r_Wbb = WRes()
                Wd = talloc(es, "Wd", [128, 11, D], BF16)[0][0]
                r_Wd = WRes()
                for c0_ in range(0, NH, 384):
                    load_w(Wa, w_up[l][:, g * NH:(g + 1) * NH], r_Wa, cols=(c0_, min(NH, c0_ + 384)))
                    load_w(Wb_, w_up[l][:, DFF + g * NH:DFF + (g + 1) * NH], r_Wbb, cols=(c0_, min(NH, c0_ + 384)))
                load_w(Wd, w_down[l][g * NH:(g + 1) * NH, :], r_Wd)
                ph = {"ps_ss": 6, "sq": Rot(talloc(es, "sq", [128, 512], BF16, 2)), "lnb": talloc(es, "lnb", [128, 512], F32)[0]}
                halo, r_halo = talloc(es, "haloF", [128, 22, 2], F32)[0]
                S.op("pool", lambda E: E.memset(halo[:], 0.0), writes=[r_halo])
                xbufs = Rot(talloc(es, "xbuf", [128, 514], F32, 4))
                accs = Rot(talloc(es, "accF", [128, TB], F32, 4))
                tgs = Rot(talloc(es, "tgF", [128, TB], F32, 2))
                hid, r_hid = talloc(es, "hid", [128, 11, TB], BF16)[0]
                xt, r_xt = talloc(es, "xtF", [128, 8, TB], F32)[0]
                last = (g == 1 and l == depth - 1)
                ybufs = Rot(talloc(es, "ybuf", [128, TB], F32, 2)) if last else None
                bA = Rot([0, 1])
                bB = Rot([2, 3])
                bD = Rot([4, 5])
                for t in range(NT):
                    tok = slice(t * TB, (t + 1) * TB)
                    S.dma("sp", xt[:], tmaj(X, t), reads=[r_X[t]], writes=[r_xt])
                    for cc in range(11):
                        accp = []
                        for half, (Wx, r_Wx, brot) in enumerate(((Wa, r_Wa, bA), (Wb_, r_Wbb, bB))):
                            b = brot.next()
                            for k in range(8):
                                mm(ps[b][:], Wx[:, k, cc * 128:(cc + 1) * 128], H[:, k, tok], k == 0, k == 7, [r_Wx.at(cc * 128), r_H[t]], [r_ps[b]])
                            hidx = half * 11 + cc
                            pcol = PC_FCONV + (half * 22 + g * 11 + cc) * 3
                            xb, r_xb = xbufs.next()
                            S.op("pool", lambda E: E.tensor_copy(out=xb[:, 0:2], in_=halo[:, hidx, :]), reads=[r_halo], writes=[r_xb])
                            S.op("act", lambda E: E.activation(out=xb[:, 2:514], in_=ps[b][:], func=AF.Copy), reads=[r_ps[b]], writes=[r_xb])
                            S.op("pool", lambda E: E.tensor_copy(out=halo[:, hidx, :], in_=xb[:, 512:514]), reads=[r_xb], writes=[r_halo])
                            acc, r_acc = accs.next()
                            S.op("act", lambda E: E.activation(out=acc[:], in_=ps[b][:], func=AF.Copy, scale=pcols[:, pcol + 2:pcol + 3]),
                                 reads=[r_ps[b], r_pc], writes=[r_acc])
                            for jj in range(2):
                                S.op("dve", lambda E: E.scalar_tensor_tensor(out=acc[:], in0=xb[:, jj:jj + 512], scalar=pcols[:, pcol + jj:pcol + jj + 1],
                                                                             in1=acc[:], op0=ALU.mult, op1=ALU.add), reads=[r_xb, r_pc, r_acc], writes=[r_acc])
                            accp.append((acc, r_acc))
                        (aA, r_aA), (aB, r_aB) = accp
                        tg, r_tg = tgs.next()
                        S.op("act", lambda E: E.activation(out=tg[:], in_=aA[:], func=AF.Silu), reads=[r_aA], writes=[r_tg])
                        S.op("dve", lambda E: E.tensor_tensor(out=hid[:, cc, :], in0=tg[:], in1=aB[:], op=ALU.mult), reads=[r_tg, r_aB], writes=[r_hid])
                    for oc in range(8):
                        b = bD.next()
                        for cc in range(11):
                            mm(ps[b][:], Wd[:, cc, oc * 128:(oc + 1) * 128], hid[:, cc, :], cc == 0, cc == 10, [r_Wd.at(oc * 128), r_hid], [r_ps[b]])
                        S.op("dve", lambda E: E.tensor_tensor(out=xt[:, oc, :], in0=ps[b][:], in1=xt[:, oc, :], op=ALU.add),
                             reads=[r_ps[b], r_xt], writes=[r_xt])
                    if not last:
                        S.dma("act", tmaj(X, t), xt[:], reads=[r_xt], writes=[r_X[t]])
                    if g == 1:
                        if last:
                            def outv(c):
                                return ybufs.next()

                            def after(c, ov, r_ov, tok=tok, t=t):
                                S.dma("sp", yT[t][:, c * TB:(c + 1) * TB], ov, reads=[r_ov], writes=[r_y[t]])
                            norm_tile(ph, xt, r_xt, PC_GNEXT, lambda c: (lambda tr: (tr[0][:], tr[1]))(ybufs.next()), None, after_chunk=after)
                        else:
                            norm_tile(ph, xt, r_xt, PC_GNEXT, lambda c: (H[:, c, tok], r_H[t]), None)
            if stop == "P6a" and g == 0:
                break
        if stop in ("P6", "P6a"):
            break

    if debug:
        S.barrier()
        S.dma("sp", fm(Hdbg), H[:], reads=r_H, writes=[Res()])
    S.finish_all()
    return nc


def _pack_params(inp):
    L = DEPTH
    pc = np.zeros((L, 128, NPC), np.float32)
    pr = np.zeros((L, NPR), np.float32)
    for l in range(L):
        pc[l, :, PC_GMIX:PC_GMIX + 8] = inp["g_mix"][l].reshape(8, 128).T
        pc[l, :, PC_GMQ:PC_GMQ + 8] = inp["g_mem_q"][l].reshape(8, 128).T
        pc[l, :, PC_GMKV:PC_GMKV + 8] = inp["g_mem_kv"][l].reshape(8, 128).T
        pc[l, :, PC_GFFN:PC_GFFN + 8] = inp["g_ffn"][l].reshape(8, 128).T
        pc[l, :, PC_CONVC:PC_CONVC + 32] = inp["w_conv_c"][l].reshape(4, 8, 128).transpose(2, 1, 0).reshape(128, 32)
        pc[l, :, PC_FCONV:PC_FCONV + 132] = inp["w_ffn_conv"][l].reshape(3, 44, 128).transpose(2, 1, 0).reshape(128, 132)
        gn = inp["g_mix"][l + 1] if l + 1 < L else inp["g_final"]
        pc[l, :, PC_GNEXT:PC_GNEXT + 8] = gn.reshape(8, 128).T
        pr[l, PR_GSGU:PR_GSGU + 512] = inp["g_sgu"][l]
        pr[l, PR_BS:PR_BS + 512] = inp["b_s"][l].reshape(512)
        pr[l, PR_BFOX:PR_BFOX + 256] = np.tile(inp["b_fox_f"][l], 32)
        pr[l, PR_BI:PR_BI + 128] = np.tile(inp["b_mlstm_i"][l], 32)
        pr[l, PR_BF:PR_BF + 128] = np.tile(inp["b_mlstm_f"][l], 32)
        pr[l, PR_GMH:PR_GMH + 512] = inp["g_mh"][l]
    return pc, pr


def make_in_maps(inp, n_cores=8):
    inp = {k: np.asarray(v) for k, v in inp.items()}
    pc, pr = _pack_params(inp)
    shared = {
        "w_in": np.ascontiguousarray(inp["w_in"]),
        "wsT": np.ascontiguousarray(inp["w_s"].transpose(0, 1, 3, 2)),
        "w_branch": np.ascontiguousarray(inp["w_branch"]),
        "w_out": np.ascontiguousarray(inp["w_out"]),
        "w_mq": np.ascontiguousarray(inp["w_mq"]),
        "w_mkv": np.ascontiguousarray(inp["w_mkv"]),
        "w_mo": np.ascontiguousarray(inp["w_mo"]),
        "w_up": np.ascontiguousarray(inp["w_up"]),
        "w_down": np.ascontiguousarray(inp["w_down"]),
        "pcols": pc,
        "prows": pr,
    }
    maps = []
    for b in range(n_cores):
        m = dict(shared)
        m["xT"] = np.ascontiguousarray(inp["x"][b].reshape(NT, TB, 8, 128).transpose(0, 3, 2, 1).reshape(NT, 128, 8 * TB))
        m["memT"] = np.ascontiguousarray(inp["mem"][b].T)
        maps.append(m)
    return maps


def _untile(a):
    return np.ascontiguousarray(np.asarray(a).reshape(NT, 128, 8, TB).transpose(0, 3, 2, 1).reshape(S_LEN, D))


def kernel(**inputs):
    nc = build_program()
    in_maps = make_in_maps(inputs)
    res = run_bass_kernel_spmd(nc, in_maps, core_ids=list(range(8)))
    out = np.stack([_untile(r["yT"]) for r in res.results], axis=0)
    return out.astype(np.float32)
```

```python
import numpy as np
from contextlib import ExitStack
import concourse.bass as bass
import concourse.mybir as mybir
from concourse.bass_utils import run_bass_kernel_spmd

F32 = mybir.dt.float32
BF16 = mybir.dt.bfloat16
ALU = mybir.AluOpType
AF = mybir.ActivationFunctionType

D = 1024
S_LEN = 4096
DEPTH = 4
NMEM = 256
TB = 512
NT = S_LEN // TB
NBLK = S_LEN // 128
DFF = 2816
EPS = 1e-6
IN_COLS = 7696
A_U, A_V = 0, 512
B_Q, B_K, B_V, B_F = 1024, 1536, 2048, 2560
C_Q, C_K, C_V, C_I, C_F, C_O = 2568, 3080, 3592, 4104, 4108, 4112
G_0 = 4624
PC_GMIX, PC_GMQ, PC_GMKV, PC_GFFN, PC_CONVC, PC_FCONV, PC_GNEXT = 0, 8, 16, 24, 32, 64, 196
NPC = 204
PR_GSGU, PR_BS, PR_BFOX, PR_BI, PR_BF, PR_GMH = 0, 512, 1024, 1280, 1408, 1536
NPR = 2048


class Res:
    __slots__ = ("w", "r")

    def __init__(self):
        self.w = None
        self.r = []


class Sched:
    def __init__(self, nc):
        self.nc = nc
        self.eng = {"pe": nc.tensor, "dve": nc.vector, "act": nc.scalar, "pool": nc.gpsimd, "sp": nc.sync}
        self.sem = {k: nc.alloc_semaphore(name="s_" + k) for k in self.eng}
        self.cnt = {k: 0 for k in self.eng}
        self.seen = {k: {} for k in self.eng}
        self.dsem = {}
        self.dcnt = {}
        self.ninst = 0

    def _deps(self, e, reads, writes):
        deps = {}

        def add(tok, raw):
            k, v = tok
            if k == e and e == "pe":
                return
            if deps.get(k, 0) < v:
                deps[k] = v

        for r in reads:
            if r.w is not None:
                add(r.w, True)
        for w in writes:
            if w.w is not None:
                add(w.w, False)
            for t in w.r:
                add(t, False)
        return deps

    def _semobj(self, k):
        return self.sem[k] if k in self.sem else self.dsem[k]

    def _emit_waits(self, e, deps):
        seen = self.seen[e]
        need = [(k, v) for k, v in deps.items() if seen.get(k, 0) < v]
        E = self.eng[e]
        for (k, v) in need[1:]:
            E.wait_ge(self._semobj(k), v)
            seen[k] = v
            self.ninst += 1
        return need[0] if need else None

    def _mark(self, tok, reads, writes):
        for r in reads:
            r.r.append(tok)
        for w in writes:
            w.w = tok
            w.r = []

    def op(self, e, fn, reads=(), writes=()):
        first = self._emit_waits(e, self._deps(e, reads, writes))
        ins = fn(self.eng[e])
        if first:
            ins._wait_ge(self._semobj(first[0]), first[1])
            self.seen[e][first[0]] = first[1]
        self.cnt[e] += 1
        self.ninst += 1
        ins.then_inc(self.sem[e], 1)
        tok = (e, self.cnt[e])
        self._mark(tok, reads, writes)
        return tok

    def dma(self, q, out, in_, reads=(), writes=(), **kw):
        first = self._emit_waits(q, self._deps(q, reads, writes))
        i = self.dcnt.get(q, 0)
        self.dcnt[q] = i + 1
        key = "d_%s_%d" % (q, i % 16)
        if key not in self.dsem:
            self.dsem[key] = self.nc.alloc_semaphore(name=key)
            self.dcnt[key] = 0
        elif self.seen[q].get(key, 0) < self.dcnt[key]:
            self.eng[q].wait_ge(self.dsem[key], self.dcnt[key])
            self.seen[q][key] = self.dcnt[key]
            self.ninst += 1
        ins = self.eng[q].dma_start(out=out, in_=in_, **kw)
        if first:
            ins._wait_ge(self._semobj(first[0]), first[1])
            self.seen[q][first[0]] = first[1]
        self.dcnt[key] += 16
        self.ninst += 1
        ins.then_inc(self.dsem[key], 16)
        tok = (key, self.dcnt[key])
        self._mark(tok, reads, writes)
        return tok

    def barrier(self):
        snap = [(k, self.cnt[k]) for k in self.sem if k != "sp" and self.cnt[k] > 0]
        snap += [(k, self.dcnt[k]) for k in self.dsem]
        for e in self.eng:
            seen = self.seen[e]
            for (k, v) in snap:
                if k != e and seen.get(k, 0) < v:
                    self.eng[e].wait_ge(self._semobj(k), v)
                    seen[k] = v
                    self.ninst += 1

    def finish_all(self):
        self.barrier()


class WRes:
    def __init__(self):
        self.parts = []

    def at(self, c):
        for c0, c1, r in self.parts:
            if c0 <= c < c1:
                return r
        raise KeyError(c)

    def all(self):
        return [r for _, _, r in self.parts]


def _flat(lst):
    out = []
    for x in lst:
        if isinstance(x, (list, tuple)):
            out.extend(_flat(x))
        else:
            out.append(x)
    return out


class Rot:
    def __init__(self, items):
        self.items = items
        self.i = 0

    def next(self):
        it = self.items[self.i % len(self.items)]
        self.i += 1
        return it


def build_program(depth=DEPTH, stop=None, debug=False):
    nc = bass.Bass("TRN2", target_bir_lowering=False)
    S = Sched(nc)
    ctx = ExitStack()

    def dram_in(name, shape, dt=F32):
        return nc.dram_tensor(name, shape, dt, kind="ExternalInput").ap()

    skind = "ExternalOutput" if debug else "Internal"

    xT_in = dram_in("xT", [NT, 128, 8 * TB])
    memT_in = dram_in("memT", [D, NMEM])
    w_in = dram_in("w_in", [DEPTH, D, IN_COLS])
    wsT_in = dram_in("wsT", [DEPTH, 4, 128, 128])
    w_branch = dram_in("w_branch", [DEPTH, 3, 512, D])
    w_out = dram_in("w_out", [DEPTH, D, D])
    w_mq = dram_in("w_mq", [DEPTH, D, D])
    w_mkv = dram_in("w_mkv", [DEPTH, D, 2 * D])
    w_mo = dram_in("w_mo", [DEPTH, D, D])
    w_up = dram_in("w_up", [DEPTH, D, 2 * DFF])
    w_down = dram_in("w_down", [DEPTH, DFF, D])
    pcols_in = dram_in("pcols", [DEPTH, 128, NPC])
    prows_in = dram_in("prows", [DEPTH, NPR])
    yT = nc.dram_tensor("yT", [NT, 128, 8 * TB], F32, kind="ExternalOutput").ap()
    X = nc.dram_tensor("Xs", [NT, 128, 8 * TB], F32, kind=skind).ap()
    Mr = [nc.dram_tensor("M%d" % r, [NT, 128, 8 * TB], BF16, kind=skind).ap() for r in range(3)]
    YB = nc.dram_tensor("YBs", [512, S_LEN], BF16, kind=skind).ap()
    Hdbg = nc.dram_tensor("Hdbg", [D, S_LEN], BF16, kind="ExternalOutput").ap() if debug else None

    def fm(ap2d):
        return ap2d.rearrange("(c p) t -> p c t", p=128)

    def tmaj(ap3d, t):
        return ap3d[t].rearrange("p (c t) -> p c t", t=TB)

    r_X = [Res() for _ in range(NT)]
    r_M = [[Res() for _ in range(NT)] for _ in range(3)]
    r_YB = [Res() for _ in range(NT)]
    r_y = [Res() for _ in range(NT)]

    def sb(name, shape, dt):
        return nc.alloc_sbuf_tensor(name, shape, dt)

    H = sb("H", [128, 8, S_LEN], BF16)
    r_H = [Res() for _ in range(NT)]
    ident = sb("ident", [128, 128], BF16)
    ones_bf = sb("ones_bf", [128, 128], BF16)
    tri_bf = sb("tri_bf", [128, 128], BF16)
    tri4 = sb("tri4", [128, 512], BF16)
    tri4s = sb("tri4s", [128, 512], F32)
    U32 = sb("U32", [128, 128], F32)
    ones32 = sb("ones32", [128, 128], F32)
    Umid = sb("Umid", [128, 128], F32)
    pcols = sb("pcols_sb", [128, NPC], F32)
    prows = sb("prows_sb", [128, NPR], F32)
    wtap = sb("wtap", [128, 32], F32)
    r_const, r_pc, r_pr, r_wtap = Res(), Res(), Res(), Res()

    ps = [nc.alloc_psum_tensor("ps%d" % i, [128, 512], F32) for i in range(7)]
    pst = nc.alloc_psum_tensor("pst", [128, 1024], BF16)
    r_ps = [Res() for _ in range(9)]

    S.op("pool", lambda E: E.memset(ones_bf[:], 1.0), writes=[r_const])
    S.op("pool", lambda E: E.memset(ones32[:], 1.0), writes=[r_const])
    S.op("pool", lambda E: E.memset(ident[:], 1.0), writes=[r_const])
    S.op("pool", lambda E: E.affine_select(out=ident[:], in_=ident[:], pattern=[[1, 128]], compare_op=ALU.is_equal,
                                           fill=0.0, base=0, channel_multiplier=-1), reads=[r_const], writes=[r_const])
    S.op("pool", lambda E: E.memset(tri_bf[:], 1.0), writes=[r_const])
    S.op("pool", lambda E: E.affine_select(out=tri_bf[:], in_=tri_bf[:], pattern=[[1, 128]], compare_op=ALU.is_ge,
                                           fill=0.0, base=0, channel_multiplier=-1), reads=[r_const], writes=[r_const])
    for h_ in range(4):
        S.op("pool", lambda E: E.tensor_copy(out=tri4[:, h_ * 128:(h_ + 1) * 128], in_=tri_bf[:]), reads=[r_const], writes=[r_const])
    S.op("pool", lambda E: E.tensor_scalar(out=tri4s[:], in0=tri4[:], scalar1=float(128 ** -0.5), scalar2=None, op0=ALU.mult), reads=[r_const], writes=[r_const])
    S.op("pool", lambda E: E.memset(U32[:], 1.0), writes=[r_const])
    S.op("pool", lambda E: E.affine_select(out=U32[:], in_=U32[:], pattern=[[1, 128]], compare_op=ALU.is_ge,
                                           fill=0.0, base=0, channel_multiplier=-1), reads=[r_const], writes=[r_const])
    S.op("pool", lambda E: E.memset(Umid[:], 1.0), writes=[r_const])
    S.op("pool", lambda E: E.affine_select(out=Umid[:], in_=Umid[:], pattern=[[0, 128]], compare_op=ALU.is_ge,
                                           fill=0.0, base=64, channel_multiplier=-1), reads=[r_const], writes=[r_const])

    def mm(out, lhsT, rhs, start, stop, reads, writes):
        S.op("pe", lambda E: E.matmul(out, lhsT=lhsT, rhs=rhs, start=start, stop=stop), reads=_flat(reads), writes=writes)

    def load_w(dst, src2d, wres, ncol_split=512, cols=None):
        n = src2d.shape[1]
        v = src2d.rearrange("(c p) n -> p c n", p=128)
        rng = [(c0, min(n, c0 + ncol_split)) for c0 in range(0, n, ncol_split)] if cols is None else [cols]
        for (c0, c1) in rng:
            r = Res()
            S.dma("pool", dst[:, :, c0:c1], v[:, :, c0:c1], writes=[r])
            wres.parts.append((c0, c1, r))

    def norm_tile(ph, xt, r_xt, gcol0, out_views, r_out, w=TB, after_chunk=None, defer=False):
        ssb = ph["ps_ss"]
        sqs = []
        for c in range(8):
            sq, r_sq = ph["sq"].next()
            S.op("act", lambda E: E.activation(out=sq[:, 0:w], in_=xt[:, c, :], func=AF.Square), reads=[r_xt], writes=[r_sq])
            sqs.append((sq, r_sq))
            if not defer:
                mm(ps[ssb][:, 0:w], ones_bf[:], sq[:, 0:w], c == 0, c == 7, [r_sq, r_const], [r_ps[ssb]])

        def tail():
            if defer:
                for c, (sq, r_sq) in enumerate(sqs):
                    mm(ps[ssb][:, 0:w], ones_bf[:], sq[:, 0:w], c == 0, c == 7, [r_sq, r_const], [r_ps[ssb]])
            lnb, r_ln = ph["lnb"]
            S.op("act", lambda E: E.activation(out=lnb[:, 0:w], in_=ps[ssb][:, 0:w], func=AF.Ln, scale=1.0 / D, bias=EPS),
                 reads=[r_ps[ssb]], writes=[r_ln])
            S.op("act", lambda E: E.activation(out=lnb[:, 0:w], in_=lnb[:, 0:w], func=AF.Exp, scale=-0.5), reads=[r_ln], writes=[r_ln])
            for c in range(8):
                ov, r_ov = out_views(c)
                S.op("dve", lambda E: E.scalar_tensor_tensor(out=ov, in0=xt[:, c, :], scalar=pcols[:, gcol0 + c:gcol0 + c + 1],
                                                             in1=lnb[:, 0:w], op0=ALU.mult, op1=ALU.mult),
                     reads=[r_xt, r_ln, r_pc], writes=[r_ov])
                if after_chunk is not None:
                    after_chunk(c, ov, r_ov)

        if defer:
            return tail
        tail()
        return None

    def phase_begin():
        S.barrier()
        return ExitStack()

    uid = [0]

    def talloc(es, name, shape, dt, n=1):
        items = []
        for i in range(n):
            uid[0] += 1
            t = es.enter_context(nc.sbuf_tensor("%s_%d_%d" % (name, i, uid[0]), shape, dt))
            items.append((t, Res()))
        return items

    def gate_branch(r, t, yT_views, r_y, Wg, Wb, r_w, es_bufs, hook=None):
        for oc in range(8):
            bb = es_bufs["bank_b"].next()
            bg = es_bufs["bank_g"].next()
            for k in range(8):
                mm(ps[bg][:], Wg[:, k, oc * 128:(oc + 1) * 128], H[:, k, t * TB:(t + 1) * TB], k == 0, k == 7,
                   [r_w[0].at(oc * 128), r_H[t]], [r_ps[bg]])
            for k in range(4):
                mm(ps[bb][:], Wb[:, k, oc * 128:(oc + 1) * 128], yT_views(k), k == 0, k == 3, [r_w[1].at(oc * 128), r_y], [r_ps[bb]])
            tg, r_tg = es_bufs["tg"].next()
            S.op("act", lambda E: E.activation(out=tg[:], in_=ps[bg][:], func=AF.Tanh, scale=0.5), reads=[r_ps[bg]], writes=[r_tg])
            msb, r_msb = es_bufs["msb"].next()
            S.op("dve", lambda E: E.scalar_tensor_tensor(out=msb[:], in0=tg[:], scalar=1.0, in1=ps[bb][:],
                                                         op0=ALU.add, op1=ALU.mult), reads=[r_tg, r_ps[bb]], writes=[r_msb])
            S.dma("sp", Mr[r][t][:, oc * TB:(oc + 1) * TB], msb[:], reads=[r_msb], writes=[r_M[r][t]])
            if hook is not None:
                hook(oc)

    for l in range(depth):
        S.barrier()
        S.dma("sp", pcols[:], pcols_in[l], writes=[r_pc])
        S.dma("sp", prows[:], prows_in[l].partition_broadcast(128), writes=[r_pr])
        S.op("dve", lambda E: E.tensor_scalar(out=wtap[:, 0:16], in0=pcols[:, PC_CONVC:PC_CONVC + 16], scalar1=0.5, scalar2=None,
                                              op0=ALU.mult), reads=[r_pc], writes=[r_wtap])
        S.op("dve", lambda E: E.tensor_scalar(out=wtap[:, 16:32], in0=pcols[:, PC_CONVC + 16:PC_CONVC + 32],
                                              scalar1=0.5 * (128 ** -0.5), scalar2=None, op0=ALU.mult), reads=[r_pc], writes=[r_wtap])

        if l == 0:
            with phase_begin() as es:
                ph = {"ps_ss": 0, "sq": Rot(talloc(es, "sq", [128, 512], BF16, 2)), "lnb": talloc(es, "lnb", [128, 512], F32)[0]}
                xts = Rot(talloc(es, "xt", [128, 8, TB], F32, 2))
                for t in range(NT):
                    xt, r_xt = xts.next()
                    S.dma("sp", xt[:], tmaj(xT_in, t), writes=[r_xt])
                    S.dma("act", tmaj(X, t), xt[:], reads=[r_xt], writes=[r_X[t]])
                    norm_tile(ph, xt, r_xt, PC_GMIX, lambda c: (H[:, c, t * TB:(t + 1) * TB], r_H[t]), None)
        if stop == "P0":
            break

        with phase_begin() as es:
            Wu = talloc(es, "Wu", [128, 8, 512], BF16)[0][0]
            r_Wu = WRes()
            Wv = talloc(es, "Wv", [128, 8, 512], BF16)[0][0]
            r_Wv = WRes()
            Wg = talloc(es, "Wg", [128, 8, D], BF16)[0][0]
            r_Wg = WRes()
            Wb = talloc(es, "Wb", [128, 4, D], BF16)[0][0]
            r_Wb = WRes()
            wsT32, r_ws32 = talloc(es, "wsT32", [128, 4, 128], F32)[0]
            wsT, r_ws = talloc(es, "wsT", [128, 4, 128], BF16)[0]
            load_w(Wu, w_in[l][:, A_U:A_U + 512], r_Wu)
            load_w(Wv, w_in[l][:, A_V:A_V + 512], r_Wv)
            S.dma("sp", wsT32[:], wsT_in[l].rearrange("g s t -> s g t"), writes=[r_ws32])
            S.op("dve", lambda E: E.memset(wsT32[64:128, :, 0:64], 0.0), writes=[r_ws32])
            S.op("dve", lambda E: E.tensor_copy(out=wsT[:], in_=wsT32[:]), reads=[r_ws32], writes=[r_ws])
            load_w(Wg, w_in[l][:, G_0:G_0 + D], r_Wg)
            load_w(Wb, w_branch[l, 0], r_Wb)
            u_sb = talloc(es, "u_sb", [128, 4, TB], F32)[0]
            v_sb = talloc(es, "v_sb", [128, 4, 512], F32)[0]
            v_n = talloc(es, "v_n", [128, 4, 512], BF16)[0]
            ssv = talloc(es, "ssv", [128, 4], F32)[0]
            junk = talloc(es, "junk", [128, 512], BF16)[0]
            yaT = Rot(talloc(es, "yaT", [128, 4, TB], BF16, 2))
            tmpm = Rot(talloc(es, "tmpm", [128, TB], F32, 2))
            bufs = {"msb": Rot(talloc(es, "msb", [128, TB], BF16, 3)), "tg": Rot(talloc(es, "tg", [128, TB], F32, 2)),
                    "bank_b": Rot([6, 0]), "bank_g": Rot([1, 2])}
            bu = Rot([0, 1])
            bv = Rot([2, 3])
            bm = Rot([4, 5])
            mh4 = talloc(es, "mh4", [128, 4], F32)[0]
            S.op("pool", lambda E: E.memset(mh4[0][:], -0.5), writes=[mh4[1]])
            for t in range(NT):
                tok = slice(t * TB, (t + 1) * TB)
                S.op("dve", lambda E: E.memset(ssv[0][:], 0.0), writes=[ssv[1]])
                for blk in range(4):
                    b = bv.next()
                    tk = slice(t * TB + blk * 128, t * TB + (blk + 1) * 128)
                    for k in range(8):
                        mm(ps[b][:], H[:, k, tk], Wv[:, k, :], k == 0, k == 7, [r_Wv.all(), r_H[t]], [r_ps[b]])
                    S.op("act", lambda E: E.activation(out=v_sb[0][:, blk, :], in_=ps[b][:], func=AF.Gelu_apprx_tanh),
                         reads=[r_ps[b]], writes=[v_sb[1]])
                    S.op("act", lambda E: E.activation(out=junk[0][:], in_=v_sb[0][:, blk, :], func=AF.Square,
                                                       accum_out=ssv[0][:, blk:blk + 1]), reads=[v_sb[1], ssv[1]], writes=[junk[1], ssv[1]])
                for g in range(4):
                    b = bu.next()
                    for k in range(8):
                        mm(ps[b][:], Wu[:, k, g * 128:(g + 1) * 128], H[:, k, tok], k == 0, k == 7, [r_Wu.at(g * 128), r_H[t]], [r_ps[b]])
                    S.op("act", lambda E: E.activation(out=u_sb[0][:, g, :], in_=ps[b][:], func=AF.Gelu_apprx_tanh),
                         reads=[r_ps[b]], writes=[u_sb[1]])
                S.op("dve", lambda E: E.tensor_scalar(out=ssv[0][:], in0=ssv[0][:], scalar1=1.0 / 512, scalar2=EPS, op0=ALU.mult,
                                                      op1=ALU.add), reads=[ssv[1]], writes=[ssv[1]])
                S.op("pool", lambda E: E.tensor_tensor(out=ssv[0][:], in0=ssv[0][:], in1=mh4[0][:], op=ALU.pow),
                     reads=[ssv[1], mh4[1]], writes=[ssv[1]])
                for blk in range(4):
                    S.op("dve", lambda E: E.scalar_tensor_tensor(out=v_n[0][:, blk, :], in0=v_sb[0][:, blk, :], scalar=ssv[0][:, blk:blk + 1],
                                                                 in1=prows[:, PR_GSGU:PR_GSGU + 512], op0=ALU.mult, op1=ALU.mult),
                         reads=[v_sb[1], ssv[1], r_pr], writes=[v_n[1]])
                ya, r_ya = yaT.next()
                for g in range(4):
                    b = bm.next()
                    for blk in range(4):
                        mm(ps[b][:, blk * 128:(blk + 1) * 128], v_n[0][:, blk, g * 128:(g + 1) * 128], wsT[:, g, :], True, True,
                           [v_n[1], r_ws], [r_ps[b]])
                    tm, r_tm = tmpm.next()
                    for blk in range(4):
                        S.op("dve", lambda E: E.tensor_tensor(out=tm[:, blk * 128:(blk + 1) * 128], in0=ps[b][:, blk * 128:(blk + 1) * 128],
                                                              in1=prows[:, PR_BS + g * 128:PR_BS + (g + 1) * 128], op=ALU.add),
                             reads=[r_ps[b], r_pr], writes=[r_tm])
                    S.op("dve", lambda E: E.tensor_tensor(out=ya[:, g, :], in0=tm[:], in1=u_sb[0][:, g, :], op=ALU.mult),
                         reads=[r_tm, u_sb[1]], writes=[r_ya])
                gate_branch(0, t, lambda k: ya[:, k, :], r_ya, Wg, Wb, [r_Wg, r_Wb], bufs)
        if stop == "P1":
            break

        with phase_begin() as es:
            Wq = talloc(es, "Wq", [128, 8, 512], BF16)[0][0]
            r_Wq = WRes()
            Wk = talloc(es, "Wk", [128, 8, 512], BF16)[0][0]
            r_Wk = WRes()
            Wv = talloc(es, "Wv", [128, 8, 512], BF16)[0][0]
            r_Wv = WRes()
            Wf = talloc(es, "Wf", [128, 8, 8], BF16)[0][0]
            r_Wf = WRes()
            load_w(Wf, w_in[l][:, B_F:B_F + 8], r_Wf)
            load_w(Wq, w_in[l][:, B_Q:B_Q + 512], r_Wq)
            load_w(Wk, w_in[l][:, B_K:B_K + 512], r_Wk)
            load_w(Wv, w_in[l][:, B_V:B_V + 512], r_Wv)
            for blk in range(NBLK):
                for k in range(8):
                    mm(ps[0][:, blk * 8:(blk + 1) * 8], H[:, k, blk * 128:(blk + 1) * 128], Wf[:, k, :], k == 0, k == 7,
                       [r_Wf.all(), r_H[blk // 4]], [r_ps[0]])
            zf, r_zf = talloc(es, "zf", [128, 256], F32)[0]
            tot, r_tot = talloc(es, "tot", [128, 256], F32)[0]
            pre, r_pre = talloc(es, "pre", [128, 256], F32)[0]
            Csp, r_Csp = talloc(es, "Csp", [128, 256], F32)[0]
            Rsp, r_Rsp = talloc(es, "Rsp", [128, 256], F32)[0]
            S.op("dve", lambda E: E.tensor_tensor(out=zf[:], in0=ps[0][:, 0:256], in1=prows[:, PR_BFOX:PR_BFOX + 256], op=ALU.add),
                 reads=[r_ps[0], r_pr], writes=[r_zf])
            S.op("act", lambda E: E.activation(out=zf[:], in_=zf[:], func=AF.Exp, scale=-1.0), reads=[r_zf], writes=[r_zf])
            S.op("act", lambda E: E.activation(out=zf[:], in_=zf[:], func=AF.Ln, bias=1.0), reads=[r_zf], writes=[r_zf])
            mm(ps[1][:, 0:256], U32[:], zf[:], True, True, [r_const, r_zf], [r_ps[1]])
            mm(ps[2][:, 0:256], ones32[:], zf[:], True, True, [r_const, r_zf], [r_ps[2]])
            mm(ps[3][:, 0:256], Umid[:], zf[:], True, True, [r_const, r_zf], [r_ps[3]])
            S.op("dve", lambda E: E.tensor_copy(out=tot[:], in_=ps[2][:, 0:256]), reads=[r_ps[2]], writes=[r_tot])
            S.op("dve", lambda E: E.memset(pre[:], 0.0), writes=[r_pre])
            pre3 = pre[:].rearrange("p (m h) -> p m h", h=8)
            tot3 = tot[:].rearrange("p (m h) -> p m h", h=8)
            for h in range(8):
                S.op("dve", lambda E: E.tensor_tensor_scan(out=pre3[:, 1:32, h], data0=ones32[:, 0:31], data1=tot3[:, 0:31, h], initial=0.0,
                                                           op0=ALU.mult, op1=ALU.add), reads=[r_tot, r_const], writes=[r_pre])
            S.op("dve", lambda E: E.tensor_tensor(out=Csp[:], in0=ps[1][:, 0:256], in1=pre[:], op=ALU.add), reads=[r_ps[1], r_pre], writes=[r_Csp])
            S.op("dve", lambda E: E.tensor_tensor(out=Rsp[:], in0=ps[3][:, 0:256], in1=pre[:], op=ALU.add), reads=[r_ps[3], r_pre], writes=[r_Rsp])
            Csp3 = Csp[:].rearrange("p (m h) -> p m h", h=8)

            qT, r_qT = talloc(es, "qT", [128, S_LEN], BF16)[0]
            kT, r_kT = talloc(es, "kT", [128, S_LEN], BF16)[0]
            vaug, r_va = talloc(es, "vaug", [128, NBLK, 192], BF16)[0]
            biasT, r_bias = talloc(es, "biasT", [128, 2, NBLK, NBLK], F32)[0]
            ybT, r_ybT = talloc(es, "ybT", [128, S_LEN], BF16)[0]
            PTs = Rot(talloc(es, "PT", [128, TB], BF16, 5))
            recs = Rot(talloc(es, "rec", [128, TB], F32, 2))
            bqk = Rot([0, 1])
            bvv = Rot([2])
            bS = Rot([0, 1, 2, 5])
            bA = Rot([3, 4])
            bB = Rot([6])
            combs = Rot(talloc(es, "comb", [128, TB], F32, 2))
            alpha, r_alpha = talloc(es, "alpha", [128, 256], F32)[0]
            Rsp4 = Rsp[:].rearrange("p (t b h) -> p t b h", b=4, h=8)
            al4 = alpha[:].rearrange("p (t b h) -> p t b h", b=4, h=8)
            for b_ in range(4):
                S.op("dve", lambda E: E.tensor_tensor(out=al4[:, :, b_, :], in0=Rsp4[:, :, b_, :], in1=Rsp4[:, :, 0, :], op=ALU.subtract),
                     reads=[r_Rsp], writes=[r_alpha])
            S.op("act", lambda E: E.activation(out=alpha[:], in_=alpha[:], func=AF.Exp, scale=-1.0), reads=[r_alpha], writes=[r_alpha])
            S.op("pool", lambda E: E.memset(vaug[:, :, 64:128], 1.0), writes=[r_va])
            for j in range(4):
                fc = slice(j * 128, (j + 1) * 128)
                for t in range(NT):
                    tok = slice(t * TB, (t + 1) * TB)
                    b = bqk.next()
                    for k in range(8):
                        mm(ps[b][:], Wq[:, k, fc], H[:, k, tok], k == 0, k == 7, [r_Wq.at(j * 128), r_H[t]], [r_ps[b]])
                    S.op("dve", lambda E: E.tensor_copy(out=qT[:, tok], in_=ps[b][:]), reads=[r_ps[b]], writes=[r_qT])
                    b = bqk.next()
                    for k in range(8):
                        mm(ps[b][:], Wk[:, k, fc], H[:, k, tok], k == 0, k == 7, [r_Wk.at(j * 128), r_H[t]], [r_ps[b]])
                    S.op("dve", lambda E: E.tensor_copy(out=kT[:, tok], in_=ps[b][:]), reads=[r_ps[b]], writes=[r_kT])
                    b = bvv.next()
                    for blk in range(4):
                        tk = slice(t * TB + blk * 128, t * TB + (blk + 1) * 128)
                        for k in range(8):
                            mm(ps[b][:, blk * 128:(blk + 1) * 128], H[:, k, tk], Wv[:, k, fc], k == 0, k == 7, [r_Wv.at(j * 128), r_H[t]], [r_ps[b]])
                    psv = ps[b][:].rearrange("p (b c) -> p b c", c=128)
                    S.op("dve", lambda E: E.tensor_copy(out=vaug[:, 4 * t:4 * t + 4, 0:64], in_=psv[:, :, 0:64]),
                         reads=[r_ps[b]], writes=[r_va])
                    S.op("dve", lambda E: E.tensor_copy(out=vaug[:, 4 * t:4 * t + 4, 128:192], in_=psv[:, :, 64:128]),
                         reads=[r_ps[b]], writes=[r_va])
                for hh in range(2):
                    h = 2 * j + hh
                    for n in range(NBLK):
                        S.op("dve", lambda E: E.tensor_scalar(out=biasT[:, hh, n, 0:n + 1], in0=Csp3[:, 0:n + 1, h],
                                                              scalar1=Rsp[:, n * 8 + h:n * 8 + h + 1], scalar2=None, op0=ALU.subtract),
                             reads=[r_Csp, r_Rsp], writes=[r_bias])
                items = []
                for hh in range(2):
                    for t in range(NT):
                        for m in range(4 * t + 4):
                            items.append((hh, t, m))
                state = {}

                def stage1(it):
                    hh, t, m = it
                    prt = slice(64 * hh, 64 * hh + 64)
                    jin = m - 4 * t
                    c0 = max(jin, 0) * 128
                    bs_ = bS.next()
                    mm(ps[bs_][:, c0:TB], kT[prt, m * 128:(m + 1) * 128], qT[prt, t * TB + c0:(t + 1) * TB], True, True,
                       [r_kT, r_qT], [r_ps[bs_]])
                    PT, r_PT = PTs.next()
                    if jin < 0:
                        S.op("act", lambda E: E.activation(out=PT[:], in_=ps[bs_][:], func=AF.Exp, scale=0.125, bias=biasT[:, hh, 4 * t, m:m + 1]),
                             reads=[r_ps[bs_], r_bias], writes=[r_PT])
                    else:
                        for qb in range(jin, 4):
                            n = 4 * t + qb
                            S.op("act", lambda E: E.activation(out=PT[:, qb * 128:(qb + 1) * 128], in_=ps[bs_][:, qb * 128:(qb + 1) * 128],
                                                               func=AF.Exp, scale=0.125, bias=biasT[:, hh, n, m:m + 1]),
                                 reads=[r_ps[bs_], r_bias], writes=[r_PT])
                        S.op("pool", lambda E: E.tensor_tensor(out=PT[:, c0:c0 + 128], in0=PT[:, c0:c0 + 128], in1=tri_bf[:], op=ALU.mult),
                             reads=[r_PT, r_const], writes=[r_PT])
                    state[it] = (PT, r_PT)

                def stage2(it):
                    hh, t, m = it
                    h = 2 * j + hh
                    prt = slice(64 * hh, 64 * hh + 64)
                    oth = slice(64 - 64 * hh, 128 - 64 * hh)
                    vcols = slice(64 * hh, 64 * hh + 128)
                    jin = m - 4 * t
                    c0 = max(jin, 0) * 128
                    PT, r_PT = state.pop(it)
                    if m == 0 and t > 0:
                        state[("A", hh, t)] = bA.next()
                    if jin == 0:
                        state[("B", hh, t)] = bB.next()
                    if jin < 0:
                        ba = state[("A", hh, t)]
                        mm(ps[ba][:], vaug[:, m, vcols], PT[:], m == 0, m == 4 * t - 1, [r_va, r_PT], [r_ps[ba]])
                        return
                    bb = state[("B", hh, t)]
                    mm(ps[bb][:, c0:TB], vaug[:, m, vcols], PT[:, c0:TB], jin == 0, jin == 3, [r_va, r_PT], [r_ps[bb]])
                    if jin < 3:
                        return
                    comb, r_comb = combs.next()
                    S.op("dve", lambda E: E.tensor_copy(out=comb[:], in_=ps[bb][:]), reads=[r_ps[bb]], writes=[r_comb])
                    if t > 0:
                        ba = state.pop(("A", hh, t))
                        for qb in range(4):
                            n = 4 * t + qb
                            S.op("dve", lambda E: E.scalar_tensor_tensor(out=comb[:, qb * 128:(qb + 1) * 128], in0=ps[ba][:, qb * 128:(qb + 1) * 128],
                                                                         scalar=alpha[:, n * 8 + h:n * 8 + h + 1], in1=comb[:, qb * 128:(qb + 1) * 128],
                                                                         op0=ALU.mult, op1=ALU.add), reads=[r_ps[ba], r_comb, r_alpha], writes=[r_comb])
                    state.pop(("B", hh, t))
                    rec, r_rec = recs.next()
                    S.op("dve", lambda E: E.reciprocal(out=rec[prt, :], in_=comb[oth, :]), reads=[r_comb], writes=[r_rec])
                    S.op("dve", lambda E: E.tensor_tensor(out=ybT[prt, t * TB:(t + 1) * TB], in0=comb[prt, :], in1=rec[prt, :], op=ALU.mult),
                         reads=[r_comb, r_rec], writes=[r_ybT])

                LA = 3
                for i, it in enumerate(items):
                    stage1(it)
                    if i >= LA:
                        stage2(items[i - LA])
                for it in items[len(items) - LA:]:
                    stage2(it)
                S.dma("sp", YB[j * 128:(j + 1) * 128, :], ybT[:], reads=[r_ybT], writes=r_YB)
        if stop == "P2a":
            break
        with phase_begin() as es:
            Wg = talloc(es, "Wg", [128, 8, D], BF16)[0][0]
            r_Wg = WRes()
            Wb = talloc(es, "Wb", [128, 4, D], BF16)[0][0]
            r_Wb = WRes()
            load_w(Wg, w_in[l][:, G_0 + D:G_0 + 2 * D], r_Wg)
            load_w(Wb, w_branch[l, 1], r_Wb)
            ybt = Rot(talloc(es, "ybt", [128, 4, TB], BF16, 2))
            bufs = {"msb": Rot(talloc(es, "msb", [128, TB], BF16, 3)), "tg": Rot(talloc(es, "tg", [128, TB], F32, 2)),
                    "bank_b": Rot([0, 1]), "bank_g": Rot([2, 3])}
            for t in range(NT):
                yb, r_yb = ybt.next()
                S.dma("sp", yb[:], YB.rearrange("(c p) t -> p c t", p=128)[:, :, t * TB:(t + 1) * TB], reads=[r_YB[t]], writes=[r_yb])
                gate_branch(1, t, lambda k: yb[:, k, :], r_yb, Wg, Wb, [r_Wg, r_Wb], bufs)
        if stop == "P2":
            break

        with phase_begin() as es:
            Wqk = talloc(es, "Wqk", [128, 8, 1024], BF16)[0][0]
            r_Wqk = WRes()
            Wv = talloc(es, "Wv3", [128, 8, 512], BF16)[0][0]
            r_Wv = WRes()
            Wo = talloc(es, "Wo3", [128, 8, 512], BF16)[0][0]
            r_Wo = WRes()
            Wif = talloc(es, "Wif", [128, 8, 8], BF16)[0][0]
            r_Wif = WRes()
            Wg = talloc(es, "Wg", [128, 8, D], BF16)[0][0]
            r_Wg = WRes()
            Wb = talloc(es, "Wb", [128, 4, D], BF16)[0][0]
            r_Wb = WRes()
            load_w(Wif, w_in[l][:, C_I:C_I + 8], r_Wif)
            load_w(Wqk, w_in[l][:, C_Q:C_Q + 1024], r_Wqk)
            load_w(Wv, w_in[l][:, C_V:C_V + 512], r_Wv)
            load_w(Wo, w_in[l][:, C_O:C_O + 512], r_Wo)
            load_w(Wg, w_in[l][:, G_0 + 2 * D:G_0 + 3 * D], r_Wg)
            load_w(Wb, w_branch[l, 2], r_Wb)
            gmh_h, r_gmh = talloc(es, "gmh_h", [128, 512], F32)[0]
            S.op("dve", lambda E: E.tensor_scalar(out=gmh_h[:], in0=prows[:, PR_GMH:PR_GMH + 512], scalar1=0.5, scalar2=None, op0=ALU.mult),
                 reads=[r_pr], writes=[r_gmh])
            for blk in range(NBLK):
                for k in range(8):
                    mm(ps[0][:, blk * 8:(blk + 1) * 8], H[:, k, blk * 128:(blk + 1) * 128], Wif[:, k, :], k == 0, k == 7,
                       [r_Wif.all(), r_H[blk // 4]], [r_ps[0]])
            ps3 = ps[0][:, 0:256].rearrange("p (b c) -> p b c", c=8)
            gi, r_gi = talloc(es, "gi", [128, 128], F32)[0]
            zf, r_zf = talloc(es, "zf3", [128, 128], F32)[0]
            E1, r_E1 = talloc(es, "E1", [128, 128], F32)[0]
            emb, r_emb = talloc(es, "emb", [128, 128], F32)[0]
            eg, r_eg = talloc(es, "eg", [128, 128], F32)[0]
            v4 = lambda ap: ap.rearrange("p (b c) -> p b c", c=4)
            S.op("dve", lambda E: E.tensor_tensor(out=v4(gi[:]), in0=ps3[:, :, 0:4], in1=v4(prows[:, PR_BI:PR_BI + 128]), op=ALU.add),
                 reads=[r_ps[0], r_pr], writes=[r_gi])
            S.op("dve", lambda E: E.tensor_tensor(out=v4(zf[:]), in0=ps3[:, :, 4:8], in1=v4(prows[:, PR_BF:PR_BF + 128]), op=ALU.add),
                 reads=[r_ps[0], r_pr], writes=[r_zf])
            S.op("act", lambda E: E.activation(out=zf[:], in_=zf[:], func=AF.Exp, scale=-1.0), reads=[r_zf], writes=[r_zf])
            S.op("act", lambda E: E.activation(out=zf[:], in_=zf[:], func=AF.Ln, bias=1.0), reads=[r_zf], writes=[r_zf])
            mm(ps[1][:, 0:128], U32[:], zf[:], True, True, [r_const, r_zf], [r_ps[1]])
            mm(ps[2][:, 0:128], ones32[:], zf[:], True, True, [r_const, r_zf], [r_ps[2]])
            S.op("dve", lambda E: E.tensor_tensor(out=gi[:], in0=ps[1][:, 0:128], in1=gi[:], op=ALU.add), reads=[r_ps[1], r_gi], writes=[r_gi])
            S.op("act", lambda E: E.activation(out=E1[:], in_=gi[:], func=AF.Exp), reads=[r_gi], writes=[r_E1])
            S.op("act", lambda E: E.activation(out=emb[:], in_=ps[1][:, 0:128], func=AF.Exp), reads=[r_ps[1]], writes=[r_emb])
            S.op("act", lambda E: E.activation(out=eg[:], in_=ps[2][:, 0:128], func=AF.Exp, scale=-1.0), reads=[r_ps[2]], writes=[r_eg])

            halo, r_halo = talloc(es, "halo", [128, 8, 3], F32)[0]
            S.op("pool", lambda E: E.memset(halo[:], 0.0), writes=[r_halo])
            prebufs = Rot(talloc(es, "prebuf", [128, 515], F32, 2))
            accs = Rot(talloc(es, "acc", [128, TB], F32, 2))
            tgs = Rot(talloc(es, "tg3", [128, TB], F32, 2))
            qkTs = Rot(talloc(es, "qkT", [128, 8, TB], BF16, 2))
            Vaugs = Rot(talloc(es, "Vaug", [128, 4, 129], BF16, 2))
            ogs = Rot(talloc(es, "og", [128, 512], F32, 2))
            Ats = Rot(talloc(es, "At", [128, 512], BF16, 2))
            ktoks = Rot(talloc(es, "ktok", [128, 512], BF16, 2))
            Cst, r_Cst = talloc(es, "Cst", [128, 4, 129], F32)[0]
            Cbf, r_Cbf = talloc(es, "Cbf", [128, 4, 129], BF16)[0]
            S.op("pool", lambda E: E.memset(Cst[:], 0.0), writes=[r_Cst])
            S.op("pool", lambda E: E.memset(Cbf[:], 0.0), writes=[r_Cbf])
            hhs = Rot(talloc(es, "hh", [128, 4, 128], F32, 2))
            dns = Rot(talloc(es, "dn", [128, 8], F32, 2))
            ss4s = Rot(talloc(es, "ss4", [128, 4], F32, 2))
            junk, r_junk = talloc(es, "junk3", [128, 128], BF16)[0]
            mh4, r_mh4 = talloc(es, "mh4_3", [128, 4], F32)[0]
            S.op("pool", lambda E: E.memset(mh4[:], -0.5), writes=[r_mh4])
            hcs = Rot(talloc(es, "hc", [128, 512], F32, 2))
            yctoks = Rot(talloc(es, "yctok", [128, 512], BF16, 2))
            ycTs = Rot(talloc(es, "ycT", [128, 4, TB], BF16, 1))
            bufs = {"msb": Rot(talloc(es, "msb", [128, TB], BF16, 3)), "tg": Rot(talloc(es, "tgg", [128, TB], F32, 2)),
                    "bank_b": Rot([0, 2]), "bank_g": Rot([1, 3])}
            bqk = Rot([0, 1])
            bvo = Rot([2, 3])
            def conv_chunk(t, c, qkT, r_qkT):
                tok = slice(t * TB, (t + 1) * TB)
                b = bvo.next()
                for k in range(8):
                    mm(ps[b][:], Wqk[:, k, c * 128:(c + 1) * 128], H[:, k, tok], k == 0, k == 7, [r_Wqk.at(c * 128), r_H[t]], [r_ps[b]])
                pbuf, r_pb = prebufs.next()
                S.op("pool", lambda E: E.tensor_copy(out=pbuf[:, 0:3], in_=halo[:, c, :]), reads=[r_halo], writes=[r_pb])
                S.op("act", lambda E: E.activation(out=pbuf[:, 3:515], in_=ps[b][:], func=AF.Copy), reads=[r_ps[b]], writes=[r_pb])
                acc, r_acc = accs.next()
                S.op("act", lambda E: E.activation(out=acc[:], in_=ps[b][:], func=AF.Copy, scale=pcols[:, PC_CONVC + c * 4 + 3:PC_CONVC + c * 4 + 4]),
                     reads=[r_ps[b], r_pc], writes=[r_acc])
                S.op("pool", lambda E: E.tensor_copy(out=halo[:, c, :], in_=pbuf[:, 512:515]), reads=[r_pb], writes=[r_halo])
                for jj in range(0, 3):
                    S.op("dve", lambda E: E.scalar_tensor_tensor(out=acc[:], in0=pbuf[:, jj:jj + 512], scalar=pcols[:, PC_CONVC + c * 4 + jj:PC_CONVC + c * 4 + jj + 1],
                                                                 in1=acc[:], op0=ALU.mult, op1=ALU.add), reads=[r_pb, r_pc, r_acc], writes=[r_acc])
                S.op("act", lambda E: E.activation(out=qkT[:, c, :], in_=acc[:], func=AF.Silu), reads=[r_acc], writes=[r_qkT])

            qk_of = {0: qkTs.next(), 1: qkTs.next()}
            for c in range(8):
                conv_chunk(0, c, *qk_of[0])
            for c in range(8):
                conv_chunk(1, c, *qk_of[1])
            ycT_of = {}
            blkst = {}

            def front(blk):
                t, bi = divmod(blk, 4)
                tk = slice(t * TB + bi * 128, t * TB + (bi + 1) * 128)
                bc = slice(bi * 128, (bi + 1) * 128)
                qkT, r_qkT = qk_of[t]
                b = bvo.next()
                for k in range(8):
                    mm(ps[b][:], H[:, k, tk], Wv[:, k, :], k == 0, k == 7, [r_Wv.all(), r_H[t]], [r_ps[b]])
                Vaug, r_Va = Vaugs.next()
                for h in range(4):
                    S.op("act", lambda E: E.activation(out=Vaug[:, h, 0:128], in_=ps[b][:, h * 128:(h + 1) * 128], func=AF.Copy,
                                                       scale=E1[:, blk * 4 + h:blk * 4 + h + 1]), reads=[r_ps[b], r_E1], writes=[r_Va])
                S.op("act", lambda E: E.activation(out=Vaug[:, :, 128:129], in_=E1[:, blk * 4:blk * 4 + 4].rearrange("p (h o) -> p h o", o=1),
                                                   func=AF.Copy), reads=[r_E1], writes=[r_Va])
                b2 = bvo.next()
                for k in range(8):
                    mm(ps[b2][:], H[:, k, tk], Wo[:, k, :], k == 0, k == 7, [r_Wo.all(), r_H[t]], [r_ps[b2]])
                og, r_og = ogs.next()
                S.op("act", lambda E: E.activation(out=og[:], in_=ps[b2][:], func=AF.Tanh, scale=0.5), reads=[r_ps[b2]], writes=[r_og])
                for h in range(4):
                    mm(ps[4][:, h * 128:(h + 1) * 128], qkT[:, 4 + h, bc], qkT[:, h, bc], True, True, [r_qkT], [r_ps[4]])
                for h in range(4):
                    S.op("pe", lambda E: E.transpose(pst[:, 512 + h * 128:512 + (h + 1) * 128], qkT[:, 4 + h, bc], ident[:]),
                         reads=[r_qkT, r_const], writes=[r_ps[7]])
                At, r_At = Ats.next()
                S.op("dve", lambda E: E.tensor_tensor(out=At[:], in0=ps[4][:], in1=tri4s[:], op=ALU.mult), reads=[r_ps[4], r_const], writes=[r_At])
                ktok, r_kt = ktoks.next()
                S.op("act", lambda E: E.activation(out=ktok[:], in_=pst[:, 512:1024], func=AF.Copy, scale=float(128 ** -0.5)), reads=[r_ps[7]], writes=[r_kt])
                blkst[blk] = (Vaug, r_Va, og, r_og, At, r_At, ktok, r_kt)

            def mid(blk):
                t, bi = divmod(blk, 4)
                bc = slice(bi * 128, (bi + 1) * 128)
                qkT, r_qkT = qk_of[t]
                Vaug, r_Va, og, r_og, At, r_At, ktok, r_kt = blkst[blk]
                for h in range(4):
                    bn = 5 + h // 2
                    cs = slice((h % 2) * 129, (h % 2) * 129 + 129)
                    mm(ps[bn][:, cs], At[:, h * 128:(h + 1) * 128], Vaug[:, h, :], True, False, [r_At, r_Va], [r_ps[bn]])
                    mm(ps[bn][:, cs], qkT[:, h, bc], Cbf[:, h, :], False, True, [r_qkT, r_Cbf], [r_ps[bn]])
                for h in range(4):
                    bk = h // 2
                    cs = slice((h % 2) * 129, (h % 2) * 129 + 129)
                    mm(ps[bk][:, cs], ktok[:, h * 128:(h + 1) * 128], Vaug[:, h, :], True, True, [r_kt, r_Va], [r_ps[bk]])
                for h in range(4):
                    bk = h // 2
                    cs = slice((h % 2) * 129, (h % 2) * 129 + 129)
                    if blk == 0:
                        S.op("dve", lambda E: E.tensor_copy(out=Cst[:, h, :], in_=ps[bk][:, cs]), reads=[r_ps[bk]], writes=[r_Cst])
                    else:
                        S.op("dve", lambda E: E.scalar_tensor_tensor(out=Cst[:, h, :], in0=Cst[:, h, :], scalar=eg[:, (blk - 1) * 4 + h:(blk - 1) * 4 + h + 1],
                                                                     in1=ps[bk][:, cs], op0=ALU.mult, op1=ALU.add),
                             reads=[r_Cst, r_eg, r_ps[bk]], writes=[r_Cst])
                for h in range(4):
                    S.op("act", lambda E: E.activation(out=Cbf[:, h, :], in_=Cst[:, h, :], func=AF.Copy, scale=eg[:, blk * 4 + h:blk * 4 + h + 1]),
                         reads=[r_Cst, r_eg], writes=[r_Cbf])

            def backA(blk):
                hh, r_hh = hhs.next()
                dn, r_dn = dns.next()
                for bn_ in range(2):
                    bn = 5 + bn_
                    wv = ps[bn][:, 0:258].rearrange("p (h c) -> p h c", c=129)[:, :, 128]
                    dcol = slice(2 * bn_, 2 * bn_ + 2)
                    S.op("dve", lambda E: E.tensor_scalar(out=dn[:, dcol], in0=wv, scalar1=-1.0, scalar2=None, op0=ALU.mult),
                         reads=[r_ps[bn]], writes=[r_dn])
                    S.op("dve", lambda E: E.tensor_tensor(out=dn[:, dcol], in0=dn[:, dcol], in1=wv, op=ALU.max), reads=[r_ps[bn], r_dn], writes=[r_dn])
                    S.op("dve", lambda E: E.tensor_tensor(out=dn[:, dcol], in0=dn[:, dcol], in1=emb[:, blk * 4 + 2 * bn_:blk * 4 + 2 * bn_ + 2], op=ALU.max),
                         reads=[r_dn, r_emb], writes=[r_dn])
                S.op("dve", lambda E: E.reciprocal(out=dn[:, 4:8], in_=dn[:, 0:4]), reads=[r_dn], writes=[r_dn])
                for h in range(4):
                    bn = 5 + h // 2
                    cs0 = (h % 2) * 129
                    S.op("dve", lambda E: E.tensor_scalar(out=hh[:, h, :], in0=ps[bn][:, cs0:cs0 + 128], scalar1=dn[:, 4 + h:5 + h], scalar2=None, op0=ALU.mult),
                         reads=[r_ps[bn], r_dn], writes=[r_hh])
                ss4, r_ss4 = ss4s.next()
                S.op("pool", lambda E: E.memset(ss4[:], 0.0), writes=[r_ss4])
                for h in range(4):
                    S.op("act", lambda E: E.activation(out=junk[:], in_=hh[:, h, :], func=AF.Square, accum_out=ss4[:, h:h + 1]),
                         reads=[r_hh, r_ss4], writes=[r_junk, r_ss4])
                blkst[("A", blk)] = (hh, r_hh, ss4, r_ss4)

            def backB(blk):
                t, bi = divmod(blk, 4)
                bc = slice(bi * 128, (bi + 1) * 128)
                Vaug, r_Va, og, r_og, At, r_At, ktok, r_kt = blkst.pop(blk)
                hh, r_hh, ss4, r_ss4 = blkst.pop(("A", blk))
                ycT, r_ycT = ycT_of[t]
                S.op("dve", lambda E: E.tensor_scalar(out=ss4[:], in0=ss4[:], scalar1=1.0 / 128, scalar2=EPS, op0=ALU.mult, op1=ALU.add),
                     reads=[r_ss4], writes=[r_ss4])
                S.op("pool", lambda E: E.tensor_tensor(out=ss4[:], in0=ss4[:], in1=mh4[:], op=ALU.pow), reads=[r_ss4, r_mh4], writes=[r_ss4])
                hc, r_hc = hcs.next()
                for h in range(4):
                    S.op("dve", lambda E: E.scalar_tensor_tensor(out=hc[:, h * 128:(h + 1) * 128], in0=hh[:, h, :], scalar=ss4[:, h:h + 1],
                                                                 in1=gmh_h[:, h * 128:(h + 1) * 128], op0=ALU.mult, op1=ALU.mult),
                         reads=[r_hh, r_ss4, r_gmh], writes=[r_hc])
                yct, r_yct = yctoks.next()
                S.op("dve", lambda E: E.scalar_tensor_tensor(out=yct[:], in0=og[:], scalar=1.0, in1=hc[:], op0=ALU.add, op1=ALU.mult),
                     reads=[r_og, r_hc], writes=[r_yct])

                def fin():
                    for c in range(4):
                        S.op("pe", lambda E: E.transpose(pst[:, c * 128:(c + 1) * 128], yct[:, c * 128:(c + 1) * 128], ident[:]),
                             reads=[r_yct, r_const], writes=[r_ps[7]])
                    S.op("act", lambda E: E.activation(out=ycT[:, :, bc], in_=pst[:, 0:512].rearrange("p (c t) -> p c t", t=128), func=AF.Copy),
                         reads=[r_ps[7]], writes=[r_ycT])
                return fin

            pending = []
            front(0)
            for blk in range(NBLK):
                t, bi = divmod(blk, 4)
                if bi == 0:
                    ycT_of[t] = ycTs.next()
                mid(blk)
                for f in pending:
                    f()
                pending = []
                backA(blk)
                if blk + 1 < NBLK:
                    front(blk + 1)
                pending.append(backB(blk))
                if bi == 3:
                    for f in pending:
                        f()
                    pending = []
                    ycT, r_ycT = ycT_of.pop(t)
                    qk_of.pop(t)
                    hook = None
                    if t + 2 < NT:
                        qk_of[t + 2] = qkTs.next()
                        hook = (lambda oc, tt=t + 2: conv_chunk(tt, oc, *qk_of[tt]))
                    gate_branch(2, t, lambda k: ycT[:, k, :], r_ycT, Wg, Wb, [r_Wg, r_Wb], bufs, hook=hook)
        if stop == "P3":
            break

        with phase_begin() as es:
            Wo_ = talloc(es, "Wout", [128, 8, D], BF16)[0][0]
            r_Wo_ = WRes()
            load_w(Wo_, w_out[l], r_Wo_)
            ph = {"ps_ss": 2, "sq": Rot(talloc(es, "sq", [128, 512], BF16, 8)), "lnb": talloc(es, "lnb", [128, 512], F32)[0]}
            mts = [Rot(talloc(es, "mt%d" % r, [128, 8, TB], BF16, 2)) for r in range(3)]
            mgs = Rot(talloc(es, "merged", [128, 8, TB], BF16, 2))
            tails = []
            xts = Rot(talloc(es, "xt", [128, 8, TB], F32, 2))
            bo = Rot([0, 1])
            loaded = {}

            xloaded = {}

            def issue_loads(tt):
                mt_ = [mts[r].next() for r in range(3)]
                for r in range(3):
                    S.dma("sp", mt_[r][0][:], tmaj(Mr[r], tt), reads=[r_M[r][tt]], writes=[mt_[r][1]])
                loaded[tt] = mt_

            def issue_xload(tt):
                xt_, r_xt_ = xts.next()
                S.dma("sp", xt_[:], tmaj(X, tt), reads=[r_X[tt]], writes=[r_xt_])
                xloaded[tt] = (xt_, r_xt_)

            issue_loads(0)
            issue_xload(0)
            for t in range(NT):
                tok = slice(t * TB, (t + 1) * TB)
                if t + 1 < NT:
                    issue_loads(t + 1)
                mt = loaded.pop(t)
                xt, r_xt = xloaded.pop(t)
                merged, r_mg = mgs.next()
                S.op("dve", lambda E: E.tensor_tensor(out=merged[:], in0=mt[0][0][:], in1=mt[1][0][:], op=ALU.add),
                     reads=[mt[0][1], mt[1][1]], writes=[r_mg])
                S.op("dve", lambda E: E.tensor_tensor(out=merged[:], in0=merged[:], in1=mt[2][0][:], op=ALU.add),
                     reads=[r_mg, mt[2][1]], writes=[r_mg])
                for oc in range(8):
                    b = bo.next()
                    for k in range(8):
                        mm(ps[b][:], Wo_[:, k, oc * 128:(oc + 1) * 128], merged[:, k, :], k == 0, k == 7, [r_Wo_.at(oc * 128), r_mg], [r_ps[b]])
                    S.op("dve", lambda E: E.scalar_tensor_tensor(out=xt[:, oc, :], in0=ps[b][:], scalar=0.5, in1=xt[:, oc, :], op0=ALU.mult, op1=ALU.add),
                         reads=[r_ps[b], r_xt], writes=[r_xt])
                    if oc == 2:
                        for f in tails:
                            f()
                        tails = []
                        if t + 1 < NT:
                            issue_xload(t + 1)
                S.dma("act", tmaj(X, t), xt[:], reads=[r_xt], writes=[r_X[t]])
                tails.append(norm_tile(ph, xt, r_xt, PC_GMQ, lambda c, tok=tok, t=t: (H[:, c, tok], r_H[t]), None, defer=True))
            for f in tails:
                f()
        if stop == "P4":
            break

        with phase_begin() as es:
            kmemT, r_km = talloc(es, "kmemT", [128, 8, NMEM], BF16)[0]
            vmem, r_vm = talloc(es, "vmem", [128, 2, D], BF16)[0]
            ph = {"ps_ss": 2, "sq": Rot(talloc(es, "sq", [128, 512], BF16, 8)), "lnb": talloc(es, "lnb", [128, 512], F32)[0]}
            with ExitStack() as es2:
                Wkv = talloc(es2, "Wkv", [128, 8, 2 * D], BF16)[0][0]
                r_Wkv = WRes()
                load_w(Wkv, w_mkv[l], r_Wkv)
                memx, r_memx = talloc(es2, "memx", [128, 8, NMEM], F32)[0]
                memn, r_memn = talloc(es2, "memn", [128, 8, NMEM], BF16)[0]
                S.dma("sp", memx[:], fm(memT_in), writes=[r_memx])
                norm_tile(ph, memx, r_memx, PC_GMKV, lambda c: (memn[:, c, :], r_memn), None, w=NMEM)
                bb = Rot([0, 1])
                for c in range(8):
                    b = bb.next()
                    for k in range(8):
                        mm(ps[b][:, 0:NMEM], Wkv[:, k, c * 128:(c + 1) * 128], memn[:, k, :], k == 0, k == 7, [r_Wkv.at(c * 128), r_memn], [r_ps[b]])
                    S.op("act", lambda E: E.activation(out=kmemT[:, c, :], in_=ps[b][:, 0:NMEM], func=AF.Copy), reads=[r_ps[b]], writes=[r_km])
                for mtk in range(2):
                    for hf in range(2):
                        b = bb.next()
                        for k in range(8):
                            mm(ps[b][:], memn[:, k, mtk * 128:(mtk + 1) * 128], Wkv[:, k, D + hf * 512:D + (hf + 1) * 512], k == 0, k == 7,
                               [r_Wkv.at(D + hf * 512), r_memn], [r_ps[b]])
                        S.op("dve", lambda E: E.tensor_copy(out=vmem[:, mtk, hf * 512:(hf + 1) * 512], in_=ps[b][:]), reads=[r_ps[b]], writes=[r_vm])
                S.barrier()
            Wmq = talloc(es, "Wmq", [128, 8, D], BF16)[0][0]
            r_Wmq = WRes()
            Wmo = talloc(es, "Wmo", [128, 8, D], BF16)[0][0]
            r_Wmo = WRes()
            load_w(Wmq, w_mq[l], r_Wmq)
            load_w(Wmo, w_mo[l], r_Wmo)
            qm, r_qm = talloc(es, "qm", [128, 8, TB], BF16)[0]
            om, r_om = talloc(es, "om", [128, 8, TB], BF16)[0]
            PTm = Rot(talloc(es, "PTm", [128, TB], BF16, 4))
            rls = Rot(talloc(es, "rl", [128, TB], F32, 2))
            xts = Rot(talloc(es, "xt", [128, 8, TB], F32, 2))
            bq = Rot([0, 1])
            bs_ = Rot([3, 4])
            tails = []
            for t in range(NT):
                tok = slice(t * TB, (t + 1) * TB)
                xt, r_xt = xts.next()
                S.dma("sp", xt[:], tmaj(X, t), reads=[r_X[t]], writes=[r_xt])
                for c in range(8):
                    if c == 3:
                        for f in tails:
                            f()
                        tails = []
                    b = bq.next()
                    for k in range(8):
                        mm(ps[b][:], Wmq[:, k, c * 128:(c + 1) * 128], H[:, k, tok], k == 0, k == 7, [r_Wmq.at(c * 128), r_H[t]], [r_ps[b]])
                    if c % 2 == 0:
                        S.op("act", lambda E: E.activation(out=qm[:, c, :], in_=ps[b][:], func=AF.Copy), reads=[r_ps[b]], writes=[r_qm])
                    else:
                        S.op("dve", lambda E: E.tensor_copy(out=qm[:, c, :], in_=ps[b][:]), reads=[r_ps[b]], writes=[r_qm])
                for hd in range(4):
                    pts = []
                    for mtk in range(2):
                        b = bs_.next()
                        for kc in range(2):
                            mm(ps[b][:], kmemT[:, 2 * hd + kc, mtk * 128:(mtk + 1) * 128], qm[:, 2 * hd + kc, :], kc == 0, kc == 1,
                               [r_km, r_qm], [r_ps[b]])
                        PT, r_PT = PTm.next()
                        S.op("act", lambda E: E.activation(out=PT[:], in_=ps[b][:], func=AF.Exp, scale=1.0 / 16), reads=[r_ps[b]], writes=[r_PT])
                        pts.append((PT, r_PT))
                    for mtk in range(2):
                        mm(ps[5][:], ones_bf[:], pts[mtk][0][:], mtk == 0, mtk == 1, [r_const, pts[mtk][1]], [r_ps[5]])
                    rl, r_rl = rls.next()
                    S.op("act", lambda E: E.activation(out=rl[:], in_=ps[5][:], func=AF.Ln), reads=[r_ps[5]], writes=[r_rl])
                    S.op("act", lambda E: E.activation(out=rl[:], in_=rl[:], func=AF.Exp, scale=-1.0), reads=[r_rl], writes=[r_rl])
                    for ec in range(2):
                        for mtk in range(2):
                            mm(ps[6][:], vmem[:, mtk, (2 * hd + ec) * 128:(2 * hd + ec + 1) * 128], pts[mtk][0][:], mtk == 0, mtk == 1,
                               [r_vm, pts[mtk][1]], [r_ps[6]])
                        S.op("dve", lambda E: E.tensor_tensor(out=om[:, 2 * hd + ec, :], in0=ps[6][:], in1=rl[:], op=ALU.mult),
                             reads=[r_ps[6], r_rl], writes=[r_om])
                for oc in range(8):
                    b = bq.next()
                    for k in range(8):
                        mm(ps[b][:], Wmo[:, k, oc * 128:(oc + 1) * 128], om[:, k, :], k == 0, k == 7, [r_Wmo.at(oc * 128), r_om], [r_ps[b]])
                    S.op("dve", lambda E: E.tensor_tensor(out=xt[:, oc, :], in0=ps[b][:], in1=xt[:, oc, :], op=ALU.add),
                         reads=[r_ps[b], r_xt], writes=[r_xt])
                S.dma("act", tmaj(X, t), xt[:], reads=[r_xt], writes=[r_X[t]])
                tails.append(norm_tile(ph, xt, r_xt, PC_GFFN, lambda c, tok=tok, t=t: (H[:, c, tok], r_H[t]), None, defer=True))
            for f in tails:
                f()
        if stop == "P5":
            break

        NH = DFF // 2
        for g in range(2):
            with phase_begin() as es:
                Wa = talloc(es, "Wa", [128, 8, NH], BF16)[0][0]
                r_Wa = WRes()
                Wb_ = talloc(es, "Wbb", [128, 8, NH], BF16)[0][0]
                r_Wbb = WRes()
                Wd = talloc(es, "Wd", [128, 11, D], BF16)[0][0]
                r_Wd = WRes()
                for c0_ in range(0, NH, 384):
                    load_w(Wa, w_up[l][:, g * NH:(g + 1) * NH], r_Wa, cols=(c0_, min(NH, c0_ + 384)))
                    load_w(Wb_, w_up[l][:, DFF + g * NH:DFF + (g + 1) * NH], r_Wbb, cols=(c0_, min(NH, c0_ + 384)))
                load_w(Wd, w_down[l][g * NH:(g + 1) * NH, :], r_Wd)
                ph = {"ps_ss": 6, "sq": Rot(talloc(es, "sq", [128, 512], BF16, 2)), "lnb": talloc(es, "lnb", [128, 512], F32)[0]}
                halo, r_halo = talloc(es, "haloF", [128, 22, 2], F32)[0]
                S.op("pool", lambda E: E.memset(halo[:], 0.0), writes=[r_halo])
                xbufs = Rot(talloc(es, "xbuf", [128, 514], F32, 4))
                accs = Rot(talloc(es, "accF", [128, TB], F32, 4))
                tgs = Rot(talloc(es, "tgF", [128, TB], F32, 2))
                hid, r_hid = talloc(es, "hid", [128, 11, TB], BF16)[0]
                xt, r_xt = talloc(es, "xtF", [128, 8, TB], F32)[0]
                last = (g == 1 and l == depth - 1)
                ybufs = Rot(talloc(es, "ybuf", [128, TB], F32, 2)) if last else None
                bA = Rot([0, 1])
                bB = Rot([2, 3])
                bD = Rot([4, 5])
                for t in range(NT):
                    tok = slice(t * TB, (t + 1) * TB)
                    S.dma("sp", xt[:], tmaj(X, t), reads=[r_X[t]], writes=[r_xt])
                    for cc in range(11):
                        accp = []
                        for half, (Wx, r_Wx, brot) in enumerate(((Wa, r_Wa, bA), (Wb_, r_Wbb, bB))):
                            b = brot.next()
                            for k in range(8):
                                mm(ps[b][:], Wx[:, k, cc * 128:(cc + 1) * 128], H[:, k, tok], k == 0, k == 7, [r_Wx.at(cc * 128), r_H[t]], [r_ps[b]])
                            hidx = half * 11 + cc
                            pcol = PC_FCONV + (half * 22 + g * 11 + cc) * 3
                            xb, r_xb = xbufs.next()
                            S.op("pool", lambda E: E.tensor_copy(out=xb[:, 0:2], in_=halo[:, hidx, :]), reads=[r_halo], writes=[r_xb])
                            S.op("act", lambda E: E.activation(out=xb[:, 2:514], in_=ps[b][:], func=AF.Copy), reads=[r_ps[b]], writes=[r_xb])
                            S.op("pool", lambda E: E.tensor_copy(out=halo[:, hidx, :], in_=xb[:, 512:514]), reads=[r_xb], writes=[r_halo])
                            acc, r_acc = accs.next()
                            S.op("act", lambda E: E.activation(out=acc[:], in_=ps[b][:], func=AF.Copy, scale=pcols[:, pcol + 2:pcol + 3]),
                                 reads=[r_ps[b], r_pc], writes=[r_acc])
                            for jj in range(2):
                                S.op("dve", lambda E: E.scalar_tensor_tensor(out=acc[:], in0=xb[:, jj:jj + 512], scalar=pcols[:, pcol + jj:pcol + jj + 1],
                                                                             in1=acc[:], op0=ALU.mult, op1=ALU.add), reads=[r_xb, r_pc, r_acc], writes=[r_acc])
                            accp.append((acc, r_acc))
                        (aA, r_aA), (aB, r_aB) = accp
                        tg, r_tg = tgs.next()
                        S.op("act", lambda E: E.activation(out=tg[:], in_=aA[:], func=AF.Silu), reads=[r_aA], writes=[r_tg])
                        S.op("dve", lambda E: E.tensor_tensor(out=hid[:, cc, :], in0=tg[:], in1=aB[:], op=ALU.mult), reads=[r_tg, r_aB], writes=[r_hid])
                    for oc in range(8):
                        b = bD.next()
                        for cc in range(11):
                            mm(ps[b][:], Wd[:, cc, oc * 128:(oc + 1) * 128], hid[:, cc, :], cc == 0, cc == 10, [r_Wd.at(oc * 128), r_hid], [r_ps[b]])
                        S.op("dve", lambda E: E.tensor_tensor(out=xt[:, oc, :], in0=ps[b][:], in1=xt[:, oc, :], op=ALU.add),
                             reads=[r_ps[b], r_xt], writes=[r_xt])
                    if not last:
                        S.dma("act", tmaj(X, t), xt[:], reads=[r_xt], writes=[r_X[t]])
                    if g == 1:
                        if last:
                            def outv(c):
                                return ybufs.next()

                            def after(c, ov, r_ov, tok=tok, t=t):
                                S.dma("sp", yT[t][:, c * TB:(c + 1) * TB], ov, reads=[r_ov], writes=[r_y[t]])
                            norm_tile(ph, xt, r_xt, PC_GNEXT, lambda c: (lambda tr: (tr[0][:], tr[1]))(ybufs.next()), None, after_chunk=after)
                        else:
                            norm_tile(ph, xt, r_xt, PC_GNEXT, lambda c: (H[:, c, tok], r_H[t]), None)
            if stop == "P6a" and g == 0:
                break
        if stop in ("P6", "P6a"):
            break

    if debug:
        S.barrier()
        S.dma("sp", fm(Hdbg), H[:], reads=r_H, writes=[Res()])
    S.finish_all()
    return nc


def _pack_params(inp):
    L = DEPTH
    pc = np.zeros((L, 128, NPC), np.float32)
    pr = np.zeros((L, NPR), np.float32)
    for l in range(L):
        pc[l, :, PC_GMIX:PC_GMIX + 8] = inp["g_mix"][l].reshape(8, 128).T
        pc[l, :, PC_GMQ:PC_GMQ + 8] = inp["g_mem_q"][l].reshape(8, 128).T
        pc[l, :, PC_GMKV:PC_GMKV + 8] = inp["g_mem_kv"][l].reshape(8, 128).T
        pc[l, :, PC_GFFN:PC_GFFN + 8] = inp["g_ffn"][l].reshape(8, 128).T
        pc[l, :, PC_CONVC:PC_CONVC + 32] = inp["w_conv_c"][l].reshape(4, 8, 128).transpose(2, 1, 0).reshape(128, 32)
        pc[l, :, PC_FCONV:PC_FCONV + 132] = inp["w_ffn_conv"][l].reshape(3, 44, 128).transpose(2, 1, 0).reshape(128, 132)
        gn = inp["g_mix"][l + 1] if l + 1 < L else inp["g_final"]
        pc[l, :, PC_GNEXT:PC_GNEXT + 8] = gn.reshape(8, 128).T
        pr[l, PR_GSGU:PR_GSGU + 512] = inp["g_sgu"][l]
        pr[l, PR_BS:PR_BS + 512] = inp["b_s"][l].reshape(512)
        pr[l, PR_BFOX:PR_BFOX + 256] = np.tile(inp["b_fox_f"][l], 32)
        pr[l, PR_BI:PR_BI + 128] = np.tile(inp["b_mlstm_i"][l], 32)
        pr[l, PR_BF:PR_BF + 128] = np.tile(inp["b_mlstm_f"][l], 32)
        pr[l, PR_GMH:PR_GMH + 512] = inp["g_mh"][l]
    return pc, pr


def make_in_maps(inp, n_cores=8):
    inp = {k: np.asarray(v) for k, v in inp.items()}
    pc, pr = _pack_params(inp)
    shared = {
        "w_in": np.ascontiguousarray(inp["w_in"]),
        "wsT": np.ascontiguousarray(inp["w_s"].transpose(0, 1, 3, 2)),
        "w_branch": np.ascontiguousarray(inp["w_branch"]),
        "w_out": np.ascontiguousarray(inp["w_out"]),
        "w_mq": np.ascontiguousarray(inp["w_mq"]),
        "w_mkv": np.ascontiguousarray(inp["w_mkv"]),
        "w_mo": np.ascontiguousarray(inp["w_mo"]),
        "w_up": np.ascontiguousarray(inp["w_up"]),
        "w_down": np.ascontiguousarray(inp["w_down"]),
        "pcols": pc,
        "prows": pr,
    }
    maps = []
    for b in range(n_cores):
        m = dict(shared)
        m["xT"] = np.ascontiguousarray(inp["x"][b].reshape(NT, TB, 8, 128).transpose(0, 3, 2, 1).reshape(NT, 128, 8 * TB))
        m["memT"] = np.ascontiguousarray(inp["mem"][b].T)
        maps.append(m)
    return maps


def _untile(a):
    return np.ascontiguousarray(np.asarray(a).reshape(NT, 128, 8, TB).transpose(0, 3, 2, 1).reshape(S_LEN, D))


def kernel(**inputs):
    nc = build_program()
    in_maps = make_in_maps(inputs)
    res = run_bass_kernel_spmd(nc, in_maps, core_ids=list(range(8)))
    out = np.stack([_untile(r["yT"]) for r in res.results], axis=0)
    return out.astype(np.float32)
```

```python
import numpy as np
from contextlib import ExitStack
import concourse.bass as bass
import concourse.mybir as mybir
from concourse.bass_utils import run_bass_kernel_spmd

F32 = mybir.dt.float32
BF16 = mybir.dt.bfloat16
ALU = mybir.AluOpType
AF = mybir.ActivationFunctionType

D = 1024
S_LEN = 4096
DEPTH = 4
NMEM = 256
TB = 512
NT = S_LEN // TB
NBLK = S_LEN // 128
DFF = 2816
EPS = 1e-6
IN_COLS = 7696
A_U, A_V = 0, 512
B_Q, B_K, B_V, B_F = 1024, 1536, 2048, 2560
C_Q, C_K, C_V, C_I, C_F, C_O = 2568, 3080, 3592, 4104, 4108, 4112
G_0 = 4624
PC_GMIX, PC_GMQ, PC_GMKV, PC_GFFN, PC_CONVC, PC_FCONV, PC_GNEXT = 0, 8, 16, 24, 32, 64, 196
NPC = 204
PR_GSGU, PR_BS, PR_BFOX, PR_BI, PR_BF, PR_GMH = 0, 512, 1024, 1280, 1408, 1536
NPR = 2048


class Res:
    __slots__ = ("w", "r")

    def __init__(self):
        self.w = None
        self.r = []


class Sched:
    def __init__(self, nc):
        self.nc = nc
        self.eng = {"pe": nc.tensor, "dve": nc.vector, "act": nc.scalar, "pool": nc.gpsimd, "sp": nc.sync}
        self.sem = {k: nc.alloc_semaphore(name="s_" + k) for k in self.eng}
        self.cnt = {k: 0 for k in self.eng}
        self.seen = {k: {} for k in self.eng}
        self.dsem = {}
        self.dcnt = {}
        self.ninst = 0

    def _deps(self, e, reads, writes):
        deps = {}

        def add(tok, raw):
            k, v = tok
            if k == e and e == "pe":
                return
            if deps.get(k, 0) < v:
                deps[k] = v

        for r in reads:
            if r.w is not None:
                add(r.w, True)
        for w in writes:
            if w.w is not None:
                add(w.w, False)
            for t in w.r:
                add(t, False)
        return deps

    def _semobj(self, k):
        return self.sem[k] if k in self.sem else self.dsem[k]

    def _emit_waits(self, e, deps):
        seen = self.seen[e]
        need = [(k, v) for k, v in deps.items() if seen.get(k, 0) < v]
        E = self.eng[e]
        for (k, v) in need[1:]:
            E.wait_ge(self._semobj(k), v)
            seen[k] = v
            self.ninst += 1
        return need[0] if need else None

    def _mark(self, tok, reads, writes):
        for r in reads:
            r.r.append(tok)
        for w in writes:
            w.w = tok
            w.r = []

    def op(self, e, fn, reads=(), writes=()):
        first = self._emit_waits(e, self._deps(e, reads, writes))
        ins = fn(self.eng[e])
        if first:
            ins._wait_ge(self._semobj(first[0]), first[1])
            self.seen[e][first[0]] = first[1]
        self.cnt[e] += 1
        self.ninst += 1
        ins.then_inc(self.sem[e], 1)
        tok = (e, self.cnt[e])
        self._mark(tok, reads, writes)
        return tok

    def dma(self, q, out, in_, reads=(), writes=(), **kw):
        first = self._emit_waits(q, self._deps(q, reads, writes))
        i = self.dcnt.get(q, 0)
        self.dcnt[q] = i + 1
        key = "d_%s_%d" % (q, i % 16)
        if key not in self.dsem:
            self.dsem[key] = self.nc.alloc_semaphore(name=key)
            self.dcnt[key] = 0
        elif self.seen[q].get(key, 0) < self.dcnt[key]:
            self.eng[q].wait_ge(self.dsem[key], self.dcnt[key])
            self.seen[q][key] = self.dcnt[key]
            self.ninst += 1
        ins = self.eng[q].dma_start(out=out, in_=in_, **kw)
        if first:
            ins._wait_ge(self._semobj(first[0]), first[1])
            self.seen[q][first[0]] = first[1]
        self.dcnt[key] += 16
        self.ninst += 1
        ins.then_inc(self.dsem[key], 16)
        tok = (key, self.dcnt[key])
        self._mark(tok, reads, writes)
        return tok

    def barrier(self):
        snap = [(k, self.cnt[k]) for k in self.sem if k != "sp" and self.cnt[k] > 0]
        snap += [(k, self.dcnt[k]) for k in self.dsem]
        for e in self.eng:
            seen = self.seen[e]
            for (k, v) in snap:
                if (k != e or e in ("dve", "act", "pool")) and seen.get(k, 0) < v:
                    self.eng[e].wait_ge(self._semobj(k), v)
                    seen[k] = v
                    self.ninst += 1

    def finish_all(self):
        self.barrier()


class WRes:
    def __init__(self):
        self.parts = []

    def at(self, c):
        for c0, c1, r in self.parts:
            if c0 <= c < c1:
                return r
        raise KeyError(c)

    def all(self):
        return [r for _, _, r in self.parts]


def _flat(lst):
    out = []
    for x in lst:
        if isinstance(x, (list, tuple)):
            out.extend(_flat(x))
        else:
            out.append(x)
    return out


class Rot:
    def __init__(self, items):
        self.items = items
        self.i = 0

    def next(self):
        it = self.items[self.i % len(self.items)]
        self.i += 1
        return it


def build_program(depth=DEPTH, stop=None, debug=False):
    nc = bass.Bass("TRN2", target_bir_lowering=False)
    S = Sched(nc)
    ctx = ExitStack()

    def dram_in(name, shape, dt=F32):
        return nc.dram_tensor(name, shape, dt, kind="ExternalInput").ap()

    skind = "ExternalOutput" if debug else "Internal"

    xT_in = dram_in("xT", [NT, 128, 8 * TB])
    memT_in = dram_in("memT", [D, NMEM])
    w_in = dram_in("w_in", [DEPTH, D, IN_COLS])
    wsT_in = dram_in("wsT", [DEPTH, 4, 128, 128])
    w_branch = dram_in("w_branch", [DEPTH, 3, 512, D])
    w_out = dram_in("w_out", [DEPTH, D, D])
    w_mq = dram_in("w_mq", [DEPTH, D, D])
    w_mkv = dram_in("w_mkv", [DEPTH, D, 2 * D])
    w_mo = dram_in("w_mo", [DEPTH, D, D])
    w_up = dram_in("w_up", [DEPTH, D, 2 * DFF])
    w_down = dram_in("w_down", [DEPTH, DFF, D])
    pcols_in = dram_in("pcols", [DEPTH, 128, NPC])
    prows_in = dram_in("prows", [DEPTH, NPR])
    yT = nc.dram_tensor("yT", [NT, 128, 8 * TB], F32, kind="ExternalOutput").ap()
    X = nc.dram_tensor("Xs", [NT, 128, 8 * TB], F32, kind=skind).ap()
    Mr = [nc.dram_tensor("M%d" % r, [NT, 128, 8 * TB], BF16, kind=skind).ap() for r in range(3)]
    YB = nc.dram_tensor("YBs", [512, S_LEN], BF16, kind=skind).ap()
    Hdbg = nc.dram_tensor("Hdbg", [D, S_LEN], BF16, kind="ExternalOutput").ap() if debug else None

    def fm(ap2d):
        return ap2d.rearrange("(c p) t -> p c t", p=128)

    def tmaj(ap3d, t):
        return ap3d[t].rearrange("p (c t) -> p c t", t=TB)

    r_X = [Res() for _ in range(NT)]
    r_M = [[Res() for _ in range(NT)] for _ in range(3)]
    r_YB = [Res() for _ in range(NT)]
    r_y = [Res() for _ in range(NT)]

    def sb(name, shape, dt):
        return nc.alloc_sbuf_tensor(name, shape, dt)

    H = sb("H", [128, 8, S_LEN], BF16)
    r_H = [Res() for _ in range(NT)]
    ident = sb("ident", [128, 128], BF16)
    ones_bf = sb("ones_bf", [128, 128], BF16)
    tri_bf = sb("tri_bf", [128, 128], BF16)
    tri4 = sb("tri4", [128, 512], BF16)
    tri4s = sb("tri4s", [128, 512], F32)
    U32 = sb("U32", [128, 128], F32)
    ones32 = sb("ones32", [128, 128], F32)
    Umid = sb("Umid", [128, 128], F32)
    pcols = sb("pcols_sb", [128, NPC], F32)
    prows = sb("prows_sb", [128, NPR], F32)
    wtap = sb("wtap", [128, 32], F32)
    r_const, r_pc, r_pr, r_wtap = Res(), Res(), Res(), Res()

    ps = [nc.alloc_psum_tensor("ps%d" % i, [128, 512], F32) for i in range(7)]
    pst = nc.alloc_psum_tensor("pst", [128, 1024], BF16)
    r_ps = [Res() for _ in range(9)]

    S.op("pool", lambda E: E.memset(ones_bf[:], 1.0), writes=[r_const])
    S.op("pool", lambda E: E.memset(ones32[:], 1.0), writes=[r_const])
    S.op("pool", lambda E: E.memset(ident[:], 1.0), writes=[r_const])
    S.op("pool", lambda E: E.affine_select(out=ident[:], in_=ident[:], pattern=[[1, 128]], compare_op=ALU.is_equal,
                                           fill=0.0, base=0, channel_multiplier=-1), reads=[r_const], writes=[r_const])
    S.op("pool", lambda E: E.memset(tri_bf[:], 1.0), writes=[r_const])
    S.op("pool", lambda E: E.affine_select(out=tri_bf[:], in_=tri_bf[:], pattern=[[1, 128]], compare_op=ALU.is_ge,
                                           fill=0.0, base=0, channel_multiplier=-1), reads=[r_const], writes=[r_const])
    for h_ in range(4):
        S.op("pool", lambda E: E.tensor_copy(out=tri4[:, h_ * 128:(h_ + 1) * 128], in_=tri_bf[:]), reads=[r_const], writes=[r_const])
    S.op("pool", lambda E: E.tensor_scalar(out=tri4s[:], in0=tri4[:], scalar1=float(128 ** -0.5), scalar2=None, op0=ALU.mult), reads=[r_const], writes=[r_const])
    S.op("pool", lambda E: E.memset(U32[:], 1.0), writes=[r_const])
    S.op("pool", lambda E: E.affine_select(out=U32[:], in_=U32[:], pattern=[[1, 128]], compare_op=ALU.is_ge,
                                           fill=0.0, base=0, channel_multiplier=-1), reads=[r_const], writes=[r_const])
    S.op("pool", lambda E: E.memset(Umid[:], 1.0), writes=[r_const])
    S.op("pool", lambda E: E.affine_select(out=Umid[:], in_=Umid[:], pattern=[[0, 128]], compare_op=ALU.is_ge,
                                           fill=0.0, base=64, channel_multiplier=-1), reads=[r_const], writes=[r_const])

    def mm(out, lhsT, rhs, start, stop, reads, writes):
        S.op("pe", lambda E: E.matmul(out, lhsT=lhsT, rhs=rhs, start=start, stop=stop), reads=_flat(reads), writes=writes)

    def load_w(dst, src2d, wres, ncol_split=512, cols=None):
        n = src2d.shape[1]
        v = src2d.rearrange("(c p) n -> p c n", p=128)
        rng = [(c0, min(n, c0 + ncol_split)) for c0 in range(0, n, ncol_split)] if cols is None else [cols]
        for (c0, c1) in rng:
            r = Res()
            S.dma("pool", dst[:, :, c0:c1], v[:, :, c0:c1], writes=[r])
            wres.parts.append((c0, c1, r))

    def norm_tile(ph, xt, r_xt, gcol0, out_views, r_out, w=TB, after_chunk=None, defer=False):
        ssb = ph["ps_ss"]
        sqs = []
        for c in range(8):
            sq, r_sq = ph["sq"].next()
            S.op("act", lambda E: E.activation(out=sq[:, 0:w], in_=xt[:, c, :], func=AF.Square), reads=[r_xt], writes=[r_sq])
            sqs.append((sq, r_sq))
            if not defer:
                mm(ps[ssb][:, 0:w], ones_bf[:], sq[:, 0:w], c == 0, c == 7, [r_sq, r_const], [r_ps[ssb]])

        def tail():
            if defer:
                for c, (sq, r_sq) in enumerate(sqs):
                    mm(ps[ssb][:, 0:w], ones_bf[:], sq[:, 0:w], c == 0, c == 7, [r_sq, r_const], [r_ps[ssb]])
            lnb, r_ln = ph["lnb"]
            S.op("act", lambda E: E.activation(out=lnb[:, 0:w], in_=ps[ssb][:, 0:w], func=AF.Ln, scale=1.0 / D, bias=EPS),
                 reads=[r_ps[ssb]], writes=[r_ln])
            S.op("act", lambda E: E.activation(out=lnb[:, 0:w], in_=lnb[:, 0:w], func=AF.Exp, scale=-0.5), reads=[r_ln], writes=[r_ln])
            for c in range(8):
                ov, r_ov = out_views(c)
                S.op("dve", lambda E: E.scalar_tensor_tensor(out=ov, in0=xt[:, c, :], scalar=pcols[:, gcol0 + c:gcol0 + c + 1],
                                                             in1=lnb[:, 0:w], op0=ALU.mult, op1=ALU.mult),
                     reads=[r_xt, r_ln, r_pc], writes=[r_ov])
                if after_chunk is not None:
                    after_chunk(c, ov, r_ov)

        if defer:
            return tail
        tail()
        return None

    def phase_begin():
        S.barrier()
        return ExitStack()

    uid = [0]

    def talloc(es, name, shape, dt, n=1):
        items = []
        for i in range(n):
            uid[0] += 1
            t = es.enter_context(nc.sbuf_tensor("%s_%d_%d" % (name, i, uid[0]), shape, dt))
            items.append((t, Res()))
        return items

    def gate_branch(r, t, yT_views, r_y, Wg, Wb, r_w, es_bufs, hook=None):
        for oc in range(8):
            bb = es_bufs["bank_b"].next()
            bg = es_bufs["bank_g"].next()
            for k in range(8):
                mm(ps[bg][:], Wg[:, k, oc * 128:(oc + 1) * 128], H[:, k, t * TB:(t + 1) * TB], k == 0, k == 7,
                   [r_w[0].at(oc * 128), r_H[t]], [r_ps[bg]])
            for k in range(4):
                mm(ps[bb][:], Wb[:, k, oc * 128:(oc + 1) * 128], yT_views(k), k == 0, k == 3, [r_w[1].at(oc * 128), r_y], [r_ps[bb]])
            tg, r_tg = es_bufs["tg"].next()
            S.op("act", lambda E: E.activation(out=tg[:], in_=ps[bg][:], func=AF.Tanh, scale=0.5), reads=[r_ps[bg]], writes=[r_tg])
            msb, r_msb = es_bufs["msb"].next()
            S.op("dve", lambda E: E.scalar_tensor_tensor(out=msb[:], in0=tg[:], scalar=1.0, in1=ps[bb][:],
                                                         op0=ALU.add, op1=ALU.mult), reads=[r_tg, r_ps[bb]], writes=[r_msb])
            S.dma("sp", Mr[r][t][:, oc * TB:(oc + 1) * TB], msb[:], reads=[r_msb], writes=[r_M[r][t]])
            if hook is not None:
                hook(oc)

    for l in range(depth):
        S.barrier()
        S.dma("sp", pcols[:], pcols_in[l], writes=[r_pc])
        S.dma("sp", prows[:], prows_in[l].partition_broadcast(128), writes=[r_pr])
        S.op("dve", lambda E: E.tensor_scalar(out=wtap[:, 0:16], in0=pcols[:, PC_CONVC:PC_CONVC + 16], scalar1=0.5, scalar2=None,
                                              op0=ALU.mult), reads=[r_pc], writes=[r_wtap])
        S.op("dve", lambda E: E.tensor_scalar(out=wtap[:, 16:32], in0=pcols[:, PC_CONVC + 16:PC_CONVC + 32],
                                              scalar1=0.5 * (128 ** -0.5), scalar2=None, op0=ALU.mult), reads=[r_pc], writes=[r_wtap])

        if l == 0:
            with phase_begin() as es:
                ph = {"ps_ss": 0, "sq": Rot(talloc(es, "sq", [128, 512], BF16, 2)), "lnb": talloc(es, "lnb", [128, 512], F32)[0]}
                xts = Rot(talloc(es, "xt", [128, 8, TB], F32, 2))
                for t in range(NT):
                    xt, r_xt = xts.next()
                    S.dma("sp", xt[:], tmaj(xT_in, t), writes=[r_xt])
                    S.dma("act", tmaj(X, t), xt[:], reads=[r_xt], writes=[r_X[t]])
                    norm_tile(ph, xt, r_xt, PC_GMIX, lambda c: (H[:, c, t * TB:(t + 1) * TB], r_H[t]), None)
        if stop == "P0":
            break

        with phase_begin() as es:
            Wu = talloc(es, "Wu", [128, 8, 512], BF16)[0][0]
            r_Wu = WRes()
            Wv = talloc(es, "Wv", [128, 8, 512], BF16)[0][0]
            r_Wv = WRes()
            Wg = talloc(es, "Wg", [128, 8, D], BF16)[0][0]
            r_Wg = WRes()
            Wb = talloc(es, "Wb", [128, 4, D], BF16)[0][0]
            r_Wb = WRes()
            wsT32, r_ws32 = talloc(es, "wsT32", [128, 4, 128], F32)[0]
            wsT, r_ws = talloc(es, "wsT", [128, 4, 128], BF16)[0]
            load_w(Wu, w_in[l][:, A_U:A_U + 512], r_Wu)
            load_w(Wv, w_in[l][:, A_V:A_V + 512], r_Wv)
            S.dma("sp", wsT32[:], wsT_in[l].rearrange("g s t -> s g t"), writes=[r_ws32])
            S.op("dve", lambda E: E.memset(wsT32[64:128, :, 0:64], 0.0), writes=[r_ws32])
            S.op("dve", lambda E: E.tensor_copy(out=wsT[:], in_=wsT32[:]), reads=[r_ws32], writes=[r_ws])
            load_w(Wg, w_in[l][:, G_0:G_0 + D], r_Wg)
            load_w(Wb, w_branch[l, 0], r_Wb)
            u_sb = talloc(es, "u_sb", [128, 4, TB], F32)[0]
            v_sb = talloc(es, "v_sb", [128, 4, 512], F32)[0]
            v_n = talloc(es, "v_n", [128, 4, 512], BF16)[0]
            ssv = talloc(es, "ssv", [128, 4], F32)[0]
            junk = talloc(es, "junk", [128, 512], BF16)[0]
            yaT = Rot(talloc(es, "yaT", [128, 4, TB], BF16, 2))
            tmpm = Rot(talloc(es, "tmpm", [128, TB], F32, 2))
            bufs = {"msb": Rot(talloc(es, "msb", [128, TB], BF16, 3)), "tg": Rot(talloc(es, "tg", [128, TB], F32, 2)),
                    "bank_b": Rot([6, 0]), "bank_g": Rot([1, 2])}
            bu = Rot([0, 1])
            bv = Rot([2, 3])
            bm = Rot([4, 5])
            mh4 = talloc(es, "mh4", [128, 4], F32)[0]
            S.op("pool", lambda E: E.memset(mh4[0][:], -0.5), writes=[mh4[1]])
            for t in range(NT):
                tok = slice(t * TB, (t + 1) * TB)
                S.op("dve", lambda E: E.memset(ssv[0][:], 0.0), writes=[ssv[1]])
                for blk in range(4):
                    b = bv.next()
                    tk = slice(t * TB + blk * 128, t * TB + (blk + 1) * 128)
                    for k in range(8):
                        mm(ps[b][:], H[:, k, tk], Wv[:, k, :], k == 0, k == 7, [r_Wv.all(), r_H[t]], [r_ps[b]])
                    S.op("act", lambda E: E.activation(out=v_sb[0][:, blk, :], in_=ps[b][:], func=AF.Gelu_apprx_tanh),
                         reads=[r_ps[b]], writes=[v_sb[1]])
                    S.op("act", lambda E: E.activation(out=junk[0][:], in_=v_sb[0][:, blk, :], func=AF.Square,
                                                       accum_out=ssv[0][:, blk:blk + 1]), reads=[v_sb[1], ssv[1]], writes=[junk[1], ssv[1]])
                for g in range(4):
                    b = bu.next()
                    for k in range(8):
                        mm(ps[b][:], Wu[:, k, g * 128:(g + 1) * 128], H[:, k, tok], k == 0, k == 7, [r_Wu.at(g * 128), r_H[t]], [r_ps[b]])
                    S.op("act", lambda E: E.activation(out=u_sb[0][:, g, :], in_=ps[b][:], func=AF.Gelu_apprx_tanh),
                         reads=[r_ps[b]], writes=[u_sb[1]])
                S.op("dve", lambda E: E.tensor_scalar(out=ssv[0][:], in0=ssv[0][:], scalar1=1.0 / 512, scalar2=EPS, op0=ALU.mult,
                                                      op1=ALU.add), reads=[ssv[1]], writes=[ssv[1]])
                S.op("pool", lambda E: E.tensor_tensor(out=ssv[0][:], in0=ssv[0][:], in1=mh4[0][:], op=ALU.pow),
                     reads=[ssv[1], mh4[1]], writes=[ssv[1]])
                for blk in range(4):
                    S.op("dve", lambda E: E.scalar_tensor_tensor(out=v_n[0][:, blk, :], in0=v_sb[0][:, blk, :], scalar=ssv[0][:, blk:blk + 1],
                                                                 in1=prows[:, PR_GSGU:PR_GSGU + 512], op0=ALU.mult, op1=ALU.mult),
                         reads=[v_sb[1], ssv[1], r_pr], writes=[v_n[1]])
                ya, r_ya = yaT.next()
                for g in range(4):
                    b = bm.next()
                    for blk in range(4):
                        mm(ps[b][:, blk * 128:(blk + 1) * 128], v_n[0][:, blk, g * 128:(g + 1) * 128], wsT[:, g, :], True, True,
                           [v_n[1], r_ws], [r_ps[b]])
                    tm, r_tm = tmpm.next()
                    for blk in range(4):
                        S.op("dve", lambda E: E.tensor_tensor(out=tm[:, blk * 128:(blk + 1) * 128], in0=ps[b][:, blk * 128:(blk + 1) * 128],
                                                              in1=prows[:, PR_BS + g * 128:PR_BS + (g + 1) * 128], op=ALU.add),
                             reads=[r_ps[b], r_pr], writes=[r_tm])
                    S.op("dve", lambda E: E.tensor_tensor(out=ya[:, g, :], in0=tm[:], in1=u_sb[0][:, g, :], op=ALU.mult),
                         reads=[r_tm, u_sb[1]], writes=[r_ya])
                gate_branch(0, t, lambda k: ya[:, k, :], r_ya, Wg, Wb, [r_Wg, r_Wb], bufs)
        if stop == "P1":
            break

        with phase_begin() as es:
            Wq = talloc(es, "Wq", [128, 8, 512], BF16)[0][0]
            r_Wq = WRes()
            Wk = talloc(es, "Wk", [128, 8, 512], BF16)[0][0]
            r_Wk = WRes()
            Wv = talloc(es, "Wv", [128, 8, 512], BF16)[0][0]
            r_Wv = WRes()
            Wf = talloc(es, "Wf", [128, 8, 8], BF16)[0][0]
            r_Wf = WRes()
            load_w(Wf, w_in[l][:, B_F:B_F + 8], r_Wf)
            load_w(Wq, w_in[l][:, B_Q:B_Q + 512], r_Wq)
            load_w(Wk, w_in[l][:, B_K:B_K + 512], r_Wk)
            load_w(Wv, w_in[l][:, B_V:B_V + 512], r_Wv)
            for blk in range(NBLK):
                for k in range(8):
                    mm(ps[0][:, blk * 8:(blk + 1) * 8], H[:, k, blk * 128:(blk + 1) * 128], Wf[:, k, :], k == 0, k == 7,
                       [r_Wf.all(), r_H[blk // 4]], [r_ps[0]])
            zf, r_zf = talloc(es, "zf", [128, 256], F32)[0]
            tot, r_tot = talloc(es, "tot", [128, 256], F32)[0]
            pre, r_pre = talloc(es, "pre", [128, 256], F32)[0]
            Csp, r_Csp = talloc(es, "Csp", [128, 256], F32)[0]
            Rsp, r_Rsp = talloc(es, "Rsp", [128, 256], F32)[0]
            S.op("dve", lambda E: E.tensor_tensor(out=zf[:], in0=ps[0][:, 0:256], in1=prows[:, PR_BFOX:PR_BFOX + 256], op=ALU.add),
                 reads=[r_ps[0], r_pr], writes=[r_zf])
            S.op("act", lambda E: E.activation(out=zf[:], in_=zf[:], func=AF.Exp, scale=-1.0), reads=[r_zf], writes=[r_zf])
            S.op("act", lambda E: E.activation(out=zf[:], in_=zf[:], func=AF.Ln, bias=1.0), reads=[r_zf], writes=[r_zf])
            mm(ps[1][:, 0:256], U32[:], zf[:], True, True, [r_const, r_zf], [r_ps[1]])
            mm(ps[2][:, 0:256], ones32[:], zf[:], True, True, [r_const, r_zf], [r_ps[2]])
            mm(ps[3][:, 0:256], Umid[:], zf[:], True, True, [r_const, r_zf], [r_ps[3]])
            S.op("dve", lambda E: E.tensor_copy(out=tot[:], in_=ps[2][:, 0:256]), reads=[r_ps[2]], writes=[r_tot])
            S.op("dve", lambda E: E.memset(pre[:], 0.0), writes=[r_pre])
            pre3 = pre[:].rearrange("p (m h) -> p m h", h=8)
            tot3 = tot[:].rearrange("p (m h) -> p m h", h=8)
            for h in range(8):
                S.op("dve", lambda E: E.tensor_tensor_scan(out=pre3[:, 1:32, h], data0=ones32[:, 0:31], data1=tot3[:, 0:31, h], initial=0.0,
                                                           op0=ALU.mult, op1=ALU.add), reads=[r_tot, r_const], writes=[r_pre])
            S.op("dve", lambda E: E.tensor_tensor(out=Csp[:], in0=ps[1][:, 0:256], in1=pre[:], op=ALU.add), reads=[r_ps[1], r_pre], writes=[r_Csp])
            S.op("dve", lambda E: E.tensor_tensor(out=Rsp[:], in0=ps[3][:, 0:256], in1=pre[:], op=ALU.add), reads=[r_ps[3], r_pre], writes=[r_Rsp])
            Csp3 = Csp[:].rearrange("p (m h) -> p m h", h=8)

            qT, r_qT = talloc(es, "qT", [128, S_LEN], BF16)[0]
            kT, r_kT = talloc(es, "kT", [128, S_LEN], BF16)[0]
            vaug, r_va = talloc(es, "vaug", [128, NBLK, 192], BF16)[0]
            biasT, r_bias = talloc(es, "biasT", [128, 2, NBLK, NBLK], F32)[0]
            ybT, r_ybT = talloc(es, "ybT", [128, S_LEN], BF16)[0]
            PTs = Rot(talloc(es, "PT", [128, TB], BF16, 5))
            recs = Rot(talloc(es, "rec", [128, TB], F32, 2))
            bqk = Rot([0, 1])
            bvv = Rot([2])
            bS = Rot([0, 1, 2, 5])
            bA = Rot([3, 4])
            bB = Rot([6])
            combs = Rot(talloc(es, "comb", [128, TB], F32, 2))
            alpha, r_alpha = talloc(es, "alpha", [128, 256], F32)[0]
            Rsp4 = Rsp[:].rearrange("p (t b h) -> p t b h", b=4, h=8)
            al4 = alpha[:].rearrange("p (t b h) -> p t b h", b=4, h=8)
            for b_ in range(4):
                S.op("dve", lambda E: E.tensor_tensor(out=al4[:, :, b_, :], in0=Rsp4[:, :, b_, :], in1=Rsp4[:, :, 0, :], op=ALU.subtract),
                     reads=[r_Rsp], writes=[r_alpha])
            S.op("act", lambda E: E.activation(out=alpha[:], in_=alpha[:], func=AF.Exp, scale=-1.0), reads=[r_alpha], writes=[r_alpha])
            S.op("pool", lambda E: E.memset(vaug[:, :, 64:128], 1.0), writes=[r_va])
            for j in range(4):
                fc = slice(j * 128, (j + 1) * 128)
                for t in range(NT):
                    tok = slice(t * TB, (t + 1) * TB)
                    b = bqk.next()
                    for k in range(8):
                        mm(ps[b][:], Wq[:, k, fc], H[:, k, tok], k == 0, k == 7, [r_Wq.at(j * 128), r_H[t]], [r_ps[b]])
                    S.op("dve", lambda E: E.tensor_copy(out=qT[:, tok], in_=ps[b][:]), reads=[r_ps[b]], writes=[r_qT])
                    b = bqk.next()
                    for k in range(8):
                        mm(ps[b][:], Wk[:, k, fc], H[:, k, tok], k == 0, k == 7, [r_Wk.at(j * 128), r_H[t]], [r_ps[b]])
                    S.op("dve", lambda E: E.tensor_copy(out=kT[:, tok], in_=ps[b][:]), reads=[r_ps[b]], writes=[r_kT])
                    b = bvv.next()
                    for blk in range(4):
                        tk = slice(t * TB + blk * 128, t * TB + (blk + 1) * 128)
                        for k in range(8):
                            mm(ps[b][:, blk * 128:(blk + 1) * 128], H[:, k, tk], Wv[:, k, fc], k == 0, k == 7, [r_Wv.at(j * 128), r_H[t]], [r_ps[b]])
                    psv = ps[b][:].rearrange("p (b c) -> p b c", c=128)
                    S.op("dve", lambda E: E.tensor_copy(out=vaug[:, 4 * t:4 * t + 4, 0:64], in_=psv[:, :, 0:64]),
                         reads=[r_ps[b]], writes=[r_va])
                    S.op("dve", lambda E: E.tensor_copy(out=vaug[:, 4 * t:4 * t + 4, 128:192], in_=psv[:, :, 64:128]),
                         reads=[r_ps[b]], writes=[r_va])
                for hh in range(2):
                    h = 2 * j + hh
                    for n in range(NBLK):
                        S.op("dve", lambda E: E.tensor_scalar(out=biasT[:, hh, n, 0:n + 1], in0=Csp3[:, 0:n + 1, h],
                                                              scalar1=Rsp[:, n * 8 + h:n * 8 + h + 1], scalar2=None, op0=ALU.subtract),
                             reads=[r_Csp, r_Rsp], writes=[r_bias])
                items = []
                for hh in range(2):
                    for t in range(NT):
                        for m in range(4 * t + 4):
                            items.append((hh, t, m))
                state = {}

                def stage1(it):
                    hh, t, m = it
                    prt = slice(64 * hh, 64 * hh + 64)
                    jin = m - 4 * t
                    c0 = max(jin, 0) * 128
                    bs_ = bS.next()
                    mm(ps[bs_][:, c0:TB], kT[prt, m * 128:(m + 1) * 128], qT[prt, t * TB + c0:(t + 1) * TB], True, True,
                       [r_kT, r_qT], [r_ps[bs_]])
                    PT, r_PT = PTs.next()
                    if jin < 0:
                        S.op("act", lambda E: E.activation(out=PT[:], in_=ps[bs_][:], func=AF.Exp, scale=0.125, bias=biasT[:, hh, 4 * t, m:m + 1]),
                             reads=[r_ps[bs_], r_bias], writes=[r_PT])
                    else:
                        for qb in range(jin, 4):
                            n = 4 * t + qb
                            S.op("act", lambda E: E.activation(out=PT[:, qb * 128:(qb + 1) * 128], in_=ps[bs_][:, qb * 128:(qb + 1) * 128],
                                                               func=AF.Exp, scale=0.125, bias=biasT[:, hh, n, m:m + 1]),
                                 reads=[r_ps[bs_], r_bias], writes=[r_PT])
                        S.op("pool", lambda E: E.tensor_tensor(out=PT[:, c0:c0 + 128], in0=PT[:, c0:c0 + 128], in1=tri_bf[:], op=ALU.mult),
                             reads=[r_PT, r_const], writes=[r_PT])
                    state[it] = (PT, r_PT)

                def stage2(it):
                    hh, t, m = it
                    h = 2 * j + hh
                    prt = slice(64 * hh, 64 * hh + 64)
                    oth = slice(64 - 64 * hh, 128 - 64 * hh)
                    vcols = slice(64 * hh, 64 * hh + 128)
                    jin = m - 4 * t
                    c0 = max(jin, 0) * 128
                    PT, r_PT = state.pop(it)
                    if m == 0 and t > 0:
                        state[("A", hh, t)] = bA.next()
                    if jin == 0:
                        state[("B", hh, t)] = bB.next()
                    if jin < 0:
                        ba = state[("A", hh, t)]
                        mm(ps[ba][:], vaug[:, m, vcols], PT[:], m == 0, m == 4 * t - 1, [r_va, r_PT], [r_ps[ba]])
                        return
                    bb = state[("B", hh, t)]
                    mm(ps[bb][:, c0:TB], vaug[:, m, vcols], PT[:, c0:TB], jin == 0, jin == 3, [r_va, r_PT], [r_ps[bb]])
                    if jin < 3:
                        return
                    comb, r_comb = combs.next()
                    S.op("dve", lambda E: E.tensor_copy(out=comb[:], in_=ps[bb][:]), reads=[r_ps[bb]], writes=[r_comb])
                    if t > 0:
                        ba = state.pop(("A", hh, t))
                        for qb in range(4):
                            n = 4 * t + qb
                            S.op("dve", lambda E: E.scalar_tensor_tensor(out=comb[:, qb * 128:(qb + 1) * 128], in0=ps[ba][:, qb * 128:(qb + 1) * 128],
                                                                         scalar=alpha[:, n * 8 + h:n * 8 + h + 1], in1=comb[:, qb * 128:(qb + 1) * 128],
                                                                         op0=ALU.mult, op1=ALU.add), reads=[r_ps[ba], r_comb, r_alpha], writes=[r_comb])
                    state.pop(("B", hh, t))
                    rec, r_rec = recs.next()
                    S.op("dve", lambda E: E.reciprocal(out=rec[prt, :], in_=comb[oth, :]), reads=[r_comb], writes=[r_rec])
                    S.op("dve", lambda E: E.tensor_tensor(out=ybT[prt, t * TB:(t + 1) * TB], in0=comb[prt, :], in1=rec[prt, :], op=ALU.mult),
                         reads=[r_comb, r_rec], writes=[r_ybT])

                LA = 3
                for i, it in enumerate(items):
                    stage1(it)
                    if i >= LA:
                        stage2(items[i - LA])
                for it in items[len(items) - LA:]:
                    stage2(it)
                S.dma("sp", YB[j * 128:(j + 1) * 128, :], ybT[:], reads=[r_ybT], writes=r_YB)
        if stop == "P2a":
            break
        with phase_begin() as es:
            Wg = talloc(es, "Wg", [128, 8, D], BF16)[0][0]
            r_Wg = WRes()
            Wb = talloc(es, "Wb", [128, 4, D], BF16)[0][0]
            r_Wb = WRes()
            load_w(Wg, w_in[l][:, G_0 + D:G_0 + 2 * D], r_Wg)
            load_w(Wb, w_branch[l, 1], r_Wb)
            ybt = Rot(talloc(es, "ybt", [128, 4, TB], BF16, 2))
            bufs = {"msb": Rot(talloc(es, "msb", [128, TB], BF16, 3)), "tg": Rot(talloc(es, "tg", [128, TB], F32, 2)),
                    "bank_b": Rot([0, 1]), "bank_g": Rot([2, 3])}
            for t in range(NT):
                yb, r_yb = ybt.next()
                S.dma("sp", yb[:], YB.rearrange("(c p) t -> p c t", p=128)[:, :, t * TB:(t + 1) * TB], reads=[r_YB[t]], writes=[r_yb])
                gate_branch(1, t, lambda k: yb[:, k, :], r_yb, Wg, Wb, [r_Wg, r_Wb], bufs)
        if stop == "P2":
            break

        with phase_begin() as es:
            Wqk = talloc(es, "Wqk", [128, 8, 1024], BF16)[0][0]
            r_Wqk = WRes()
            Wv = talloc(es, "Wv3", [128, 8, 512], BF16)[0][0]
            r_Wv = WRes()
            Wo = talloc(es, "Wo3", [128, 8, 512], BF16)[0][0]
            r_Wo = WRes()
            Wif = talloc(es, "Wif", [128, 8, 8], BF16)[0][0]
            r_Wif = WRes()
            Wg = talloc(es, "Wg", [128, 8, D], BF16)[0][0]
            r_Wg = WRes()
            Wb = talloc(es, "Wb", [128, 4, D], BF16)[0][0]
            r_Wb = WRes()
            load_w(Wif, w_in[l][:, C_I:C_I + 8], r_Wif)
            load_w(Wqk, w_in[l][:, C_Q:C_Q + 1024], r_Wqk)
            load_w(Wv, w_in[l][:, C_V:C_V + 512], r_Wv)
            load_w(Wo, w_in[l][:, C_O:C_O + 512], r_Wo)
            load_w(Wg, w_in[l][:, G_0 + 2 * D:G_0 + 3 * D], r_Wg)
            load_w(Wb, w_branch[l, 2], r_Wb)
            gmh_h, r_gmh = talloc(es, "gmh_h", [128, 512], F32)[0]
            S.op("dve", lambda E: E.tensor_scalar(out=gmh_h[:], in0=prows[:, PR_GMH:PR_GMH + 512], scalar1=0.5, scalar2=None, op0=ALU.mult),
                 reads=[r_pr], writes=[r_gmh])
            for blk in range(NBLK):
                for k in range(8):
                    mm(ps[0][:, blk * 8:(blk + 1) * 8], H[:, k, blk * 128:(blk + 1) * 128], Wif[:, k, :], k == 0, k == 7,
                       [r_Wif.all(), r_H[blk // 4]], [r_ps[0]])
            ps3 = ps[0][:, 0:256].rearrange("p (b c) -> p b c", c=8)
            gi, r_gi = talloc(es, "gi", [128, 128], F32)[0]
            zf, r_zf = talloc(es, "zf3", [128, 128], F32)[0]
            E1, r_E1 = talloc(es, "E1", [128, 128], F32)[0]
            emb, r_emb = talloc(es, "emb", [128, 128], F32)[0]
            eg, r_eg = talloc(es, "eg", [128, 128], F32)[0]
            v4 = lambda ap: ap.rearrange("p (b c) -> p b c", c=4)
            S.op("dve", lambda E: E.tensor_tensor(out=v4(gi[:]), in0=ps3[:, :, 0:4], in1=v4(prows[:, PR_BI:PR_BI + 128]), op=ALU.add),
                 reads=[r_ps[0], r_pr], writes=[r_gi])
            S.op("dve", lambda E: E.tensor_tensor(out=v4(zf[:]), in0=ps3[:, :, 4:8], in1=v4(prows[:, PR_BF:PR_BF + 128]), op=ALU.add),
                 reads=[r_ps[0], r_pr], writes=[r_zf])
            S.op("act", lambda E: E.activation(out=zf[:], in_=zf[:], func=AF.Exp, scale=-1.0), reads=[r_zf], writes=[r_zf])
            S.op("act", lambda E: E.activation(out=zf[:], in_=zf[:], func=AF.Ln, bias=1.0), reads=[r_zf], writes=[r_zf])
            mm(ps[1][:, 0:128], U32[:], zf[:], True, True, [r_const, r_zf], [r_ps[1]])
            mm(ps[2][:, 0:128], ones32[:], zf[:], True, True, [r_const, r_zf], [r_ps[2]])
            S.op("dve", lambda E: E.tensor_tensor(out=gi[:], in0=ps[1][:, 0:128], in1=gi[:], op=ALU.add), reads=[r_ps[1], r_gi], writes=[r_gi])
            S.op("act", lambda E: E.activation(out=E1[:], in_=gi[:], func=AF.Exp), reads=[r_gi], writes=[r_E1])
            S.op("act", lambda E: E.activation(out=emb[:], in_=ps[1][:, 0:128], func=AF.Exp), reads=[r_ps[1]], writes=[r_emb])
            S.op("act", lambda E: E.activation(out=eg[:], in_=ps[2][:, 0:128], func=AF.Exp, scale=-1.0), reads=[r_ps[2]], writes=[r_eg])

            halo, r_halo = talloc(es, "halo", [128, 8, 3], F32)[0]
            S.op("pool", lambda E: E.memset(halo[:], 0.0), writes=[r_halo])
            prebufs = Rot(talloc(es, "prebuf", [128, 515], F32, 2))
            accs = Rot(talloc(es, "acc", [128, TB], F32, 2))
            tgs = Rot(talloc(es, "tg3", [128, TB], F32, 2))
            qkTs = Rot(talloc(es, "qkT", [128, 8, TB], BF16, 2))
            Vaugs = Rot(talloc(es, "Vaug", [128, 4, 129], BF16, 2))
            ogs = Rot(talloc(es, "og", [128, 512], F32, 2))
            Ats = Rot(talloc(es, "At", [128, 512], BF16, 2))
            ktoks = Rot(talloc(es, "ktok", [128, 512], BF16, 2))
            Cst, r_Cst = talloc(es, "Cst", [128, 4, 129], F32)[0]
            Cbf, r_Cbf = talloc(es, "Cbf", [128, 4, 129], BF16)[0]
            S.op("pool", lambda E: E.memset(Cst[:], 0.0), writes=[r_Cst])
            S.op("pool", lambda E: E.memset(Cbf[:], 0.0), writes=[r_Cbf])
            hhs = Rot(talloc(es, "hh", [128, 4, 128], F32, 2))
            dns = Rot(talloc(es, "dn", [128, 8], F32, 2))
            ss4s = Rot(talloc(es, "ss4", [128, 4], F32, 2))
            junk, r_junk = talloc(es, "junk3", [128, 128], BF16)[0]
            mh4, r_mh4 = talloc(es, "mh4_3", [128, 4], F32)[0]
            S.op("pool", lambda E: E.memset(mh4[:], -0.5), writes=[r_mh4])
            hcs = Rot(talloc(es, "hc", [128, 512], F32, 2))
            yctoks = Rot(talloc(es, "yctok", [128, 512], BF16, 2))
            ycTs = Rot(talloc(es, "ycT", [128, 4, TB], BF16, 1))
            bufs = {"msb": Rot(talloc(es, "msb", [128, TB], BF16, 3)), "tg": Rot(talloc(es, "tgg", [128, TB], F32, 2)),
                    "bank_b": Rot([0, 2]), "bank_g": Rot([1, 3])}
            bqk = Rot([0, 1])
            bvo = Rot([2, 3])
            def conv_chunk(t, c, qkT, r_qkT):
                tok = slice(t * TB, (t + 1) * TB)
                b = bvo.next()
                for k in range(8):
                    mm(ps[b][:], Wqk[:, k, c * 128:(c + 1) * 128], H[:, k, tok], k == 0, k == 7, [r_Wqk.at(c * 128), r_H[t]], [r_ps[b]])
                pbuf, r_pb = prebufs.next()
                S.op("pool", lambda E: E.tensor_copy(out=pbuf[:, 0:3], in_=halo[:, c, :]), reads=[r_halo], writes=[r_pb])
                S.op("act", lambda E: E.activation(out=pbuf[:, 3:515], in_=ps[b][:], func=AF.Copy), reads=[r_ps[b]], writes=[r_pb])
                acc, r_acc = accs.next()
                S.op("act", lambda E: E.activation(out=acc[:], in_=ps[b][:], func=AF.Copy, scale=pcols[:, PC_CONVC + c * 4 + 3:PC_CONVC + c * 4 + 4]),
                     reads=[r_ps[b], r_pc], writes=[r_acc])
                S.op("pool", lambda E: E.tensor_copy(out=halo[:, c, :], in_=pbuf[:, 512:515]), reads=[r_pb], writes=[r_halo])
                for jj in range(0, 3):
                    S.op("dve", lambda E: E.scalar_tensor_tensor(out=acc[:], in0=pbuf[:, jj:jj + 512], scalar=pcols[:, PC_CONVC + c * 4 + jj:PC_CONVC + c * 4 + jj + 1],
                                                                 in1=acc[:], op0=ALU.mult, op1=ALU.add), reads=[r_pb, r_pc, r_acc], writes=[r_acc])
                S.op("act", lambda E: E.activation(out=qkT[:, c, :], in_=acc[:], func=AF.Silu), reads=[r_acc], writes=[r_qkT])

            qk_of = {0: qkTs.next(), 1: qkTs.next()}
            for c in range(8):
                conv_chunk(0, c, *qk_of[0])
            for c in range(8):
                conv_chunk(1, c, *qk_of[1])
            ycT_of = {}
            blkst = {}

            def front(blk):
                t, bi = divmod(blk, 4)
                tk = slice(t * TB + bi * 128, t * TB + (bi + 1) * 128)
                bc = slice(bi * 128, (bi + 1) * 128)
                qkT, r_qkT = qk_of[t]
                b = bvo.next()
                for k in range(8):
                    mm(ps[b][:], H[:, k, tk], Wv[:, k, :], k == 0, k == 7, [r_Wv.all(), r_H[t]], [r_ps[b]])
                Vaug, r_Va = Vaugs.next()
                for h in range(4):
                    S.op("act", lambda E: E.activation(out=Vaug[:, h, 0:128], in_=ps[b][:, h * 128:(h + 1) * 128], func=AF.Copy,
                                                       scale=E1[:, blk * 4 + h:blk * 4 + h + 1]), reads=[r_ps[b], r_E1], writes=[r_Va])
                S.op("act", lambda E: E.activation(out=Vaug[:, :, 128:129], in_=E1[:, blk * 4:blk * 4 + 4].rearrange("p (h o) -> p h o", o=1),
                                                   func=AF.Copy), reads=[r_E1], writes=[r_Va])
                b2 = bvo.next()
                for k in range(8):
                    mm(ps[b2][:], H[:, k, tk], Wo[:, k, :], k == 0, k == 7, [r_Wo.all(), r_H[t]], [r_ps[b2]])
                og, r_og = ogs.next()
                S.op("act", lambda E: E.activation(out=og[:], in_=ps[b2][:], func=AF.Tanh, scale=0.5), reads=[r_ps[b2]], writes=[r_og])
                for h in range(4):
                    mm(ps[4][:, h * 128:(h + 1) * 128], qkT[:, 4 + h, bc], qkT[:, h, bc], True, True, [r_qkT], [r_ps[4]])
                for h in range(4):
                    S.op("pe", lambda E: E.transpose(pst[:, 512 + h * 128:512 + (h + 1) * 128], qkT[:, 4 + h, bc], ident[:]),
                         reads=[r_qkT, r_const], writes=[r_ps[7]])
                At, r_At = Ats.next()
                S.op("dve", lambda E: E.tensor_tensor(out=At[:], in0=ps[4][:], in1=tri4s[:], op=ALU.mult), reads=[r_ps[4], r_const], writes=[r_At])
                ktok, r_kt = ktoks.next()
                S.op("act", lambda E: E.activation(out=ktok[:], in_=pst[:, 512:1024], func=AF.Copy, scale=float(128 ** -0.5)), reads=[r_ps[7]], writes=[r_kt])
                blkst[blk] = (Vaug, r_Va, og, r_og, At, r_At, ktok, r_kt)

            def mid(blk):
                t, bi = divmod(blk, 4)
                bc = slice(bi * 128, (bi + 1) * 128)
                qkT, r_qkT = qk_of[t]
                Vaug, r_Va, og, r_og, At, r_At, ktok, r_kt = blkst[blk]
                for h in range(4):
                    bn = 5 + h // 2
                    cs = slice((h % 2) * 129, (h % 2) * 129 + 129)
                    mm(ps[bn][:, cs], At[:, h * 128:(h + 1) * 128], Vaug[:, h, :], True, False, [r_At, r_Va], [r_ps[bn]])
                    mm(ps[bn][:, cs], qkT[:, h, bc], Cbf[:, h, :], False, True, [r_qkT, r_Cbf], [r_ps[bn]])
                for h in range(4):
                    bk = h // 2
                    cs = slice((h % 2) * 129, (h % 2) * 129 + 129)
                    mm(ps[bk][:, cs], ktok[:, h * 128:(h + 1) * 128], Vaug[:, h, :], True, True, [r_kt, r_Va], [r_ps[bk]])
                for h in range(4):
                    bk = h // 2
                    cs = slice((h % 2) * 129, (h % 2) * 129 + 129)
                    if blk == 0:
                        S.op("dve", lambda E: E.tensor_copy(out=Cst[:, h, :], in_=ps[bk][:, cs]), reads=[r_ps[bk]], writes=[r_Cst])
                    else:
                        S.op("dve", lambda E: E.scalar_tensor_tensor(out=Cst[:, h, :], in0=Cst[:, h, :], scalar=eg[:, (blk - 1) * 4 + h:(blk - 1) * 4 + h + 1],
                                                                     in1=ps[bk][:, cs], op0=ALU.mult, op1=ALU.add),
                             reads=[r_Cst, r_eg, r_ps[bk]], writes=[r_Cst])
                for h in range(4):
                    S.op("act", lambda E: E.activation(out=Cbf[:, h, :], in_=Cst[:, h, :], func=AF.Copy, scale=eg[:, blk * 4 + h:blk * 4 + h + 1]),
                         reads=[r_Cst, r_eg], writes=[r_Cbf])

            def backA(blk):
                hh, r_hh = hhs.next()
                dn, r_dn = dns.next()
                for bn_ in range(2):
                    bn = 5 + bn_
                    wv = ps[bn][:, 0:258].rearrange("p (h c) -> p h c", c=129)[:, :, 128]
                    dcol = slice(2 * bn_, 2 * bn_ + 2)
                    S.op("dve", lambda E: E.tensor_scalar(out=dn[:, dcol], in0=wv, scalar1=-1.0, scalar2=None, op0=ALU.mult),
                         reads=[r_ps[bn]], writes=[r_dn])
                    S.op("dve", lambda E: E.tensor_tensor(out=dn[:, dcol], in0=dn[:, dcol], in1=wv, op=ALU.max), reads=[r_ps[bn], r_dn], writes=[r_dn])
                    S.op("dve", lambda E: E.tensor_tensor(out=dn[:, dcol], in0=dn[:, dcol], in1=emb[:, blk * 4 + 2 * bn_:blk * 4 + 2 * bn_ + 2], op=ALU.max),
                         reads=[r_dn, r_emb], writes=[r_dn])
                S.op("dve", lambda E: E.reciprocal(out=dn[:, 4:8], in_=dn[:, 0:4]), reads=[r_dn], writes=[r_dn])
                for h in range(4):
                    bn = 5 + h // 2
                    cs0 = (h % 2) * 129
                    S.op("dve", lambda E: E.tensor_scalar(out=hh[:, h, :], in0=ps[bn][:, cs0:cs0 + 128], scalar1=dn[:, 4 + h:5 + h], scalar2=None, op0=ALU.mult),
                         reads=[r_ps[bn], r_dn], writes=[r_hh])
                ss4, r_ss4 = ss4s.next()
                S.op("pool", lambda E: E.memset(ss4[:], 0.0), writes=[r_ss4])
                for h in range(4):
                    S.op("act", lambda E: E.activation(out=junk[:], in_=hh[:, h, :], func=AF.Square, accum_out=ss4[:, h:h + 1]),
                         reads=[r_hh, r_ss4], writes=[r_junk, r_ss4])
                blkst[("A", blk)] = (hh, r_hh, ss4, r_ss4)

            def backB(blk):
                t, bi = divmod(blk, 4)
                bc = slice(bi * 128, (bi + 1) * 128)
                Vaug, r_Va, og, r_og, At, r_At, ktok, r_kt = blkst.pop(blk)
                hh, r_hh, ss4, r_ss4 = blkst.pop(("A", blk))
                ycT, r_ycT = ycT_of[t]
                S.op("dve", lambda E: E.tensor_scalar(out=ss4[:], in0=ss4[:], scalar1=1.0 / 128, scalar2=EPS, op0=ALU.mult, op1=ALU.add),
                     reads=[r_ss4], writes=[r_ss4])
                S.op("pool", lambda E: E.tensor_tensor(out=ss4[:], in0=ss4[:], in1=mh4[:], op=ALU.pow), reads=[r_ss4, r_mh4], writes=[r_ss4])
                hc, r_hc = hcs.next()
                for h in range(4):
                    S.op("dve", lambda E: E.scalar_tensor_tensor(out=hc[:, h * 128:(h + 1) * 128], in0=hh[:, h, :], scalar=ss4[:, h:h + 1],
                                                                 in1=gmh_h[:, h * 128:(h + 1) * 128], op0=ALU.mult, op1=ALU.mult),
                         reads=[r_hh, r_ss4, r_gmh], writes=[r_hc])
                yct, r_yct = yctoks.next()
                S.op("dve", lambda E: E.scalar_tensor_tensor(out=yct[:], in0=og[:], scalar=1.0, in1=hc[:], op0=ALU.add, op1=ALU.mult),
                     reads=[r_og, r_hc], writes=[r_yct])

                def fin():
                    for c in range(4):
                        S.op("pe", lambda E: E.transpose(pst[:, c * 128:(c + 1) * 128], yct[:, c * 128:(c + 1) * 128], ident[:]),
                             reads=[r_yct, r_const], writes=[r_ps[7]])
                    S.op("act", lambda E: E.activation(out=ycT[:, :, bc], in_=pst[:, 0:512].rearrange("p (c t) -> p c t", t=128), func=AF.Copy),
                         reads=[r_ps[7]], writes=[r_ycT])
                return fin

            pending = []
            front(0)
            for blk in range(NBLK):
                t, bi = divmod(blk, 4)
                if bi == 0:
                    ycT_of[t] = ycTs.next()
                mid(blk)
                for f in pending:
                    f()
                pending = []
                backA(blk)
                if blk + 1 < NBLK:
                    front(blk + 1)
                pending.append(backB(blk))
                if bi == 3:
                    for f in pending:
                        f()
                    pending = []
                    ycT, r_ycT = ycT_of.pop(t)
                    qk_of.pop(t)
                    hook = None
                    if t + 2 < NT:
                        qk_of[t + 2] = qkTs.next()
                        hook = (lambda oc, tt=t + 2: conv_chunk(tt, oc, *qk_of[tt]))
                    gate_branch(2, t, lambda k: ycT[:, k, :], r_ycT, Wg, Wb, [r_Wg, r_Wb], bufs, hook=hook)
        if stop == "P3":
            break

        with phase_begin() as es:
            Wo_ = talloc(es, "Wout", [128, 8, D], BF16)[0][0]
            r_Wo_ = WRes()
            load_w(Wo_, w_out[l], r_Wo_)
            ph = {"ps_ss": 2, "sq": Rot(talloc(es, "sq", [128, 512], BF16, 8)), "lnb": talloc(es, "lnb", [128, 512], F32)[0]}
            mts = [Rot(talloc(es, "mt%d" % r, [128, 8, TB], BF16, 2)) for r in range(3)]
            mgs = Rot(talloc(es, "merged", [128, 8, TB], BF16, 2))
            tails = []
            xts = Rot(talloc(es, "xt", [128, 8, TB], F32, 2))
            bo = Rot([0, 1])
            loaded = {}

            xloaded = {}

            def issue_loads(tt):
                mt_ = [mts[r].next() for r in range(3)]
                for r in range(3):
                    S.dma("sp", mt_[r][0][:], tmaj(Mr[r], tt), reads=[r_M[r][tt]], writes=[mt_[r][1]])
                loaded[tt] = mt_

            def issue_xload(tt):
                xt_, r_xt_ = xts.next()
                S.dma("sp", xt_[:], tmaj(X, tt), reads=[r_X[tt]], writes=[r_xt_])
                xloaded[tt] = (xt_, r_xt_)

            issue_loads(0)
            issue_xload(0)
            for t in range(NT):
                tok = slice(t * TB, (t + 1) * TB)
                if t + 1 < NT:
                    issue_loads(t + 1)
                mt = loaded.pop(t)
                xt, r_xt = xloaded.pop(t)
                merged, r_mg = mgs.next()
                S.op("dve", lambda E: E.tensor_tensor(out=merged[:], in0=mt[0][0][:], in1=mt[1][0][:], op=ALU.add),
                     reads=[mt[0][1], mt[1][1]], writes=[r_mg])
                S.op("dve", lambda E: E.tensor_tensor(out=merged[:], in0=merged[:], in1=mt[2][0][:], op=ALU.add),
                     reads=[r_mg, mt[2][1]], writes=[r_mg])
                for oc in range(8):
                    b = bo.next()
                    for k in range(8):
                        mm(ps[b][:], Wo_[:, k, oc * 128:(oc + 1) * 128], merged[:, k, :], k == 0, k == 7, [r_Wo_.at(oc * 128), r_mg], [r_ps[b]])
                    S.op("dve", lambda E: E.scalar_tensor_tensor(out=xt[:, oc, :], in0=ps[b][:], scalar=0.5, in1=xt[:, oc, :], op0=ALU.mult, op1=ALU.add),
                         reads=[r_ps[b], r_xt], writes=[r_xt])
                    if oc == 2:
                        for f in tails:
                            f()
                        tails = []
                        if t + 1 < NT:
                            issue_xload(t + 1)
                S.dma("act", tmaj(X, t), xt[:], reads=[r_xt], writes=[r_X[t]])
                tails.append(norm_tile(ph, xt, r_xt, PC_GMQ, lambda c, tok=tok, t=t: (H[:, c, tok], r_H[t]), None, defer=True))
            for f in tails:
                f()
        if stop == "P4":
            break

        with phase_begin() as es:
            kmemT, r_km = talloc(es, "kmemT", [128, 8, NMEM], BF16)[0]
            vmem, r_vm = talloc(es, "vmem", [128, 2, D], BF16)[0]
            ph = {"ps_ss": 2, "sq": Rot(talloc(es, "sq", [128, 512], BF16, 8)), "lnb": talloc(es, "lnb", [128, 512], F32)[0]}
            with ExitStack() as es2:
                Wkv = talloc(es2, "Wkv", [128, 8, 2 * D], BF16)[0][0]
                r_Wkv = WRes()
                load_w(Wkv, w_mkv[l], r_Wkv)
                memx, r_memx = talloc(es2, "memx", [128, 8, NMEM], F32)[0]
                memn, r_memn = talloc(es2, "memn", [128, 8, NMEM], BF16)[0]
                S.dma("sp", memx[:], fm(memT_in), writes=[r_memx])
                norm_tile(ph, memx, r_memx, PC_GMKV, lambda c: (memn[:, c, :], r_memn), None, w=NMEM)
                bb = Rot([0, 1])
                for c in range(8):
                    b = bb.next()
                    for k in range(8):
                        mm(ps[b][:, 0:NMEM], Wkv[:, k, c * 128:(c + 1) * 128], memn[:, k, :], k == 0, k == 7, [r_Wkv.at(c * 128), r_memn], [r_ps[b]])
                    S.op("act", lambda E: E.activation(out=kmemT[:, c, :], in_=ps[b][:, 0:NMEM], func=AF.Copy), reads=[r_ps[b]], writes=[r_km])
                for mtk in range(2):
                    for hf in range(2):
                        b = bb.next()
                        for k in range(8):
                            mm(ps[b][:], memn[:, k, mtk * 128:(mtk + 1) * 128], Wkv[:, k, D + hf * 512:D + (hf + 1) * 512], k == 0, k == 7,
                               [r_Wkv.at(D + hf * 512), r_memn], [r_ps[b]])
                        S.op("dve", lambda E: E.tensor_copy(out=vmem[:, mtk, hf * 512:(hf + 1) * 512], in_=ps[b][:]), reads=[r_ps[b]], writes=[r_vm])
                S.barrier()
            Wmq = talloc(es, "Wmq", [128, 8, D], BF16)[0][0]
            r_Wmq = WRes()
            Wmo = talloc(es, "Wmo", [128, 8, D], BF16)[0][0]
            r_Wmo = WRes()
            load_w(Wmq, w_mq[l], r_Wmq)
            load_w(Wmo, w_mo[l], r_Wmo)
            qm, r_qm = talloc(es, "qm", [128, 8, TB], BF16)[0]
            om, r_om = talloc(es, "om", [128, 8, TB], BF16)[0]
            PTm = Rot(talloc(es, "PTm", [128, TB], BF16, 4))
            rls = Rot(talloc(es, "rl", [128, TB], F32, 2))
            xts = Rot(talloc(es, "xt", [128, 8, TB], F32, 2))
            bq = Rot([0, 1])
            bs_ = Rot([3, 4])
            tails = []
            for t in range(NT):
                tok = slice(t * TB, (t + 1) * TB)
                xt, r_xt = xts.next()
                S.dma("sp", xt[:], tmaj(X, t), reads=[r_X[t]], writes=[r_xt])
                for c in range(8):
                    if c == 3:
                        for f in tails:
                            f()
                        tails = []
                    b = bq.next()
                    for k in range(8):
                        mm(ps[b][:], Wmq[:, k, c * 128:(c + 1) * 128], H[:, k, tok], k == 0, k == 7, [r_Wmq.at(c * 128), r_H[t]], [r_ps[b]])
                    if c % 2 == 0:
                        S.op("act", lambda E: E.activation(out=qm[:, c, :], in_=ps[b][:], func=AF.Copy), reads=[r_ps[b]], writes=[r_qm])
                    else:
                        S.op("dve", lambda E: E.tensor_copy(out=qm[:, c, :], in_=ps[b][:]), reads=[r_ps[b]], writes=[r_qm])
                for hd in range(4):
                    pts = []
                    for mtk in range(2):
                        b = bs_.next()
                        for kc in range(2):
                            mm(ps[b][:], kmemT[:, 2 * hd + kc, mtk * 128:(mtk + 1) * 128], qm[:, 2 * hd + kc, :], kc == 0, kc == 1,
                               [r_km, r_qm], [r_ps[b]])
                        PT, r_PT = PTm.next()
                        S.op("act", lambda E: E.activation(out=PT[:], in_=ps[b][:], func=AF.Exp, scale=1.0 / 16), reads=[r_ps[b]], writes=[r_PT])
                        pts.append((PT, r_PT))
                    for mtk in range(2):
                        mm(ps[5][:], ones_bf[:], pts[mtk][0][:], mtk == 0, mtk == 1, [r_const, pts[mtk][1]], [r_ps[5]])
                    rl, r_rl = rls.next()
                    S.op("act", lambda E: E.activation(out=rl[:], in_=ps[5][:], func=AF.Ln), reads=[r_ps[5]], writes=[r_rl])
                    S.op("act", lambda E: E.activation(out=rl[:], in_=rl[:], func=AF.Exp, scale=-1.0), reads=[r_rl], writes=[r_rl])
                    for ec in range(2):
                        for mtk in range(2):
                            mm(ps[6][:], vmem[:, mtk, (2 * hd + ec) * 128:(2 * hd + ec + 1) * 128], pts[mtk][0][:], mtk == 0, mtk == 1,
                               [r_vm, pts[mtk][1]], [r_ps[6]])
                        S.op("dve", lambda E: E.tensor_tensor(out=om[:, 2 * hd + ec, :], in0=ps[6][:], in1=rl[:], op=ALU.mult),
                             reads=[r_ps[6], r_rl], writes=[r_om])
                for oc in range(8):
                    b = bq.next()
                    for k in range(8):
                        mm(ps[b][:], Wmo[:, k, oc * 128:(oc + 1) * 128], om[:, k, :], k == 0, k == 7, [r_Wmo.at(oc * 128), r_om], [r_ps[b]])
                    S.op("dve", lambda E: E.tensor_tensor(out=xt[:, oc, :], in0=ps[b][:], in1=xt[:, oc, :], op=ALU.add),
                         reads=[r_ps[b], r_xt], writes=[r_xt])
                S.dma("act", tmaj(X, t), xt[:], reads=[r_xt], writes=[r_X[t]])
                tails.append(norm_tile(ph, xt, r_xt, PC_GFFN, lambda c, tok=tok, t=t: (H[:, c, tok], r_H[t]), None, defer=True))
            for f in tails:
                f()
        if stop == "P5":
            break

        NH = DFF // 2
        for g in range(2):
            with phase_begin() as es:
                Wa = talloc(es, "Wa", [128, 8, NH], BF16)[0][0]
                r_Wa = WRes()
                Wb_ = talloc(es, "Wbb", [128, 8, NH], BF16)[0][0]
                r_Wbb = WRes()
                Wd = talloc(es, "Wd", [128, 11, D], BF16)[0][0]
                r_Wd = WRes()
                for c0_ in range(0, NH, 384):
                    load_w(Wa, w_up[l][:, g * NH:(g + 1) * NH], r_Wa, cols=(c0_, min(NH, c0_ + 384)))
                    load_w(Wb_, w_up[l][:, DFF + g * NH:DFF + (g + 1) * NH], r_Wbb, cols=(c0_, min(NH, c0_ + 384)))
                load_w(Wd, w_down[l][g * NH:(g + 1) * NH, :], r_Wd)
                ph = {"ps_ss": 6, "sq": Rot(talloc(es, "sq", [128, 512], BF16, 2)), "lnb": talloc(es, "lnb", [128, 512], F32)[0]}
                halo, r_halo = talloc(es, "haloF", [128, 22, 2], F32)[0]
                S.op("pool", lambda E: E.memset(halo[:], 0.0), writes=[r_halo])
                xbufs = Rot(talloc(es, "xbuf", [128, 514], F32, 4))
                accs = Rot(talloc(es, "accF", [128, TB], F32, 4))
                tgs = Rot(talloc(es, "tgF", [128, TB], F32, 2))
                hid, r_hid = talloc(es, "hid", [128, 11, TB], BF16)[0]
                xt, r_xt = talloc(es, "xtF", [128, 8, TB], F32)[0]
                last = (g == 1 and l == depth - 1)
                ybufs = Rot(talloc(es, "ybuf", [128, TB], F32, 2)) if last else None
                bA = Rot([0, 1])
                bB = Rot([2, 3])
                bD = Rot([4, 5])
                for t in range(NT):
                    tok = slice(t * TB, (t + 1) * TB)
                    S.dma("sp", xt[:], tmaj(X, t), reads=[r_X[t]], writes=[r_xt])
                    for cc in range(11):
                        accp = []
                        for half, (Wx, r_Wx, brot) in enumerate(((Wa, r_Wa, bA), (Wb_, r_Wbb, bB))):
                            b = brot.next()
                            for k in range(8):
                                mm(ps[b][:], Wx[:, k, cc * 128:(cc + 1) * 128], H[:, k, tok], k == 0, k == 7, [r_Wx.at(cc * 128), r_H[t]], [r_ps[b]])
                            hidx = half * 11 + cc
                            pcol = PC_FCONV + (half * 22 + g * 11 + cc) * 3
                            xb, r_xb = xbufs.next()
                            S.op("pool", lambda E: E.tensor_copy(out=xb[:, 0:2], in_=halo[:, hidx, :]), reads=[r_halo], writes=[r_xb])
                            S.op("act", lambda E: E.activation(out=xb[:, 2:514], in_=ps[b][:], func=AF.Copy), reads=[r_ps[b]], writes=[r_xb])
                            S.op("pool", lambda E: E.tensor_copy(out=halo[:, hidx, :], in_=xb[:, 512:514]), reads=[r_xb], writes=[r_halo])
                            acc, r_acc = accs.next()
                            S.op("act", lambda E: E.activation(out=acc[:], in_=ps[b][:], func=AF.Copy, scale=pcols[:, pcol + 2:pcol + 3]),
                                 reads=[r_ps[b], r_pc], writes=[r_acc])
                            for jj in range(2):
                                S.op("dve", lambda E: E.scalar_tensor_tensor(out=acc[:], in0=xb[:, jj:jj + 512], scalar=pcols[:, pcol + jj:pcol + jj + 1],
                                                                             in1=acc[:], op0=ALU.mult, op1=ALU.add), reads=[r_xb, r_pc, r_acc], writes=[r_acc])
                            accp.append((acc, r_acc))
                        (aA, r_aA), (aB, r_aB) = accp
                        tg, r_tg = tgs.next()
                        S.op("act", lambda E: E.activation(out=tg[:], in_=aA[:], func=AF.Silu), reads=[r_aA], writes=[r_tg])
                        S.op("dve", lambda E: E.tensor_tensor(out=hid[:, cc, :], in0=tg[:], in1=aB[:], op=ALU.mult), reads=[r_tg, r_aB], writes=[r_hid])
                    for oc in range(8):
                        b = bD.next()
                        for cc in range(11):
                            mm(ps[b][:], Wd[:, cc, oc * 128:(oc + 1) * 128], hid[:, cc, :], cc == 0, cc == 10, [r_Wd.at(oc * 128), r_hid], [r_ps[b]])
                        S.op("dve", lambda E: E.tensor_tensor(out=xt[:, oc, :], in0=ps[b][:], in1=xt[:, oc, :], op=ALU.add),
                             reads=[r_ps[b], r_xt], writes=[r_xt])
                    if not last:
                        S.dma("act", tmaj(X, t), xt[:], reads=[r_xt], writes=[r_X[t]])
                    if g == 1:
                        if last:
                            def outv(c):
                                return ybufs.next()

                            def after(c, ov, r_ov, tok=tok, t=t):
                                S.dma("sp", yT[t][:, c * TB:(c + 1) * TB], ov, reads=[r_ov], writes=[r_y[t]])
                            norm_tile(ph, xt, r_xt, PC_GNEXT, lambda c: (lambda tr: (tr[0][:], tr[1]))(ybufs.next()), None, after_chunk=after)
                        else:
                            norm_tile(ph, xt, r_xt, PC_GNEXT, lambda c: (H[:, c, tok], r_H[t]), None)
            if stop == "P6a" and g == 0:
                break
        if stop in ("P6", "P6a"):
            break

    if debug:
        S.barrier()
        S.dma("sp", fm(Hdbg), H[:], reads=r_H, writes=[Res()])
    S.finish_all()
    return nc


def _pack_params(inp):
    L = DEPTH
    pc = np.zeros((L, 128, NPC), np.float32)
    pr = np.zeros((L, NPR), np.float32)
    for l in range(L):
        pc[l, :, PC_GMIX:PC_GMIX + 8] = inp["g_mix"][l].reshape(8, 128).T
        pc[l, :, PC_GMQ:PC_GMQ + 8] = inp["g_mem_q"][l].reshape(8, 128).T
        pc[l, :, PC_GMKV:PC_GMKV + 8] = inp["g_mem_kv"][l].reshape(8, 128).T
        pc[l, :, PC_GFFN:PC_GFFN + 8] = inp["g_ffn"][l].reshape(8, 128).T
        pc[l, :, PC_CONVC:PC_CONVC + 32] = inp["w_conv_c"][l].reshape(4, 8, 128).transpose(2, 1, 0).reshape(128, 32)
        pc[l, :, PC_FCONV:PC_FCONV + 132] = inp["w_ffn_conv"][l].reshape(3, 44, 128).transpose(2, 1, 0).reshape(128, 132)
        gn = inp["g_mix"][l + 1] if l + 1 < L else inp["g_final"]
        pc[l, :, PC_GNEXT:PC_GNEXT + 8] = gn.reshape(8, 128).T
        pr[l, PR_GSGU:PR_GSGU + 512] = inp["g_sgu"][l]
        pr[l, PR_BS:PR_BS + 512] = inp["b_s"][l].reshape(512)
        pr[l, PR_BFOX:PR_BFOX + 256] = np.tile(inp["b_fox_f"][l], 32)
        pr[l, PR_BI:PR_BI + 128] = np.tile(inp["b_mlstm_i"][l], 32)
        pr[l, PR_BF:PR_BF + 128] = np.tile(inp["b_mlstm_f"][l], 32)
        pr[l, PR_GMH:PR_GMH + 512] = inp["g_mh"][l]
    return pc, pr


def make_in_maps(inp, n_cores=8):
    inp = {k: np.asarray(v) for k, v in inp.items()}
    pc, pr = _pack_params(inp)
    shared = {
        "w_in": np.ascontiguousarray(inp["w_in"]),
        "wsT": np.ascontiguousarray(inp["w_s"].transpose(0, 1, 3, 2)),
        "w_branch": np.ascontiguousarray(inp["w_branch"]),
        "w_out": np.ascontiguousarray(inp["w_out"]),
        "w_mq": np.ascontiguousarray(inp["w_mq"]),
        "w_mkv": np.ascontiguousarray(inp["w_mkv"]),
        "w_mo": np.ascontiguousarray(inp["w_mo"]),
        "w_up": np.ascontiguousarray(inp["w_up"]),
        "w_down": np.ascontiguousarray(inp["w_down"]),
        "pcols": pc,
        "prows": pr,
    }
    maps = []
    for b in range(n_cores):
        m = dict(shared)
        m["xT"] = np.ascontiguousarray(inp["x"][b].reshape(NT, TB, 8, 128).transpose(0, 3, 2, 1).reshape(NT, 128, 8 * TB))
        m["memT"] = np.ascontiguousarray(inp["mem"][b].T)
        maps.append(m)
    return maps


def _untile(a):
    return np.ascontiguousarray(np.asarray(a).reshape(NT, 128, 8, TB).transpose(0, 3, 2, 1).reshape(S_LEN, D))


def kernel(**inputs):
    nc = build_program()
    in_maps = make_in_maps(inputs)
    res = run_bass_kernel_spmd(nc, in_maps, core_ids=list(range(8)))
    out = np.stack([_untile(r["yT"]) for r in res.results], axis=0)
    return out.astype(np.float32)
```
